# Optimizing a Trainium2 kernel written in Bass

```python
import jax, jax.numpy as jnp
from jax import lax
import numpy as np

D_MODEL = 2048
BATCH = 16
SEQ = 2048
DEPTH = 1

CHUNK = 64
Q_BLOCK = 128
N_MEM = 256
EPS = 1e-6

D_FF = 5504

MLA_HEADS = 8
QK_NOPE = 128
QK_ROPE = 64
V_HEAD = 128
Q_LORA = 512
KV_LORA = 256
ROPE_THETA = 10000.0

RWKV_HEADS = 16
RWKV_HEAD = 64
RWKV_WIDTH = RWKV_HEADS * RWKV_HEAD
DECAY_LORA = 64
A_LORA = 64
GATE_LORA = 128
LNX_EPS = 64e-5

MEM_HEADS = 4
MEM_HEAD = 256

N_BRANCH = 2
MLA_COLS = Q_LORA + KV_LORA + QK_ROPE
RWKV_COLS = 3 * RWKV_WIDTH + DECAY_LORA + A_LORA + GATE_LORA
GATE_COLS = N_BRANCH * D_MODEL
IN_COLS = MLA_COLS + RWKV_COLS + GATE_COLS

kernel_name = 'hybrid_mla_rwkv7_gated_macaron_layer'


def rmsnorm(x, g, eps=EPS):
    xf = x.astype(jnp.float32)
    y = xf * lax.rsqrt(jnp.mean(xf * xf, axis=-1, keepdims=True) + eps)
    return (y * g.astype(jnp.float32)).astype(x.dtype)


def apply_rope(x, positions):
    half = x.shape[-1] // 2
    inv = ROPE_THETA ** (-jnp.arange(half, dtype=jnp.float32) / half)
    ang = positions.astype(jnp.float32)[..., None] * inv
    if x.ndim == 4:
        ang = ang[:, :, None, :]
    cos, sin = jnp.cos(ang), jnp.sin(ang)
    xf = x.astype(jnp.float32)
    x1, x2 = xf[..., :half], xf[..., half:]
    return jnp.concatenate([x1 * cos - x2 * sin, x1 * sin + x2 * cos], axis=-1).astype(x.dtype)


def swiglu(h, w_gate, w_up, w_down):
    return (jax.nn.silu(h @ w_gate) * (h @ w_up)) @ w_down


def mla_branch(c_q, c_kv, k_rope, positions, n_q_lat, w_uq, n_kv_lat, w_ukv, w_oa):
    B, S, _ = c_q.shape
    q = (rmsnorm(c_q, n_q_lat) @ w_uq).reshape(B, S, MLA_HEADS, QK_NOPE + QK_ROPE)
    q_nope = q[..., :QK_NOPE]
    q_rope = apply_rope(q[..., QK_NOPE:], positions)
    kv = (rmsnorm(c_kv, n_kv_lat) @ w_ukv).reshape(B, S, MLA_HEADS, QK_NOPE + V_HEAD)
    k_nope, v = kv[..., :QK_NOPE], kv[..., QK_NOPE:]
    k_r = apply_rope(k_rope, positions)
    scale = (QK_NOPE + QK_ROPE) ** -0.5
    outs = []
    for qb in range(S // Q_BLOCK):
        qs, qe = qb * Q_BLOCK, (qb + 1) * Q_BLOCK
        s = (jnp.einsum('bqhd,bkhd->bhqk', q_nope[:, qs:qe], k_nope[:, :qe])
             + jnp.einsum('bqhd,bkd->bhqk', q_rope[:, qs:qe], k_r[:, :qe]))
        s = s.astype(jnp.float32) * scale
        q_chunk = (qs + jnp.arange(Q_BLOCK)) // CHUNK
        k_chunk = jnp.arange(qe) // CHUNK
        mask = k_chunk[None, :] <= q_chunk[:, None]
        s = jnp.where(mask[None, None], s, -jnp.inf)
        p = jax.nn.softmax(s, axis=-1).astype(v.dtype)
        outs.append(jnp.einsum('bhqk,bkhd->bqhd', p, v[:, :qe]))
    o = jnp.concatenate(outs, axis=1).reshape(B, S, MLA_HEADS * V_HEAD)
    return o @ w_oa


def token_shift(y, mu):
    y_prev = jnp.pad(y, ((0, 0), (1, 0), (0, 0)))[:, :-1]
    return y + (y_prev - y) * mu


def rwkv7_branch(proj, mu_shift, w0, w_w2, a0, w_a2, w_g2, k_k, k_a, r_k, lnx_w, lnx_b, w_ob):
    B, S, _ = proj.shape
    C, H, N = RWKV_WIDTH, RWKV_HEADS, RWKV_HEAD
    p = token_shift(proj, mu_shift)
    r, k, v, dw, da, dg = jnp.split(
        p, [C, 2 * C, 3 * C, 3 * C + DECAY_LORA, 3 * C + DECAY_LORA + A_LORA], axis=-1)
    w = -jax.nn.softplus(-(w0 + jnp.tanh(dw) @ w_w2)) - 0.5
    decay = jnp.exp(-jnp.exp(w.astype(jnp.float32)))
    a = jax.nn.sigmoid(a0 + da @ w_a2)
    g = jax.nn.sigmoid(dg) @ w_g2
    kk = (k * k_k).reshape(B, S, H, N).astype(jnp.float32)
    kk = kk / jnp.maximum(jnp.sqrt(jnp.sum(kk * kk, axis=-1, keepdims=True)), 1e-12)
    k = k * (1.0 + (a - 1.0) * k_a)
    heads = lambda t: t.reshape(B, S, H, N).astype(jnp.float32)
    rh, kh, vh, ah, wh = heads(r), heads(k), heads(v), heads(a), heads(decay)
    a_in = -kk
    b_in = kk * ah
    xs = tuple(jnp.moveaxis(t, 1, 0) for t in (rh, wh, kh, vh, a_in, b_in))

    def step(state, inp):
        r_t, w_t, k_t, v_t, a_t, b_t = inp
        sa = jnp.einsum('bhvk,bhk->bhv', state, a_t)
        state = (state * w_t[:, :, None, :] + sa[..., None] * b_t[:, :, None, :]
                 + v_t[..., None] * k_t[:, :, None, :])
        return state, jnp.einsum('bhvk,bhk->bhv', state, r_t)

    s0 = jnp.zeros((B, H, N, N), jnp.float32)
    _, o = lax.scan(step, s0, xs)
    o = jnp.moveaxis(o, 0, 1)
    mean = jnp.mean(o, axis=-1, keepdims=True)
    var = jnp.mean(jnp.square(o - mean), axis=-1, keepdims=True)
    o = ((o - mean) * lax.rsqrt(var + LNX_EPS)).reshape(B, S, C)
    o = o * lnx_w.astype(jnp.float32) + lnx_b.astype(jnp.float32)
    bonus = jnp.sum(rh * kh * r_k.astype(jnp.float32), axis=-1, keepdims=True) * vh
    o = (o + bonus.reshape(B, S, C)).astype(proj.dtype)
    return (o * g) @ w_ob


def memory_xattn(h, mem, n_mem, w_cq, w_ckv, w_co):
    B, S, _ = h.shape
    M = mem.shape[1]
    q = (h @ w_cq).reshape(B, S, MEM_HEADS, MEM_HEAD)
    kv = (rmsnorm(mem, n_mem) @ w_ckv).reshape(B, M, MEM_HEADS, 2 * MEM_HEAD)
    k, v = kv[..., :MEM_HEAD], kv[..., MEM_HEAD:]
    s = jnp.einsum('bqhd,bmhd->bhqm', q, k).astype(jnp.float32) * (MEM_HEAD ** -0.5)
    p = jax.nn.softmax(s, axis=-1).astype(v.dtype)
    o = jnp.einsum('bhqm,bmhd->bqhd', p, v).reshape(B, S, MEM_HEADS * MEM_HEAD)
    return o @ w_co


def setup_inputs(seed: int = 0) -> dict:
    key = jax.random.key(seed)
    ks = iter(jax.random.split(key, 64))
    f32 = jnp.float32
    L, D = DEPTH, D_MODEL

    def nrm(shape, scale):
        return jax.random.normal(next(ks), shape, f32) * scale

    def gain(shape):
        return 1.0 + nrm(shape, 0.05)

    def dense(shape):
        return nrm(shape, shape[-2] ** -0.5)

    x = nrm((BATCH, SEQ, D), 1.0)
    mem = nrm((BATCH, N_MEM, D), 1.0)
    offset = jax.random.randint(next(ks), (BATCH, 1), 0, 4096, jnp.int32)
    positions = offset + jnp.arange(SEQ, dtype=jnp.int32)[None, :]
    return {
        'x': x, 'mem': mem, 'positions': positions,
        'n_ffn1_pre': gain((L, D)), 'n_ffn1_post': gain((L, D)),
        'w_ffn1_gate': dense((L, D, D_FF)), 'w_ffn1_up': dense((L, D, D_FF)),
        'w_ffn1_down': dense((L, D_FF, D)),
        'n_mix_pre': gain((L, D)), 'n_mix_post': gain((L, D)),
        'w_in': dense((L, D, IN_COLS)), 'b_gate': nrm((L, GATE_COLS), 0.02),
        'n_q_lat': gain((L, Q_LORA)), 'w_uq': dense((L, Q_LORA, MLA_HEADS * (QK_NOPE + QK_ROPE))),
        'n_kv_lat': gain((L, KV_LORA)), 'w_ukv': dense((L, KV_LORA, MLA_HEADS * (QK_NOPE + V_HEAD))),
        'w_oa': dense((L, MLA_HEADS * V_HEAD, D)),
        'mu_shift': jax.random.uniform(next(ks), (L, RWKV_COLS), f32),
        'w0': jax.random.uniform(next(ks), (L, RWKV_WIDTH), f32, -5.0, 0.0),
        'w_w2': nrm((L, DECAY_LORA, RWKV_WIDTH), 0.1 * DECAY_LORA ** -0.5),
        'a0': nrm((L, RWKV_WIDTH), 0.1),
        'w_a2': nrm((L, A_LORA, RWKV_WIDTH), 0.5 * A_LORA ** -0.5),
        'w_g2': dense((L, GATE_LORA, RWKV_WIDTH)),
        'k_k': 0.85 + nrm((L, RWKV_WIDTH), 0.05),
        'k_a': gain((L, RWKV_WIDTH)),
        'r_k': nrm((L, RWKV_HEADS, RWKV_HEAD), 0.1),
        'lnx_w': gain((L, RWKV_WIDTH)), 'lnx_b': nrm((L, RWKV_WIDTH), 0.02),
        'w_ob': dense((L, RWKV_WIDTH, D)),
        'w_o': dense((L, D, D)),
        'n_x_pre': gain((L, D)), 'n_x_post': gain((L, D)), 'n_mem': gain((L, D)),
        'w_cq': dense((L, D, MEM_HEADS * MEM_HEAD)),
        'w_ckv': dense((L, D, 2 * MEM_HEADS * MEM_HEAD)),
        'w_co': dense((L, MEM_HEADS * MEM_HEAD, D)),
        'n_ffn2_pre': gain((L, D)), 'n_ffn2_post': gain((L, D)),
        'w_ffn2_gate': dense((L, D, D_FF)), 'w_ffn2_up': dense((L, D, D_FF)),
        'w_ffn2_down': dense((L, D_FF, D)),
    }


def reference(x, mem, positions,
              n_ffn1_pre, n_ffn1_post, w_ffn1_gate, w_ffn1_up, w_ffn1_down,
              n_mix_pre, n_mix_post, w_in, b_gate,
              n_q_lat, w_uq, n_kv_lat, w_ukv, w_oa,
              mu_shift, w0, w_w2, a0, w_a2, w_g2, k_k, k_a, r_k, lnx_w, lnx_b, w_ob,
              w_o,
              n_x_pre, n_x_post, n_mem, w_cq, w_ckv, w_co,
              n_ffn2_pre, n_ffn2_post, w_ffn2_gate, w_ffn2_up, w_ffn2_down):
    B, S, _ = x.shape
    for l in range(DEPTH):
        h = rmsnorm(x, n_ffn1_pre[l])
        x = x + 0.5 * rmsnorm(swiglu(h, w_ffn1_gate[l], w_ffn1_up[l], w_ffn1_down[l]), n_ffn1_post[l])

        h = rmsnorm(x, n_mix_pre[l])
        proj = h @ w_in[l]
        c_q, c_kv, k_rope, rwkv_in, gate_logits = jnp.split(
            proj, [Q_LORA, Q_LORA + KV_LORA, MLA_COLS, MLA_COLS + RWKV_COLS], axis=-1)
        y_a = mla_branch(c_q, c_kv, k_rope, positions, n_q_lat[l], w_uq[l],
                         n_kv_lat[l], w_ukv[l], w_oa[l])
        y_b = rwkv7_branch(rwkv_in, mu_shift[l], w0[l], w_w2[l], a0[l], w_a2[l], w_g2[l],
                           k_k[l], k_a[l], r_k[l], lnx_w[l], lnx_b[l], w_ob[l])
        gates = jax.nn.sigmoid(gate_logits + b_gate[l]).reshape(B, S, N_BRANCH, D_MODEL)
        merged = gates[:, :, 0] * y_a + gates[:, :, 1] * y_b
        x = x + rmsnorm(merged @ w_o[l], n_mix_post[l])

        h = rmsnorm(x, n_x_pre[l])
        x = x + rmsnorm(memory_xattn(h, mem, n_mem[l], w_cq[l], w_ckv[l], w_co[l]), n_x_post[l])

        h = rmsnorm(x, n_ffn2_pre[l])
        x = x + 0.5 * rmsnorm(swiglu(h, w_ffn2_gate[l], w_ffn2_up[l], w_ffn2_down[l]), n_ffn2_post[l])
    return x
```

```python
import contextlib
import numpy as np
import concourse.bass as bass
import concourse.mybir as mybir
from concourse.bass_utils import run_bass_kernel_spmd

F32 = mybir.dt.float32
BF16 = mybir.dt.bfloat16
I32 = mybir.dt.int32
AF = mybir.ActivationFunctionType
ALU = mybir.AluOpType

D = 2048
D_FF = 5504
N_MEM = 256
EPS = 1e-6
P = 128
TT = 512


class Buf:
    __slots__ = ("name", "w", "r")

    def __init__(self, name=""):
        self.name = name
        self.w = None
        self.r = {}


class Eng:
    def __init__(self, K, name, h, sem, is_pe=False):
        self.K = K
        self.name = name
        self.h = h
        self.sem = sem
        self.sid = id(sem)
        self.n = 0
        self.seen = {}
        self.is_pe = is_pe
        K.sems[self.sid] = sem

    def wait_for(self, events):
        for sid, val in events:
            if sid == self.sid and self.is_pe:
                continue
            if self.seen.get(sid, 0) >= val:
                continue
            self.h.wait_ge(self.K.sems[sid], val)
            self.seen[sid] = val


class Kern:
    def __init__(self, nc, n_dma_sems=40):
        self.nc = nc
        self.sems = {}
        self.es = contextlib.ExitStack()
        mk = lambda nm: self.es.enter_context(nc.semaphore(nm))
        self.PE = Eng(self, "pe", nc.tensor, mk("s_pe"), is_pe=True)
        self.ACT = Eng(self, "act", nc.scalar, mk("s_act"))
        self.DVE = Eng(self, "dve", nc.vector, mk("s_dve"))
        self.POOL = Eng(self, "pool", nc.gpsimd, mk("s_pool"))
        self.SP = Eng(self, "sp", nc.sync, mk("s_sp"))
        self.engs = [self.PE, self.ACT, self.DVE, self.POOL, self.SP]
        self.dsems = []
        self.qpool = {}
        for e, cnt in ((self.SP, 28), (self.POOL, 12), (self.ACT, 8)):
            pool = []
            for i in range(cnt):
                s = mk("s_dma_%s%d" % (e.name, i))
                self.sems[id(s)] = s
                slot = [s, 0]
                pool.append(slot)
                self.dsems.append(slot)
            self.qpool[e.sid] = [pool, 0]

    @staticmethod
    def _deps(reads, writes):
        ev = set()
        for b in reads:
            if b.w is not None:
                ev.add(b.w)
        for b in writes:
            if b.w is not None:
                ev.add(b.w)
            ev.update(b.r.items())
        return ev

    def op(self, eng, reads, writes, fn):
        eng.wait_for(self._deps(reads, writes))
        ins = fn(eng.h)
        eng.n += 1
        ins.then_inc(eng.sem, 1)
        me = (eng.sid, eng.n)
        for b in reads:
            b.r[eng.sid] = eng.n
        for b in writes:
            b.w = me
            b.r = {}
        return ins

    def dma(self, q, out, in_, reads, writes, **kw):
        qp = self.qpool[q.sid]
        slot = qp[0][qp[1]]
        qp[1] = (qp[1] + 1) % len(qp[0])
        sem, prev = slot
        ev = self._deps(reads, writes)
        if prev > 0:
            ev.add((id(sem), prev))
        q.wait_for(ev)
        q.h.dma_start(out=out, in_=in_, **kw).then_inc(sem, 16)
        slot[1] = prev + 16
        me = (id(sem), prev + 16)
        for b in reads:
            b.r[me[0]] = me[1]
        for b in writes:
            b.w = me
            b.r = {}

    def barrier(self):
        ev = [(e.sid, e.n) for e in self.engs if e.n > 0]
        ev += [(id(s), v) for s, v in self.dsems if v > 0]
        for e in self.engs:
            e.wait_for([x for x in ev if x[0] != e.sid])

    def close(self):
        self.es.close()


class Cfg:
    def __init__(self, S=2048, NB=2, phases=("ffn1", "proj", "mla", "rwkv", "merge", "xattn", "ffn2"),
                 debug_out=()):
        self.S = S
        self.NB = NB
        self.NT = S * NB
        self.phases = phases
        self.debug_out = debug_out


Q_LORA, KV_LORA, QK_ROPE, QK_NOPE, V_HEAD, MLA_H = 512, 256, 64, 128, 128, 8
RW, RW_H, RW_N = 1024, 16, 64
RW_COLS = 3328
MLA_COLS = 832
GATE0 = MLA_COLS + RW_COLS
IN_COLS = 8256
MEM_H, MEM_D = 4, 256

W_SHAPES = {
    'w_ffn1_gate': (D, D_FF), 'w_ffn1_up': (D, D_FF), 'w_ffn1_down': (D_FF, D),
    'w_in': (D, IN_COLS), 'w_uq': (Q_LORA, 1536), 'w_ukv': (KV_LORA, 2048), 'w_oa': (1024, D),
    'w_w2': (64, RW), 'w_a2': (64, RW), 'w_g2': (128, RW), 'w_ob': (RW, D), 'w_o': (D, D),
    'w_cq': (D, 1024), 'w_ckv': (D, 2048), 'w_co': (1024, D),
    'w_ffn2_gate': (D, D_FF), 'w_ffn2_up': (D, D_FF), 'w_ffn2_down': (D_FF, D),
}
V_SHAPES = {
    'n_ffn1_pre': D, 'n_ffn1_post': D, 'n_mix_pre': D, 'n_mix_post': D, 'b_gate': 4096,
    'n_q_lat': Q_LORA, 'n_kv_lat': KV_LORA, 'mu_shift': RW_COLS, 'w0': RW, 'a0': RW, 'k_k': RW, 'k_a': RW,
    'r_k': RW, 'lnx_w': RW, 'lnx_b': RW, 'n_x_pre': D, 'n_x_post': D, 'n_mem': D,
    'n_ffn2_pre': D, 'n_ffn2_post': D,
}
PHASE_W = {
    "ffn1": ['w_ffn1_gate', 'w_ffn1_up', 'w_ffn1_down'],
    "proj": ['w_in'], "mla": ['w_uq', 'w_ukv', 'w_oa'],
    "rwkv": ['w_w2', 'w_a2', 'w_g2', 'w_ob'], "merge": ['w_in', 'w_o'],
    "xattn": ['w_cq', 'w_ckv', 'w_co'],
    "ffn2": ['w_ffn2_gate', 'w_ffn2_up', 'w_ffn2_down'],
}


def _largest_div(n, cap=2048):
    for c in range(cap, 0, -1):
        if n % c == 0:
            return c


class Ring:
    def __init__(self, tiles):
        self.t = tiles
        self.b = [Buf() for _ in tiles]
        self.i = 0

    def next(self):
        k = self.i % len(self.t)
        self.i += 1
        return self.t[k], self.b[k]


class Prog:
    def __init__(self, cfg):
        self.cfg = cfg
        nc = self.nc = bass.Bass("TRN2", target_bir_lowering=False)
        self.K = Kern(nc)
        NT = cfg.NT
        dt = nc.dram_tensor
        self.x_in = dt("x", [NT, D], F32, kind="ExternalInput").ap()
        self.mem_in = dt("mem", [cfg.NB * N_MEM, D], F32, kind="ExternalInput").ap()
        self.pos_in = dt("pos", [NT], I32, kind="ExternalInput").ap()
        self.ident_in = dt("ident", [P, P], F32, kind="ExternalInput").ap()
        self.cst_in = dt("cst", [P, 8], F32, kind="ExternalInput").ap()
        self.msk_in = dt("msk", [P, 512], F32, kind="ExternalInput").ap()
        self.msk2_in = dt("msk2", [P, 1536], F32, kind="ExternalInput").ap()
        self.msk3_in = dt("msk3", [P, 1536], F32, kind="ExternalInput").ap()
        self.out = dt("out", [NT, D], F32, kind="ExternalOutput").ap()
        self.w, self.wb, self.wbuf = {}, {}, {}
        self.wnames = [nm for nm in W_SHAPES if (not getattr(cfg, 'lite', False)) or any(nm in PHASE_W[p_] for p_ in cfg.phases)]
        for nm in W_SHAPES:
            k, n = W_SHAPES[nm]
            if nm in self.wnames:
                self.w[nm] = dt(nm, [k, n], F32, kind="ExternalInput").ap()
            self.wb[nm] = dt(nm + "_b", [k, n], BF16, kind="Internal").ap()
            self.wbuf[nm] = [Buf(nm) for _ in range(4)]
        self.v = {}
        for nm, n in V_SHAPES.items():
            self.v[nm] = dt(nm, [n], F32, kind="ExternalInput").ap()
        self.xT = [dt("xT%d" % i, [D, NT], F32, kind="Internal").ap() for i in range(2)]
        self.xT_buf = [[Buf() for t in range(NT // TT)] for i in range(2)]
        self.scr, self.scr_b = {}, {}
        for nm, rows in (("cq", Q_LORA), ("ckv", KV_LORA), ("kr", 64), ("cs", 64), ("sn", 64),
                         ("rw", RW_COLS), ("ya", D), ("yb", D)):
            kind = "ExternalInput" if (nm == "rw" and getattr(cfg, 'rw_input', False)) else "Internal"
            self.scr[nm] = dt("scr_" + nm, [rows, NT], F32, kind=kind).ap()
            self.scr_b[nm] = [Buf() for t in range(NT // 256)]
        self.dbg = {}
        for nm, rows in cfg.debug_out:
            self.dbg[nm] = dt("dbg_" + nm, [rows, NT], F32, kind="ExternalOutput").ap()

    def sbufs(self, nm, t0, n):
        return self.scr_b[nm][t0 // 256:(t0 + n + 255) // 256]

    def sb(self, es, name, shape, dtype):
        return es.enter_context(self.nc.sbuf_tensor("sb_" + name, shape, dtype))

    def ps(self, es, name, shape, dtype):
        return es.enter_context(self.nc.psum_tensor("ps_" + name, shape, dtype))

    def sring(self, es, name, n, shape, dtype):
        return Ring([self.sb(es, "%s%d" % (name, i), shape, dtype) for i in range(n)])

    def pring(self, es, name, n, shape=None, dtype=F32):
        return Ring([self.ps(es, "%s%d" % (name, i), shape or [P, 512], dtype) for i in range(n)])

    def mm(self, out, lhsT, rhs, start, stop, reads, writes):
        return self.K.op(self.K.PE, reads, writes,
                         lambda e: e.matmul(out, lhsT=lhsT, rhs=rhs, start=start, stop=stop))

    def tp(self, out, in_, ident, reads, writes):
        return self.K.op(self.K.PE, reads, writes, lambda e: e.transpose(out=out, in_=in_, identity=ident))

    def act(self, out, in_, func, reads, writes, scale=1.0, bias=None):
        if bias is None:
            return self.K.op(self.K.ACT, reads, writes,
                             lambda e: e.activation(out=out, in_=in_, func=func, scale=scale))
        return self.K.op(self.K.ACT, reads, writes,
                         lambda e: e.activation(out=out, in_=in_, func=func, scale=scale, bias=bias))

    def cp(self, eng, out, in_, reads, writes):
        if eng is self.K.ACT:
            return self.K.op(eng, reads, writes, lambda e: e.copy(out=out, in_=in_))
        return self.K.op(eng, reads, writes, lambda e: e.tensor_copy(out=out, in_=in_))

    def tt(self, eng, out, in0, in1, op, reads, writes):
        return self.K.op(eng, reads, writes, lambda e: e.tensor_tensor(out=out, in0=in0, in1=in1, op=op))

    def ts(self, eng, out, in0, s1, s2, op0, op1, reads, writes):
        if op1 is None:
            return self.K.op(eng, reads, writes,
                             lambda e: e.tensor_scalar(out=out, in0=in0, scalar1=s1, scalar2=None, op0=op0))
        return self.K.op(eng, reads, writes,
                         lambda e: e.tensor_scalar(out=out, in0=in0, scalar1=s1, scalar2=s2, op0=op0, op1=op1))

    def stt(self, out, in0, scalar, in1, op0, op1, reads, writes):
        return self.K.op(self.K.DVE, reads, writes, lambda e: e.scalar_tensor_tensor(
            out=out, in0=in0, scalar=scalar, in1=in1, op0=op0, op1=op1))

    def mset(self, eng, ap, val, reads, writes):
        return self.K.op(eng, reads, writes, lambda e: e.memset(ap, val))

    def recip(self, out, in_, reads, writes):
        return self.K.op(self.K.DVE, reads, writes, lambda e: e.reciprocal(out=out, in_=in_))

    def cast_weights(self, names):
        K = self.K
        done = set()
        for nm in names:
            if nm in done:
                continue
            done.add(nm)
            k, n = W_SHAPES[nm]
            c = _largest_div(n)
            src = self.w[nm].rearrange("k (a c) -> (k a) c", c=c)
            dst = self.wb[nm].rearrange("k (a c) -> (k a) c", c=c)
            rows = k * (n // c)
            nblk = 4 if rows >= 2048 else 1
            rb = rows // nblk
            for i in range(nblk):
                K.dma(K.POOL, dst[i * rb:(i + 1) * rb, :], src[i * rb:(i + 1) * rb, :],
                      reads=[], writes=[self.wbuf[nm][i]])

    def setup_consts(self, es):
        K, nc = self.K, self.nc
        self.ident = self.sb(es, "ident", [P, P], F32)
        self.ident_b = Buf("ident")
        K.dma(K.SP, self.ident[:], self.ident_in[:, :], reads=[], writes=[self.ident_b])
        self.ident_bf = self.sb(es, "ident_bf", [P, P], BF16)
        self.identbf_b = Buf()
        self.cp(K.DVE, self.ident_bf[:], self.ident[:], [self.ident_b], [self.identbf_b])
        self.cst = self.sb(es, "cst", [P, 8], F32)
        self.cst_b = Buf()
        K.dma(K.SP, self.cst[:], self.cst_in[:, :], reads=[], writes=[self.cst_b])
        self.msk = self.sb(es, "msk", [P, 512], F32)
        self.msk_b = Buf()
        K.dma(K.SP, self.msk[:], self.msk_in[:, :], reads=[], writes=[self.msk_b])
        self.ones_bf = self.sb(es, "ones_bf", [P, P], BF16)
        self.ones_b = Buf("ones")
        self.mset(K.DVE, self.ones_bf[:], 1.0, [], [self.ones_b])
        self.bones_bf = self.sb(es, "bones_bf", [P, P], BF16)
        self.bones_b = Buf()
        self.mset(K.DVE, self.bones_bf[:], 0.0, [], [self.bones_b])
        self.mset(K.DVE, self.bones_bf[0:64, 0:64], 1.0, [], [self.bones_b])
        self.mset(K.DVE, self.bones_bf[64:128, 64:128], 1.0, [], [self.bones_b])
        self.eps_t = self.sb(es, "eps_t", [P, 1], F32)
        self.eps_b = Buf("eps")
        self.mset(K.DVE, self.eps_t[:], EPS, [], [self.eps_b])
        self.vec, self.vec_b = {}, {}
        for nm, n in V_SHAPES.items():
            t = self.sb(es, "v_" + nm, [P, n // P], F32)
            b = Buf(nm)
            with nc.allow_non_contiguous_dma(reason="tiny per-feature vector load"):
                K.dma(K.SP, t[:], self.v[nm].rearrange("(c p) -> p c", p=P), reads=[], writes=[b])
            self.vec[nm] = t
            self.vec_b[nm] = b

    def phase_in(self):
        K, nc, cfg = self.K, self.nc, self.cfg
        with contextlib.ExitStack() as es:
            NBUF = 2
            xin = [self.sb(es, "pin_x%d" % i, [P, D], F32) for i in range(NBUF)]
            xin_b = [Buf() for _ in range(NBUF)]
            xo = [self.sb(es, "pin_o%d" % i, [P, 16, TT], F32) for i in range(2)]
            xo_b = [Buf() for _ in range(2)]
            pst = [self.ps(es, "pin_ps%d" % i, [P, 512], F32) for i in range(4)]
            pst_b = [Buf() for _ in range(4)]
            nsub = TT // P
            blk = 0
            pidx = 0
            for t in range(cfg.NT // TT):
                o, ob = xo[t % 2], xo_b[t % 2]
                for s in range(nsub):
                    xi, xib = xin[blk % NBUF], xin_b[blk % NBUF]
                    r0 = t * TT + s * P
                    K.dma(K.SP, xi[:], self.x_in[r0:r0 + P, :], reads=[], writes=[xib])
                    for g in range(4):
                        pt, ptb = pst[pidx % 4], pst_b[pidx % 4]
                        pidx += 1
                        for j in range(4):
                            c = g * 4 + j
                            K.op(K.PE, [xib, self.ident_b], [ptb],
                                 lambda e, c=c, j=j, pt=pt, xi=xi: e.transpose(
                                     out=pt[:, j * P:(j + 1) * P], in_=xi[:, c * P:(c + 1) * P],
                                     identity=self.ident[:]))
                        eng = K.ACT if (g % 2 == 0) else K.DVE
                        if eng is K.ACT:
                            K.op(eng, [ptb], [ob], lambda e, g=g, pt=pt, o=o, s=s: e.copy(
                                out=o[:, g * 4:(g + 1) * 4, s * P:(s + 1) * P],
                                in_=pt[:].rearrange("p (j q) -> p j q", j=4)))
                        else:
                            K.op(eng, [ptb], [ob], lambda e, g=g, pt=pt, o=o, s=s: e.tensor_copy(
                                out=o[:, g * 4:(g + 1) * 4, s * P:(s + 1) * P],
                                in_=pt[:].rearrange("p (j q) -> p j q", j=4)))
                    blk += 1
                K.dma(K.SP, self.xT[0].rearrange("(c p) t -> p c t", p=P)[:, :, t * TT:(t + 1) * TT],
                      o[:], reads=[ob], writes=[self.xT_buf[0][t]])
        K.barrier()

    def phase_out(self, src):
        K, nc, cfg = self.K, self.nc, self.cfg
        with contextlib.ExitStack() as es:
            xi_t = [self.sb(es, "pout_x%d" % i, [P, 16, TT], F32) for i in range(2)]
            xi_b = [Buf() for _ in range(2)]
            xo = [self.sb(es, "pout_o%d" % i, [P, D], F32) for i in range(2)]
            xo_b = [Buf() for _ in range(2)]
            pst = [self.ps(es, "pout_ps%d" % i, [P, 512], F32) for i in range(4)]
            pst_b = [Buf() for _ in range(4)]
            self.out_bufs = []
            nsub = TT // P
            blk = 0
            pidx = 0
            for t in range(cfg.NT // TT):
                xi, xib = xi_t[t % 2], xi_b[t % 2]
                K.dma(K.SP, xi[:], self.xT[src].rearrange("(c p) t -> p c t", p=P)[:, :, t * TT:(t + 1) * TT],
                      reads=[self.xT_buf[src][t]], writes=[xib])
                for s in range(nsub):
                    o, ob = xo[blk % 2], xo_b[blk % 2]
                    for g in range(4):
                        pt, ptb = pst[pidx % 4], pst_b[pidx % 4]
                        pidx += 1
                        for j in range(4):
                            c = g * 4 + j
                            K.op(K.PE, [xib, self.ident_b], [ptb],
                                 lambda e, c=c, j=j, pt=pt, xi=xi, s=s: e.transpose(
                                     out=pt[:, j * P:(j + 1) * P], in_=xi[:, c, s * P:(s + 1) * P],
                                     identity=self.ident[:]))
                        if g % 2 == 0:
                            K.op(K.ACT, [ptb], [ob], lambda e, g=g, pt=pt, o=o: e.copy(
                                out=o[:, g * 512:(g + 1) * 512], in_=pt[:]))
                        else:
                            K.op(K.DVE, [ptb], [ob], lambda e, g=g, pt=pt, o=o: e.tensor_copy(
                                out=o[:, g * 512:(g + 1) * 512], in_=pt[:]))
                    r0 = t * TT + s * P
                    fin = Buf("out")
                    K.dma(K.SP, self.out[r0:r0 + P, :], o[:], reads=[ob], writes=[fin])
                    self.out_bufs.append(fin)
                    blk += 1
        K.barrier()

    def rms_rstd(self, src, src_b, nchunks, width, dfeat, R, rows=P):
        K = self.K
        pstat, pstat_b = R["pstat"]
        for c in range(nchunks):
            q, qb = R["sq"].next()
            self.act(q[:, :width], src(c), AF.Square, [src_b(c)], [qb])
            self.mm(pstat[:, :width], self.ones_bf[:], q[:, :width], c == 0, c == nchunks - 1,
                    [qb, self.ones_b], [pstat_b])
        tmp, tmp_b = R["tmp"]
        rstd, rstd_b = R["rstd"]
        self.act(tmp[:, :width], pstat[:, :width], AF.Sqrt, [pstat_b, self.eps_b], [tmp_b],
                 scale=1.0 / dfeat, bias=self.eps_t[:])
        self.recip(rstd[:, :width], tmp[:, :width], [tmp_b], [rstd_b])

    def norm_res(self, es, tag, width=TT):
        R = {}
        R["pstat"] = (self.ps(es, tag + "pstat", [P, 512], F32), Buf())
        R["sq"] = self.sring(es, tag + "sq", 2, [P, width], BF16)
        R["tmp"] = (self.sb(es, tag + "tmp", [P, width], F32), Buf())
        R["rstd"] = (self.sb(es, tag + "rstd", [P, width], F32), Buf())
        return R

    def wslab(self, ring, wname, r0, nkc, c0, w, q=None):
        K = self.K
        t, b = ring.next()
        view = self.wb[wname][r0:r0 + nkc * P, :].rearrange("(c p) n -> p c n", p=P)
        for ca in range(0, nkc, 16):
            cb = min(nkc, ca + 16)
            K.dma(q or K.SP, t[:, ca:cb, :w], view[:, ca:cb, c0:c0 + w], reads=self.wbuf[wname], writes=[b])
        return t, b

    def tile_phase(self, tag, src, dst, n_pre, n_post, half, setup, body, post=True):
        K, nc, cfg = self.K, self.nc, self.cfg
        KC = D // P
        with contextlib.ExitStack() as es:
            xt = self.sb(es, tag + "xt", [P, KC, TT], F32)
            xt_b = [Buf() for _ in range(KC)]
            hT = self.sb(es, tag + "hT", [P, KC, TT], BF16)
            hT_b = Buf()
            R = self.norm_res(es, tag)
            rstd, rstd_b = R["rstd"]
            xr = self.sring(es, tag + "xr", 3, [P, TT], F32)
            gph = self.sb(es, tag + "gph", [P, KC], F32)
            gph_b = Buf()
            ctx = setup(es)
            gpre, gpre_b = self.vec[n_pre], self.vec_b[n_pre]
            if post:
                gpost, gpost_b = self.vec[n_post], self.vec_b[n_post]
                self.ts(K.DVE, gph[:], gpost[:], 0.5 if half else 1.0, None, ALU.mult, None, [gpost_b], [gph_b])
            srcT = self.xT[src].rearrange("(c p) t -> p c t", p=P)
            if post:
                dstT = self.xT[dst].rearrange("(c p) t -> p c t", p=P)
            for t in range(cfg.NT // TT):
                t0 = t * TT
                K.dma(K.SP, xt[:], srcT[:, :, t0:t0 + TT], reads=[self.xT_buf[src][t]], writes=xt_b)
                self.rms_rstd(lambda c: xt[:, c, :], lambda c: xt_b[c], KC, TT, D, R)
                for c in range(KC):
                    self.stt(hT[:, c, :], xt[:, c, :], gpre[:, c:c + 1], rstd[:], ALU.mult, ALU.mult,
                             [xt_b[c], gpre_b, rstd_b], [hT_b])

                def emit(n, y_ps, y_b):
                    self.cp(K.DVE, xt[:, n, :], y_ps, [y_b], [xt_b[n]])

                body(ctx, t, t0, hT, hT_b, emit)
                if not post:
                    continue
                self.rms_rstd(lambda c: xt[:, c, :], lambda c: xt_b[c], KC, TT, D, R)
                for c in range(KC):
                    r, rb = xr.next()
                    K.dma(K.SP, r[:], srcT[:, c, t0:t0 + TT], reads=[self.xT_buf[src][t]], writes=[rb])
                    self.stt(xt[:, c, :], xt[:, c, :], gph[:, c:c + 1], rstd[:], ALU.mult, ALU.mult,
                             [xt_b[c], gph_b, rstd_b], [xt_b[c]])
                    self.tt(K.DVE, xt[:, c, :], xt[:, c, :], r[:], ALU.add, [xt_b[c], rb], [xt_b[c]])
                K.dma(K.ACT, dstT[:, :, t0:t0 + TT], xt[:], reads=xt_b, writes=[self.xT_buf[dst][t]])
        K.barrier()

    def phase_ffn(self, tag, src, dst, n_pre, n_post, wg, wu, wd):
        K = self.K
        KC, FC, SW = D // P, D_FF // P, 256

        def setup(es):
            c = {}
            c["it"] = (self.sb(es, tag + "it", [P, FC, TT], BF16), Buf())
            c["wg"] = self.sring(es, tag + "wg", 2, [P, KC, SW], BF16)
            c["wu"] = self.sring(es, tag + "wu", 2, [P, KC, SW], BF16)
            c["wd"] = self.sring(es, tag + "wd", 2, [P, FC, SW], BF16)
            c["sg"] = self.sring(es, tag + "sg", 2, [P, TT], F32)
            c["pg"] = self.pring(es, tag + "pg", 2)
            c["pu"] = self.pring(es, tag + "pu", 2)
            c["py"] = self.pring(es, tag + "py", 2)
            return c

        def body(c, t, t0, hT, hT_b, emit):
            it, it_b = c["it"]
            for s in range((D_FF + SW - 1) // SW):
                c0 = s * SW
                w = min(SW, D_FF - c0)
                a, ab = self.wslab(c["wg"], wg, 0, KC, c0, w)
                u, ub = self.wslab(c["wu"], wu, 0, KC, c0, w)
                for j in range(w // P):
                    n = c0 // P + j
                    g_ps, g_b = c["pg"].next()
                    u_ps, u_b = c["pu"].next()
                    sgt, sgb = c["sg"].next()
                    for kc in range(KC):
                        self.mm(g_ps[:], a[:, kc, j * P:(j + 1) * P], hT[:, kc, :], kc == 0, kc == KC - 1,
                                [ab, hT_b], [g_b])
                    for kc in range(KC):
                        self.mm(u_ps[:], u[:, kc, j * P:(j + 1) * P], hT[:, kc, :], kc == 0, kc == KC - 1,
                                [ub, hT_b], [u_b])
                    self.act(sgt[:], g_ps[:], AF.Silu, [g_b], [sgb])
                    self.tt(K.DVE, it[:, n, :], sgt[:], u_ps[:], ALU.mult, [sgb, u_b], [it_b])
            for s in range(D // SW):
                c0 = s * SW
                dw, dwb = self.wslab(c["wd"], wd, 0, FC, c0, SW)
                for j in range(SW // P):
                    n = c0 // P + j
                    y_ps, y_b = c["py"].next()
                    for kc in range(FC):
                        self.mm(y_ps[:], dw[:, kc, j * P:(j + 1) * P], it[:, kc, :], kc == 0, kc == FC - 1,
                                [dwb, it_b], [y_b])
                    emit(n, y_ps[:], y_b)

        self.tile_phase(tag, src, dst, n_pre, n_post, True, setup, body)

    def rope_tables(self, es, t0, cs, cs_b, sn, sn_b, W):
        K = self.K
        pos_i, pos_f, tq, ti, tf, m = (W[k] for k in ("pos_i", "pos_f", "tq", "ti", "tf", "m"))
        wb = W["b"]
        K.dma(K.SP, pos_i[:], self.pos_in[t0:t0 + TT].partition_broadcast(64), reads=[], writes=[wb])
        self.cp(K.DVE, pos_f[:], pos_i[:], [wb], [wb])
        for which, out, out_b in ((0, sn, sn_b), (1, cs, cs_b)):
            self.ts(K.DVE, tq[:], pos_f[:], self.cst[0:64, 0:1], 0.25 * which, ALU.mult, ALU.add,
                    [wb, self.cst_b], [wb])
            self.cp(K.DVE, ti[:], tq[:], [wb], [wb])
            self.cp(K.DVE, tf[:], ti[:], [wb], [wb])
            self.tt(K.DVE, tq[:], tq[:], tf[:], ALU.subtract, [wb], [wb])
            self.ts(K.DVE, m[:], tq[:], 0.5, None, ALU.is_gt, None, [wb], [wb])
            self.tt(K.DVE, tq[:], tq[:], m[:], ALU.subtract, [wb], [wb])
            self.ts(K.DVE, m[:], tq[:], -0.5, None, ALU.is_lt, None, [wb], [wb])
            self.tt(K.DVE, tq[:], tq[:], m[:], ALU.add, [wb], [wb])
            self.act(out[:], tq[:], AF.Sin, [wb], [out_b], scale=2.0 * np.pi * (1.0 - 1e-6))
        self.ts(K.DVE, sn[:], sn[:], self.cst[0:64, 1:2], None, ALU.mult, None, [sn_b, self.cst_b], [sn_b])

    def rope_work(self, es, tag):
        W = {"b": Buf()}
        W["pos_i"] = self.sb(es, tag + "pos_i", [64, TT], I32)
        W["ti"] = self.sb(es, tag + "ti", [64, TT], I32)
        for k in ("pos_f", "tq", "tf", "m"):
            W[k] = self.sb(es, tag + k, [64, TT], F32)
        return W

    def phase_proj(self, src):
        K = self.K
        KC, SW = D // P, 256
        tag = "pj"
        segs = [("cq", 0, Q_LORA), ("ckv", Q_LORA, KV_LORA), ("rw", MLA_COLS, RW_COLS)]

        def setup(es):
            c = {}
            c["w"] = self.sring(es, tag + "w", 3, [P, KC, SW], BF16)
            c["wr"] = (self.sb(es, tag + "wr", [P, KC, 64], BF16), Buf())
            c["wrs"] = (self.sb(es, tag + "wrs", [P, KC, 64], BF16), Buf())
            c["pp"] = self.pring(es, tag + "pp", 4)
            c["st"] = self.sring(es, tag + "st", 4, [P, TT], F32)
            c["cs"] = (self.sb(es, tag + "cs", [64, TT], F32), Buf())
            c["sn"] = (self.sb(es, tag + "sn", [64, TT], F32), Buf())
            c["rt"] = self.sring(es, tag + "rt", 2, [64, TT], F32)
            c["W"] = self.rope_work(es, tag)
            wr, wrb = c["wr"]
            wrs, wrsb = c["wrs"]
            view = self.wb['w_in'].rearrange("(c p) n -> p c n", p=P)
            kr0 = Q_LORA + KV_LORA
            with self.nc.allow_non_contiguous_dma(reason="64-col rope weight slab"):
                K.dma(K.SP, wr[:], view[:, :, kr0:kr0 + 64], reads=self.wbuf['w_in'], writes=[wrb])
                K.dma(K.SP, wrs[:, :, 0:32], view[:, :, kr0 + 32:kr0 + 64], reads=self.wbuf['w_in'], writes=[wrsb])
                K.dma(K.SP, wrs[:, :, 32:64], view[:, :, kr0:kr0 + 32], reads=self.wbuf['w_in'], writes=[wrsb])
            return c

        def body(c, t, t0, hT, hT_b, emit):
            cs, cs_b = c["cs"]
            sn, sn_b = c["sn"]
            self.rope_tables(None, t0, cs, cs_b, sn, sn_b, c["W"])
            K.dma(K.ACT, self.scr["cs"][:, t0:t0 + TT], cs[:], reads=[cs_b], writes=self.sbufs("cs", t0, TT))
            K.dma(K.ACT, self.scr["sn"][:, t0:t0 + TT], sn[:], reads=[sn_b], writes=self.sbufs("sn", t0, TT))
            wr, wrb = c["wr"]
            wrs, wrsb = c["wrs"]
            p1, p1b = c["pp"].next()
            p2, p2b = c["pp"].next()
            for kc in range(KC):
                self.mm(p1[0:64, :], wr[:, kc, :], hT[:, kc, :], kc == 0, kc == KC - 1, [wrb, hT_b], [p1b])
            for kc in range(KC):
                self.mm(p2[0:64, :], wrs[:, kc, :], hT[:, kc, :], kc == 0, kc == KC - 1, [wrsb, hT_b], [p2b])
            r1, r1b = c["rt"].next()
            r2, r2b = c["rt"].next()
            self.tt(K.DVE, r1[:], p1[0:64, :], cs[:], ALU.mult, [p1b, cs_b], [r1b])
            self.tt(K.DVE, r2[:], p2[0:64, :], sn[:], ALU.mult, [p2b, sn_b], [r2b])
            self.tt(K.POOL, r1[:], r1[:], r2[:], ALU.add, [r1b, r2b], [r1b])
            K.dma(K.ACT, self.scr["kr"][:, t0:t0 + TT], r1[:], reads=[r1b], writes=self.sbufs("kr", t0, TT))
            for nm, col0, rows in segs:
                for s in range(rows // SW):
                    wt, wtb = self.wslab(c["w"], 'w_in', 0, KC, col0 + s * SW, SW)
                    for j in range(SW // P):
                        n = s * (SW // P) + j
                        ps, psb = c["pp"].next()
                        for kc in range(KC):
                            self.mm(ps[:], wt[:, kc, j * P:(j + 1) * P], hT[:, kc, :], kc == 0, kc == KC - 1,
                                    [wtb, hT_b], [psb])
                        st, stb = c["st"].next()
                        self.cp(K.ACT if n % 2 == 0 else K.DVE, st[:], ps[:], [psb], [stb])
                        K.dma(K.ACT, self.scr[nm][n * P:(n + 1) * P, t0:t0 + TT], st[:], reads=[stb],
                              writes=self.sbufs(nm, t0, TT))

        self.tile_phase(tag, src, None, 'n_mix_pre', None, False, setup, body, post=False)

    def phase_mla(self):
        K, cfg = self.K, self.cfg
        S, NB = cfg.S, cfg.NB
        tag = "ml"
        NKB = S // P
        scale = float((QK_NOPE + QK_ROPE) ** -0.5)
        with contextlib.ExitStack() as es:
            wuq = self.sb(es, tag + "wuq", [P, 4, 1536], BF16)
            wuqs = self.sb(es, tag + "wuqs", [P, 4, 8, 64], BF16)
            wukv = self.sb(es, tag + "wukv", [P, 2, 2048], BF16)
            woa = self.sb(es, tag + "woa", [P, 8, D], BF16)
            wuq_b, wuqs_b, wukv_b, woa_b = Buf(), Buf(), Buf(), Buf()
            vq = self.wb['w_uq'].rearrange("(c p) n -> p c n", p=P)
            K.dma(K.SP, wuq[:], vq, reads=self.wbuf['w_uq'], writes=[wuq_b])
            vq4 = self.wb['w_uq'].rearrange("(c p) (h d) -> p c h d", p=P, h=8)
            with self.nc.allow_non_contiguous_dma(reason="rope weight half-swap (64B runs)"):
                for kc in range(4):
                    K.dma(K.SP, wuqs[:, kc, :, 0:32], vq4[:, kc, :, 160:192], reads=self.wbuf['w_uq'], writes=[wuqs_b])
                    K.dma(K.SP, wuqs[:, kc, :, 32:64], vq4[:, kc, :, 128:160], reads=self.wbuf['w_uq'], writes=[wuqs_b])
            K.dma(K.SP, wukv[:], self.wb['w_ukv'].rearrange("(c p) n -> p c n", p=P),
                  reads=self.wbuf['w_ukv'], writes=[wukv_b])
            K.dma(K.SP, woa[:], self.wb['w_oa'].rearrange("(c p) n -> p c n", p=P),
                  reads=self.wbuf['w_oa'], writes=[woa_b])
            wukv4 = wukv[:].rearrange("p c (h two d) -> p c h two d", h=8, two=2)

            Kn = self.sb(es, tag + "Kn", [P, 8, S], BF16)
            Kn_b = [Buf() for _ in range(S // TT)]
            Vs = self.sb(es, tag + "Vs", [P, NKB, 1024], BF16)
            Vs_b = [Buf() for _ in range(NKB)]
            Kr = self.sb(es, tag + "Kr", [64, S], BF16)
            Kr_b = [Buf() for _ in range(S // TT)]
            pt = self.sb(es, tag + "pt", [P, NKB, TT], BF16)
            pt_b = [Buf() for _ in range(NKB)]
            lat = self.sb(es, tag + "lat", [P, 4, TT], F32)
            lat_b = [Buf() for _ in range(4)]
            latn = self.sb(es, tag + "latn", [P, 4, TT], BF16)
            latn_b = Buf()
            krl = self.sb(es, tag + "krl", [64, TT], F32)
            krl_b = Buf()
            cs = self.sb(es, tag + "cs", [64, TT], F32)
            sn = self.sb(es, tag + "sn", [64, TT], F32)
            cs_b, sn_b = Buf(), Buf()
            qn_r = self.sring(es, tag + "qn", 2, [P, TT], BF16)
            qr_r = self.sring(es, tag + "qr", 2, [64, TT], BF16)
            rt = self.sring(es, tag + "rt", 4, [64, TT], F32)
            rec = self.sring(es, tag + "rec", 2, [P, TT], F32)
            oT = self.sb(es, tag + "oT", [P, 8, TT], BF16)
            oT_b = [Buf() for _ in range(8)]
            yst = self.sring(es, tag + "yst", 4, [P, TT], F32)
            R = self.norm_res(es, tag)
            rstd, rstd_b = R["rstd"]
            sps = self.pring(es, tag + "sps", 2)
            ops_, ops_b = self.ps(es, tag + "ops", [P, 512], F32), Buf()
            rps, rps_b = self.ps(es, tag + "rps", [P, 512], F32), Buf()
            mps = self.pring(es, tag + "mps", 3)
            nq, nq_b = self.vec['n_q_lat'], self.vec_b['n_q_lat']
            nkv, nkv_b = self.vec['n_kv_lat'], self.vec_b['n_kv_lat']

            for b in range(NB):
                for tt_ in range(S // TT):
                    t0 = b * S + tt_ * TT
                    K.dma(K.SP, lat[:, 0:2, :], self.scr["ckv"].rearrange("(c p) t -> p c t", p=P)[:, :, t0:t0 + TT],
                          reads=self.sbufs("ckv", t0, TT), writes=lat_b[0:2])
                    self.rms_rstd(lambda c: lat[:, c, :], lambda c: lat_b[c], 2, TT, KV_LORA, R)
                    for c in range(2):
                        self.stt(latn[:, c, :], lat[:, c, :], nkv[:, c:c + 1], rstd[:], ALU.mult, ALU.mult,
                                 [lat_b[c], nkv_b, rstd_b], [latn_b])
                    for h in range(8):
                        ps, psb = mps.next()
                        for kc in range(2):
                            self.mm(ps[:], wukv[:, kc, h * 256:h * 256 + 128], latn[:, kc, :], kc == 0, kc == 1,
                                    [wukv_b, latn_b], [psb])
                        self.cp(K.ACT if h % 2 == 0 else K.DVE, Kn[:, h, tt_ * TT:(tt_ + 1) * TT], ps[:],
                                [psb], [Kn_b[tt_]])
                    for sb_ in range(TT // P):
                        blk = tt_ * (TT // P) + sb_
                        for g in range(2):
                            ps, psb = mps.next()
                            for kc in range(2):
                                self.mm(ps[:].rearrange("p (h d) -> p h d", h=4),
                                        latn[:, kc, sb_ * P:(sb_ + 1) * P], wukv4[:, kc, g * 4:(g + 1) * 4, 1, :],
                                        kc == 0, kc == 1, [wukv_b, latn_b], [psb])
                            self.cp(K.ACT if g == 0 else K.DVE, Vs[:, blk, g * 512:(g + 1) * 512], ps[:],
                                    [psb], [Vs_b[blk]])
                    K.dma(K.SP, krl[:], self.scr["kr"][:, t0:t0 + TT], reads=self.sbufs("kr", t0, TT), writes=[krl_b])
                    self.cp(K.POOL, Kr[:, tt_ * TT:(tt_ + 1) * TT], krl[:], [krl_b], [Kr_b[tt_]])
                for qt in range(S // TT):
                    t0 = b * S + qt * TT
                    K.dma(K.SP, lat[:], self.scr["cq"].rearrange("(c p) t -> p c t", p=P)[:, :, t0:t0 + TT],
                          reads=self.sbufs("cq", t0, TT), writes=lat_b)
                    K.dma(K.SP, cs[:], self.scr["cs"][:, t0:t0 + TT], reads=self.sbufs("cs", t0, TT), writes=[cs_b])
                    K.dma(K.SP, sn[:], self.scr["sn"][:, t0:t0 + TT], reads=self.sbufs("sn", t0, TT), writes=[sn_b])
                    self.rms_rstd(lambda c: lat[:, c, :], lambda c: lat_b[c], 4, TT, Q_LORA, R)
                    for c in range(4):
                        self.stt(latn[:, c, :], lat[:, c, :], nq[:, c:c + 1], rstd[:], ALU.mult, ALU.mult,
                                 [lat_b[c], nq_b, rstd_b], [latn_b])
                    nkb = 4 * (qt + 1)
                    for h in range(8):
                        ps, psb = mps.next()
                        for kc in range(4):
                            self.mm(ps[:], wuq[:, kc, h * 192:h * 192 + 128], latn[:, kc, :], kc == 0, kc == 3,
                                    [wuq_b, latn_b], [psb])
                        qn, qnb = qn_r.next()
                        self.cp(K.ACT, qn[:], ps[:], [psb], [qnb])
                        p1, p1b = mps.next()
                        p2, p2b = mps.next()
                        for kc in range(4):
                            self.mm(p1[0:64, :], wuq[:, kc, h * 192 + 128:h * 192 + 192], latn[:, kc, :],
                                    kc == 0, kc == 3, [wuq_b, latn_b], [p1b])
                        for kc in range(4):
                            self.mm(p2[0:64, :], wuqs[:, kc, h, :], latn[:, kc, :], kc == 0, kc == 3,
                                    [wuqs_b, latn_b], [p2b])
                        r1, r1b = rt.next()
                        r2, r2b = rt.next()
                        self.tt(K.DVE, r1[:], p1[0:64, :], cs[:], ALU.mult, [p1b, cs_b], [r1b])
                        self.tt(K.DVE, r2[:], p2[0:64, :], sn[:], ALU.mult, [p2b, sn_b], [r2b])
                        qr, qrb = qr_r.next()
                        self.tt(K.POOL, qr[:], r1[:], r2[:], ALU.add, [r1b, r2b], [qrb])
                        for kb in range(nkb):
                            qlo = max(0, kb * P - qt * TT)
                            sp, spb = sps.next()
                            self.mm(sp[:, qlo:], Kn[:, h, kb * P:(kb + 1) * P], qn[:, qlo:], True, False,
                                    [Kn_b[kb // 4], qnb], [spb])
                            self.mm(sp[:, qlo:], Kr[:, kb * P:(kb + 1) * P], qr[:, qlo:], False, True,
                                    [Kr_b[kb // 4], qrb], [spb])
                            self.act(pt[:, kb, qlo:], sp[:, qlo:], AF.Exp, [spb], [pt_b[kb]], scale=scale)
                            if kb * P >= qt * TT:
                                self.mset(K.POOL, pt[64:128, kb, qlo:qlo + 64], 0.0, [], [pt_b[kb]])
                        for kb in range(nkb):
                            qlo = max(0, kb * P - qt * TT)
                            self.mm(ops_[:, qlo:], Vs[:, kb, h * P:(h + 1) * P], pt[:, kb, qlo:], kb == 0, kb == nkb - 1,
                                    [Vs_b[kb], pt_b[kb]], [ops_b])
                        for kb in range(nkb):
                            qlo = max(0, kb * P - qt * TT)
                            self.mm(rps[:, qlo:], self.ones_bf[:], pt[:, kb, qlo:], kb == 0, kb == nkb - 1,
                                    [self.ones_b, pt_b[kb]], [rps_b])
                        rc, rcb = rec.next()
                        self.recip(rc[:], rps[:], [rps_b], [rcb])
                        self.tt(K.DVE, oT[:, h, :], ops_[:], rc[:], ALU.mult, [ops_b, rcb], [oT_b[h]])
                    for n in range(16):
                        ps, psb = mps.next()
                        for h in range(8):
                            self.mm(ps[:], woa[:, h, n * P:(n + 1) * P], oT[:, h, :], h == 0, h == 7,
                                    [woa_b, oT_b[h]], [psb])
                        st, stb = yst.next()
                        self.cp(K.ACT if n % 2 == 0 else K.DVE, st[:], ps[:], [psb], [stb])
                        K.dma(K.ACT, self.scr["ya"][n * P:(n + 1) * P, t0:t0 + TT], st[:], reads=[stb],
                              writes=self.sbufs("ya", t0, TT))
        K.barrier()

    def phase_merge(self, src, dst, use_b=True):
        K = self.K
        KC, SW = D // P, 256
        tag = "mg"

        def setup(es):
            c = {}
            c["wga"] = self.sring(es, tag + "wga", 2, [P, KC, SW], BF16)
            c["wgb"] = self.sring(es, tag + "wgb", 2, [P, KC, SW], BF16)
            c["wo"] = self.sring(es, tag + "wo", 2, [P, KC, SW], BF16)
            c["mT"] = (self.sb(es, tag + "mT", [P, KC, TT], BF16), Buf())
            c["ya"] = self.sring(es, tag + "ya", 4, [P, TT], F32)
            c["yb"] = self.sring(es, tag + "yb", 4, [P, TT], F32)
            c["sa"] = self.sring(es, tag + "sa", 2, [P, TT], F32)
            c["sbb"] = self.sring(es, tag + "sbb", 2, [P, TT], F32)
            c["pa"] = self.pring(es, tag + "pa", 2)
            c["pb"] = self.pring(es, tag + "pb", 2)
            c["py"] = self.pring(es, tag + "py", 2)
            return c

        bg, bg_b = self.vec['b_gate'], self.vec_b['b_gate']

        def body(c, t, t0, hT, hT_b, emit):
            mT, mT_b = c["mT"]
            for s in range(D // SW):
                wa, wab = self.wslab(c["wga"], 'w_in', 0, KC, GATE0 + s * SW, SW)
                wb_, wbb = self.wslab(c["wgb"], 'w_in', 0, KC, GATE0 + D + s * SW, SW)
                for j in range(SW // P):
                    n = s * (SW // P) + j
                    pa, pab = c["pa"].next()
                    pb, pbb = c["pb"].next()
                    for kc in range(KC):
                        self.mm(pa[:], wa[:, kc, j * P:(j + 1) * P], hT[:, kc, :], kc == 0, kc == KC - 1,
                                [wab, hT_b], [pab])
                    for kc in range(KC):
                        self.mm(pb[:], wb_[:, kc, j * P:(j + 1) * P], hT[:, kc, :], kc == 0, kc == KC - 1,
                                [wbb, hT_b], [pbb])
                    ya, yab = c["ya"].next()
                    yb, ybb = c["yb"].next()
                    K.dma(K.SP, ya[:], self.scr["ya"][n * P:(n + 1) * P, t0:t0 + TT],
                          reads=self.sbufs("ya", t0, TT), writes=[yab])
                    sa, sab = c["sa"].next()
                    self.act(sa[:], pa[:], AF.Sigmoid, [pab, bg_b], [sab], bias=bg[:, n:n + 1])
                    self.tt(K.DVE, sa[:], sa[:], ya[:], ALU.mult, [sab, yab], [sab])
                    if use_b:
                        K.dma(K.SP, yb[:], self.scr["yb"][n * P:(n + 1) * P, t0:t0 + TT],
                              reads=self.sbufs("yb", t0, TT), writes=[ybb])
                        sbb, sbbb = c["sbb"].next()
                        self.act(sbb[:], pb[:], AF.Sigmoid, [pbb, bg_b], [sbbb], bias=bg[:, 16 + n:17 + n])
                        self.tt(K.POOL, sbb[:], sbb[:], yb[:], ALU.mult, [sbbb, ybb], [sbbb])
                        self.tt(K.DVE, mT[:, n, :], sa[:], sbb[:], ALU.add, [sab, sbbb], [mT_b])
                    else:
                        self.cp(K.DVE, mT[:, n, :], sa[:], [sab], [mT_b])
            for s in range(D // SW):
                wo, wob = self.wslab(c["wo"], 'w_o', 0, KC, s * SW, SW)
                for j in range(SW // P):
                    n = s * (SW // P) + j
                    y_ps, y_b = c["py"].next()
                    for kc in range(KC):
                        self.mm(y_ps[:], wo[:, kc, j * P:(j + 1) * P], mT[:, kc, :], kc == 0, kc == KC - 1,
                                [wob, mT_b], [y_b])
                    emit(n, y_ps[:], y_b)

        self.tile_phase(tag, src, dst, 'n_mix_pre', 'n_mix_post', False, setup, body)

    def phase_xattn(self, src, dst):
        K, cfg = self.K, self.cfg
        KC, SW = D // P, 256
        tag = "xa"
        scale = float(MEM_D ** -0.5)
        NMB = N_MEM // P

        def setup(es):
            c = {}
            c["wq"] = self.sring(es, tag + "wq", 2, [P, KC, SW], BF16)
            c["wkv"] = self.sring(es, tag + "wkv", 2, [P, KC, SW], BF16)
            c["wco"] = self.sring(es, tag + "wco", 2, [P, 8, SW], BF16)
            c["qT"] = (self.sb(es, tag + "qT", [P, 8, TT], BF16), [Buf() for _ in range(8)])
            c["oT"] = (self.sb(es, tag + "oT", [P, 8, TT], BF16), [Buf() for _ in range(8)])
            c["KT"] = (self.sb(es, tag + "KT", [P, cfg.NB, 8, N_MEM], BF16), Buf())
            c["Vm"] = (self.sb(es, tag + "Vm", [P, cfg.NB, NMB, 1024], BF16), Buf())
            c["pt"] = self.sring(es, tag + "pt", 4, [P, TT], BF16)
            c["rec"] = self.sring(es, tag + "rec", 2, [P, TT], F32)
            c["pm"] = self.pring(es, tag + "pm", 3)
            c["po"] = (self.ps(es, tag + "po", [P, 512], F32), Buf())
            c["pr"] = (self.ps(es, tag + "pr", [P, 512], F32), Buf())
            KT, KT_b = c["KT"]
            Vm, Vm_b = c["Vm"]
            with contextlib.ExitStack() as es2:
                mt = self.sring(es2, tag + "mt", 2, [P, D], F32)
                mT = self.sb(es2, tag + "mTm", [P, KC, N_MEM], F32)
                mT_b = [Buf() for _ in range(KC)]
                mn = self.sb(es2, tag + "mn", [P, KC, N_MEM], BF16)
                mn_b = Buf()
                R = self.norm_res(es2, tag + "m", width=N_MEM)
                rstd, rstd_b = R["rstd"]
                gm, gm_b = self.vec['n_mem'], self.vec_b['n_mem']
                for b in range(cfg.NB):
                    for mb in range(NMB):
                        m_, m_b = mt.next()
                        r0 = b * N_MEM + mb * P
                        K.dma(K.SP, m_[:], self.mem_in[r0:r0 + P, :], reads=[], writes=[m_b])
                        for g in range(4):
                            ps, psb = c["pm"].next()
                            for j in range(4):
                                self.tp(ps[:, j * P:(j + 1) * P], m_[:, (g * 4 + j) * P:(g * 4 + j + 1) * P],
                                        self.ident[:], [m_b, self.ident_b], [psb])
                            self.cp(K.ACT if g % 2 == 0 else K.DVE, mT[:, g * 4:(g + 1) * 4, mb * P:(mb + 1) * P],
                                    ps[:].rearrange("p (j q) -> p j q", j=4), [psb], mT_b[g * 4:(g + 1) * 4])
                    self.rms_rstd(lambda cc: mT[:, cc, :], lambda cc: mT_b[cc], KC, N_MEM, D, R)
                    for cc in range(KC):
                        self.stt(mn[:, cc, :], mT[:, cc, :], gm[:, cc:cc + 1], rstd[:, :N_MEM], ALU.mult, ALU.mult,
                                 [mT_b[cc], gm_b, rstd_b], [mn_b])
                    for s in range(2048 // SW):
                        wt, wtb = self.wslab(c["wkv"], 'w_ckv', 0, KC, s * SW, SW)
                        h, part = s // 2, s % 2
                        if part == 0:
                            for j in range(2):
                                ps, psb = c["pm"].next()
                                for kc in range(KC):
                                    self.mm(ps[:, :N_MEM], wt[:, kc, j * P:(j + 1) * P], mn[:, kc, :], kc == 0,
                                            kc == KC - 1, [wtb, mn_b], [psb])
                                self.cp(K.ACT, KT[:, b, h * 2 + j, :], ps[:, :N_MEM], [psb], [KT_b])
                        else:
                            for mb in range(NMB):
                                ps, psb = c["pm"].next()
                                for kc in range(KC):
                                    self.mm(ps[:, :SW], mn[:, kc, mb * P:(mb + 1) * P], wt[:, kc, :], kc == 0,
                                            kc == KC - 1, [wtb, mn_b], [psb])
                                self.cp(K.DVE, Vm[:, b, mb, h * 256:(h + 1) * 256], ps[:, :SW], [psb], [Vm_b])
                K.barrier()
            c["py"] = self.pring(es, tag + "py", 2)
            return c

        def body(c, t, t0, hT, hT_b, emit):
            b = t0 // cfg.S
            qT, qT_b = c["qT"]
            oT, oT_b = c["oT"]
            KT, KT_b = c["KT"]
            Vm, Vm_b = c["Vm"]
            for s in range(1024 // SW):
                wt, wtb = self.wslab(c["wq"], 'w_cq', 0, KC, s * SW, SW)
                for j in range(SW // P):
                    n = s * (SW // P) + j
                    ps, psb = c["pm"].next()
                    for kc in range(KC):
                        self.mm(ps[:], wt[:, kc, j * P:(j + 1) * P], hT[:, kc, :], kc == 0, kc == KC - 1,
                                [wtb, hT_b], [psb])
                    self.cp(K.ACT, qT[:, n, :], ps[:], [psb], [qT_b[n]])
            po, po_b = c["po"]
            pr, pr_b = c["pr"]
            for h in range(MEM_H):
                pts = []
                for mb in range(NMB):
                    ps, psb = c["pm"].next()
                    for dc in range(2):
                        self.mm(ps[:], KT[:, b, h * 2 + dc, mb * P:(mb + 1) * P], qT[:, h * 2 + dc, :], dc == 0, dc == 1,
                                [KT_b, qT_b[h * 2 + dc]], [psb])
                    p_, p_b = c["pt"].next()
                    self.act(p_[:], ps[:], AF.Exp, [psb], [p_b], scale=scale)
                    pts.append((p_, p_b))
                for mb in range(NMB):
                    self.mm(pr[:], self.ones_bf[:], pts[mb][0][:], mb == 0, mb == NMB - 1,
                            [self.ones_b, pts[mb][1]], [pr_b])
                rc, rcb = c["rec"].next()
                self.recip(rc[:], pr[:], [pr_b], [rcb])
                for dc in range(2):
                    for mb in range(NMB):
                        self.mm(po[:], Vm[:, b, mb, h * 256 + dc * P:h * 256 + (dc + 1) * P], pts[mb][0][:],
                                mb == 0, mb == NMB - 1, [Vm_b, pts[mb][1]], [po_b])
                    self.tt(K.DVE, oT[:, h * 2 + dc, :], po[:], rc[:], ALU.mult, [po_b, rcb], [oT_b[h * 2 + dc]])
            for s in range(D // SW):
                wt, wtb = self.wslab(c["wco"], 'w_co', 0, 8, s * SW, SW)
                for j in range(SW // P):
                    n = s * (SW // P) + j
                    y_ps, y_b = c["py"].next()
                    for kc in range(8):
                        self.mm(y_ps[:], wt[:, kc, j * P:(j + 1) * P], oT[:, kc, :], kc == 0, kc == 7,
                                [wtb, oT_b[kc]], [y_b])
                    emit(n, y_ps[:], y_b)

        self.tile_phase(tag, src, dst, 'n_x_pre', 'n_x_post', False, setup, body)

    def phase_rwkv(self):
        K, cfg = self.K, self.cfg
        S, NB = cfg.S, cfg.NB
        RT, C = 256, 64
        NCH = RT // C
        NPAIR = 16 * NCH
        tag = "rk"
        with contextlib.ExitStack() as es:
            lww = self.sb(es, tag + "lww", [P, RW], BF16)
            wg2 = self.sb(es, tag + "wg2", [P, RW], BF16)
            wob = self.sb(es, tag + "wob", [P, 8, D], BF16)
            lww_b, wg2_b, wob_b = Buf(), Buf(), Buf()
            K.dma(K.SP, lww[0:64, :], self.wb['w_w2'][:, :], reads=self.wbuf['w_w2'], writes=[lww_b])
            K.dma(K.SP, lww[64:128, :], self.wb['w_a2'][:, :], reads=self.wbuf['w_a2'], writes=[lww_b])
            K.dma(K.SP, wg2[:], self.wb['w_g2'][:, :], reads=self.wbuf['w_g2'], writes=[wg2_b])
            K.dma(K.SP, wob[:], self.wb['w_ob'].rearrange("(c p) n -> p c n", p=P),
                  reads=self.wbuf['w_ob'], writes=[wob_b])
            m2 = self.sb(es, tag + "m2", [P, 1280], F32)
            m2_b = Buf()
            K.dma(K.SP, m2[:], self.msk2_in[:, 0:1280], reads=[], writes=[m2_b])
            lmask = m2[0:64, 0:512]
            imask = m2[0:64, 512:1024]
            rsm = m2[:, 1024:1280]
            eps2 = self.sb(es, tag + "eps2", [P, 1], F32)
            eps2_b = Buf()
            self.mset(K.DVE, eps2[:], 64e-5, [], [eps2_b])
            omk = self.sb(es, tag + "omk", [P, 8], F32)
            omk_b = Buf()
            self.ts(K.DVE, omk[:], self.vec['k_a'][:], -1.0, 1.0, ALU.mult, ALU.add, [self.vec_b['k_a']], [omk_b])
            V = self.vec
            Vb = self.vec_b

            yt = self.sring(es, tag + "yt", 3, [P, RT + 1], F32)
            tmp = self.sring(es, tag + "tmp", 21, [P, RT], F32)
            tbf = self.sring(es, tag + "tbf", 4, [P, RT], BF16)
            dwa = self.sb(es, tag + "dwa", [P, RT], BF16)
            dwa_b = Buf()
            sgt = self.sb(es, tag + "sgt", [P, RT], BF16)
            sgt_b = Buf()
            NP2 = 16 * 2
            AR = self.sb(es, tag + "AR", [P, 8, NCH, 2, C], BF16)
            VV = self.sb(es, tag + "VV", [P, 8, NCH, 2, C], BF16)
            BKz = [self.sb(es, tag + "BKz%d" % i, [P, 8, NCH, 2, C], BF16) for i in range(2)]
            Az = [self.sb(es, tag + "Az%d" % i, [P, 8, NCH, C], BF16) for i in range(2)]
            AR_b = [Buf() for _ in range(8)]
            BK_b = [Buf() for _ in range(8)]
            VV_b = [Buf() for _ in range(8)]
            self.mset(K.POOL, VV[:], 0.0, [], VV_b)
            for i in range(2):
                self.mset(K.POOL, BKz[i][:], 0.0, [], BK_b)
                self.mset(K.POOL, Az[i][:], 0.0, [], AR_b)
            bonus = self.sb(es, tag + "bonus", [P, 8, RT], F32)
            bonus_b = [Buf() for _ in range(8)]
            PC = self.sb(es, tag + "PC", [P, 8, NCH], F32)
            PC_b = [Buf() for _ in range(8)]
            Gm = self.sb(es, tag + "Gm", [P, NP2, 128], BF16)
            Gm_b = [Buf() for _ in range(NP2 // 4)]
            Ltb = self.sb(es, tag + "Ltb", [P, 16, 128], BF16)
            Lnb = self.sb(es, tag + "Lnb", [P, 16, 128], BF16)
            Gb = self.sb(es, tag + "Gb", [P, 16, 128], BF16)
            Ltb_b = [Buf() for _ in range(4)]
            Lnb_b = [Buf() for _ in range(4)]
            Gb_b = [Buf() for _ in range(4)]
            m3 = self.sb(es, tag + "m3", [P, 1536], F32)
            m3_b = Buf()
            K.dma(K.SP, m3[:], self.msk3_in[:, :], reads=[], writes=[m3_b])
            TTa = self.sb(es, tag + "TTa", [64, NP2, C], BF16)
            TT_b = [Buf() for _ in range(NP2 // 8)]
            lt_r = self.sring(es, tag + "ltr", 7, [P, 4, 128], BF16)
            ln_r = self.sring(es, tag + "lnr", 7, [P, 4, 128], BF16)
            BKt = self.sb(es, tag + "BKt", [P, NP2, 128], BF16)
            BKt_b = [Buf() for _ in range(NP2 // 4)]
            UV = self.sb(es, tag + "UV", [P, NP2, 128], BF16)
            UVv_b = [Buf() for _ in range(4)]
            UVu_b = [[Buf() for _ in range(2)] for _ in range(2)]
            self.mset(K.POOL, UV[:], 0.0, [], UVv_b + UVu_b[0] + UVu_b[1])
            UV5 = UV[:].rearrange("p (h r c) v -> p h r c v", r=2, c=2)
            Hf = self.sb(es, tag + "Hf", [P, 8, 128], F32)
            Hb = self.sb(es, tag + "Hb", [P, 8, 128], BF16)
            Hf_b = [Buf(), Buf()]
            Hb_b = [Buf(), Buf()]
            Xs = self.sring(es, tag + "Xs", 4, [64, 512], BF16)
            osb = self.sb(es, tag + "osb", [P, 8, RT], F32)
            osb_b = [[Buf() for _ in range(NCH)] for _ in range(2)]
            og = self.sb(es, tag + "og", [P, 8, RT], BF16)
            og_b = [Buf() for _ in range(8)]
            yst = self.sring(es, tag + "yst", 2, [P, RT], F32)
            pa = self.pring(es, tag + "pa", 4)
            ptr = self.pring(es, tag + "ptr", 2, [P, 512], BF16)
            pc = self.pring(es, tag + "pc", 2)
            rwT = self.scr["rw"]

            def v3(ap):
                return ap.rearrange("p (c t) -> p c t", t=C)

            def load_shift(ci, t0, first):
                y, yb = yt.next()
                if first:
                    self.mset(K.POOL, y[:, 0:1], 0.0, [], [yb])
                    K.dma(K.SP, y[:, 1:RT + 1], rwT[ci * P:(ci + 1) * P, t0:t0 + RT],
                          reads=self.sbufs("rw", t0, RT), writes=[yb])
                else:
                    K.dma(K.SP, y[:, :], rwT[ci * P:(ci + 1) * P, t0 - 1:t0 + RT],
                          reads=self.sbufs("rw", t0 - 1, RT + 1), writes=[yb])
                d, db = tmp.next()
                self.tt(K.DVE, d[:], y[:, 0:RT], y[:, 1:RT + 1], ALU.subtract, [yb], [db])
                o, ob = tmp.next()
                self.stt(o[:], d[:], V['mu_shift'][:, ci:ci + 1], y[:, 1:RT + 1], ALU.mult, ALU.add,
                         [db, yb, Vb['mu_shift']], [ob])
                return o, ob

            for b in range(NB):
                self.mset(K.DVE, Hf[:], 0.0, [], Hf_b)
                self.mset(K.DVE, Hb[:], 0.0, [], Hb_b)
                for ti in range(S // RT):
                    t0 = b * S + ti * RT
                    first = (ti == 0)
                    p24, p24b = load_shift(24, t0, first)
                    self.act(dwa[0:64, :], p24[0:64, :], AF.Tanh, [p24b], [dwa_b])
                    self.cp(K.DVE, dwa[64:128, :], p24[64:128, :], [p24b], [dwa_b])
                    p25, p25b = load_shift(25, t0, first)
                    self.act(sgt[:], p25[:], AF.Sigmoid, [p25b], [sgt_b])
                    for n in range(8):
                        r, rb = load_shift(n, t0, first)
                        k, kb = load_shift(8 + n, t0, first)
                        v, vb = load_shift(16 + n, t0, first)
                        ps, psb = pa.next()
                        self.mm(ps[:, :RT], lww[0:64, n * P:(n + 1) * P], dwa[0:64, :], True, True,
                                [lww_b, dwa_b], [psb])
                        lw, lwb = tmp.next()
                        self.act(lw[:], ps[:, :RT], AF.Sigmoid, [psb, Vb['w0']], [lwb], bias=V['w0'][:, n:n + 1])
                        self.ts(K.DVE, lw[:], lw[:], -0.6065306597126334, None, ALU.mult, None, [lwb], [lwb])
                        ps2, ps2b = pa.next()
                        self.mm(ps2[:, :RT], lww[64:128, n * P:(n + 1) * P], dwa[64:128, :], True, True,
                                [lww_b, dwa_b], [ps2b])
                        a, ab = tmp.next()
                        self.act(a[:], ps2[:, :RT], AF.Sigmoid, [ps2b, Vb['a0']], [ab], bias=V['a0'][:, n:n + 1])
                        cs, csb = tmp.next()
                        K.op(K.DVE, [lwb, m2_b], [csb], lambda e, cs=cs, lw=lw: e.tensor_tensor_scan(
                            out=cs[:], data0=rsm, data1=lw[:], initial=0.0, op0=ALU.mult, op1=ALU.add))
                        e_in, e_inb = tmp.next()
                        self.act(e_in[:], cs[:], AF.Exp, [csb], [e_inb])
                        e_ng, e_ngb = tmp.next()
                        self.act(e_ng[:], cs[:], AF.Exp, [csb], [e_ngb], scale=-1.0)
                        dx, dxb = tmp.next()
                        self.tt(K.POOL, dx[:], cs[:], lw[:], ALU.subtract, [csb, lwb], [dxb])
                        e_ex, e_exb = tmp.next()
                        self.act(e_ex[:], dx[:], AF.Exp, [dxb], [e_exb])
                        self.cp(K.POOL, PC[:, n, :], v3(e_in[:])[:, :, C - 1], [e_inb], [PC_b[n]])
                        kk, kkb = tmp.next()
                        self.ts(K.DVE, kk[:], k[:], V['k_k'][:, n:n + 1], None, ALU.mult, None, [kb, Vb['k_k']], [kkb])
                        q2, q2b = tbf.next()
                        self.act(q2[:], kk[:], AF.Square, [kkb], [q2b])
                        ps3, ps3b = pa.next()
                        self.mm(ps3[:, :RT], self.bones_bf[:], q2[:], True, True, [self.bones_b, q2b], [ps3b])
                        nr, nrb = tmp.next()
                        self.act(nr[:], ps3[:, :RT], AF.Sqrt, [ps3b], [nrb])
                        self.ts(K.DVE, nr[:], nr[:], 1e-12, None, ALU.max, None, [nrb], [nrb])
                        self.recip(nr[:], nr[:], [nrb], [nrb])
                        self.tt(K.DVE, kk[:], kk[:], nr[:], ALU.mult, [kkb, nrb], [kkb])
                        km, kmb = tmp.next()
                        self.ts(K.DVE, km[:], a[:], V['k_a'][:, n:n + 1], omk[:, n:n + 1], ALU.mult, ALU.add,
                                [ab, Vb['k_a'], omk_b], [kmb])
                        self.tt(K.DVE, km[:], km[:], k[:], ALU.mult, [kmb, kb], [kmb])
                        rk, rkb = tbf.next()
                        self.stt(rk[:], r[:], V['r_k'][:, n:n + 1], km[:], ALU.mult, ALU.mult,
                                 [rb, kmb, Vb['r_k']], [rkb])
                        ps4, ps4b = pa.next()
                        self.mm(ps4[:, :RT], self.bones_bf[:], rk[:], True, True, [self.bones_b, rkb], [ps4b])
                        self.tt(K.DVE, bonus[:, n, :], ps4[:, :RT], v[:], ALU.mult, [ps4b, vb], [bonus_b[n]])
                        at, atb = tmp.next()
                        self.stt(at[:], kk[:], -1.0, e_ex[:], ALU.mult, ALU.mult, [kkb, e_exb], [atb])
                        self.cp(K.ACT, AR[:, n, :, 0, :], v3(at[:]), [atb], [AR_b[n]])
                        self.cp(K.POOL, Az[0][0:64, n, :, :], v3(at[0:64, :]), [atb], [AR_b[n]])
                        self.cp(K.POOL, Az[1][64:128, n, :, :], v3(at[64:128, :]), [atb], [AR_b[n]])
                        self.tt(K.POOL, AR[:, n, :, 1, :], v3(r[:]), v3(e_in[:]), ALU.mult, [rb, e_inb], [AR_b[n]])
                        kb2, kb2b = tmp.next()
                        self.tt(K.POOL, kb2[:], kk[:], a[:], ALU.mult, [kkb, ab], [kb2b])
                        self.tt(K.DVE, kb2[:], kb2[:], e_ng[:], ALU.mult, [kb2b, e_ngb], [kb2b])
                        kt, ktb = tmp.next()
                        self.tt(K.POOL, kt[:], km[:], e_ng[:], ALU.mult, [kmb, e_ngb], [ktb])
                        self.cp(K.ACT, BKz[0][0:64, n, :, 0, :], v3(kb2[0:64, :]), [kb2b], [BK_b[n]])
                        self.cp(K.ACT, BKz[1][64:128, n, :, 0, :], v3(kb2[64:128, :]), [kb2b], [BK_b[n]])
                        self.cp(K.DVE, BKz[0][0:64, n, :, 1, :], v3(kt[0:64, :]), [ktb], [BK_b[n]])
                        self.cp(K.DVE, BKz[1][64:128, n, :, 1, :], v3(kt[64:128, :]), [ktb], [BK_b[n]])
                        self.cp(K.ACT, VV[:, n, :, 1, :], v3(v[:]), [vb], [VV_b[n]])

                    def f2(ap):
                        return ap.rearrange("p a t -> p (a t)")

                    if getattr(cfg, 'rstop', 99) < 2:
                        continue
                    for hf in range(2):
                        def pr(q, hf=hf):
                            h, cl = divmod(q, 2)
                            return h, h // 2, h % 2, 2 * hf + cl
                        for g4 in range(NP2 // 4):
                            ps, psb = pa.next()
                            for i in range(4):
                                h, hp, par, c = pr(g4 * 4 + i)
                                self.mm(ps[:, i * 128:(i + 1) * 128], f2(BKz[par][:, hp, c, :, :]), f2(AR[:, hp, c, :, :]),
                                        True, True, [BK_b[hp], AR_b[hp]], [psb])
                            self.tt(K.DVE, f2(Gm[:, g4 * 4:(g4 + 1) * 4, :]), ps[:], self.msk[:], ALU.mult,
                                    [psb, self.msk_b], [Gm_b[g4]])
                            pt1, pt1b = ptr.next()
                            for i in range(4):
                                h, hp, par, c = pr(g4 * 4 + i)
                                self.tp(pt1[:, i * 128:(i + 1) * 128], f2(BKz[par][:, hp, c, :, :]), self.ident_bf[:],
                                        [BK_b[hp], self.identbf_b], [pt1b])
                            self.cp(K.ACT, f2(BKt[:, g4 * 4:(g4 + 1) * 4, :]), pt1[:], [pt1b], [BKt_b[g4]])
                        for gv in range(4):
                            pt2, pt2b = ptr.next()
                            for i in range(4):
                                hpl, cl = divmod(i, 2)
                                hp, c = gv * 2 + hpl, 2 * hf + cl
                                self.tp(pt2[:, i * 128:(i + 1) * 128], f2(VV[:, hp, c, :, :]), self.ident_bf[:],
                                        [VV_b[hp], self.identbf_b], [pt2b])
                            pv = pt2[64:128, :].rearrange("p (h c v) -> p h c v", h=2, c=2)
                            self.cp(K.ACT, UV5[64:128, gv * 2:gv * 2 + 2, 0, :, 0:64], pv[:, :, :, 0:64], [pt2b], [UVv_b[gv]])
                            self.cp(K.DVE, UV5[64:128, gv * 2:gv * 2 + 2, 1, :, 64:128], pv[:, :, :, 64:128], [pt2b],
                                    [UVv_b[gv]])
                        c0 = 2 * hf
                        for g in range(4):
                            ps2, ps2b = pa.next()
                            for i in range(4):
                                h = g * 4 + i
                                hp, par = h // 2, h % 2
                                bk0 = BKz[par][:, hp, c0:c0 + 2, 0, :]
                                a0 = AR[:, hp, c0:c0 + 2, 0, :]
                                az0 = Az[par][:, hp, c0:c0 + 2, :]
                                o2 = ps2[:, i * 128:(i + 1) * 128].rearrange("p (a t) -> p a t", a=2)
                                self.mm(o2, az0.rearrange("p a t -> p (a t)"), bk0, True, True, [BK_b[hp], AR_b[hp]], [ps2b])
                            self.tt(K.DVE, f2(Lnb[:, g * 4:(g + 1) * 4, :]), ps2[:], m3[:, 512:1024], ALU.mult,
                                    [ps2b, m3_b], [Lnb_b[g]])
                            ptl, ptlb = ptr.next()
                            for i in range(4):
                                self.tp(ptl[:, i * 128:(i + 1) * 128], Lnb[:, g * 4 + i, :], self.ident_bf[:],
                                        [Lnb_b[g], self.identbf_b], [ptlb])
                            self.cp(K.ACT, f2(Ltb[:, g * 4:(g + 1) * 4, :]), ptl[:], [ptlb], [Ltb_b[g]])
                        if getattr(cfg, 'rstop', 99) < 3:
                            continue
                        ist = {}
                        for g in range(4):
                            G = Gb[:, g * 4:(g + 1) * 4, :]
                            gb = Gb_b[g]
                            self.tt(K.DVE, f2(G), f2(Ltb[:, g * 4:(g + 1) * 4, :]), m3[:, 1024:1536], ALU.add,
                                    [Ltb_b[g], m3_b], [gb])
                            ist[g] = dict(G=G, G2=f2(G), gb=gb, lt_bufs=[Ltb_b[g]], ln_bufs=[Lnb_b[g]],
                                          lt_cur=[Ltb[:, g * 4 + i, :] for i in range(4)],
                                          ln_cur=[Lnb[:, g * 4 + i, :] for i in range(4)])
                        for lvl in range(5):
                            last = (lvl == 4)
                            for g in range(4):
                                z = ist[g]
                                G, G2, gb = z["G"], z["G2"], z["gb"]
                                lt_cur, ln_cur, lt_bufs, ln_bufs = z["lt_cur"], z["ln_cur"], z["lt_bufs"], z["ln_bufs"]
                                if not last:
                                    p_lt, p_ltb = pa.next()
                                    for i in range(4):
                                        self.mm(p_lt[:, i * 128:(i + 1) * 128], ln_cur[i], lt_cur[i], True, True,
                                                lt_bufs + ln_bufs, [p_ltb])
                                p_ln, p_lnb = pa.next()
                                for i in range(4):
                                    self.mm(p_ln[:, i * 128:(i + 1) * 128], lt_cur[i], ln_cur[i], True, True,
                                            lt_bufs + ln_bufs, [p_lnb])
                                ln_n, ln_nb = ln_r.next()
                                self.cp(K.ACT, f2(ln_n[:]), p_ln[:], [p_lnb], [ln_nb])
                                if not last:
                                    lt_n, lt_nb = lt_r.next()
                                    self.cp(K.DVE, f2(lt_n[:]), p_lt[:], [p_ltb], [lt_nb])
                                p_g, p_gb = pa.next()
                                for i in range(4):
                                    self.mm(p_g[:, i * 128:(i + 1) * 128], ln_n[:, i, :], G[:, i, :], True, True,
                                            [ln_nb, gb], [p_gb])
                                self.tt(K.DVE, G2, p_g[:], G2, ALU.add, [p_gb, gb], [gb])
                                z["ln_cur"] = [ln_n[:, i, :] for i in range(4)]
                                z["ln_bufs"] = [ln_nb]
                                if not last:
                                    z["lt_cur"] = [lt_n[:, i, :] for i in range(4)]
                                    z["lt_bufs"] = [lt_nb]
                        TT4 = TTa[:].rearrange("p (h c) t -> p h c t", c=2)
                        for g in range(4):
                            gb = Gb_b[g]
                            self.cp(K.ACT, TT4[:, g * 4:(g + 1) * 4, 0, :], Gb[0:64, g * 4:(g + 1) * 4, 0:C], [gb], [TT_b[g]])
                            ps, psb = pa.next()
                            for i in range(4):
                                self.mm(ps[0:64, i * C:(i + 1) * C], self.ident_bf[:, 64:128], Gb[:, g * 4 + i, C:2 * C],
                                        True, True, [gb, self.identbf_b], [psb])
                            self.cp(K.DVE, TT4[:, g * 4:(g + 1) * 4, 1, :],
                                    ps[0:64, 0:4 * C].rearrange("p (a t) -> p a t", t=C), [psb], [TT_b[g]])
                        if getattr(cfg, 'rstop', 99) < 4:
                            continue
                        for cl in range(2):
                            c = 2 * hf + cl
                            sst = {0: {}, 1: {}}

                            def stage1(hg, cl=cl, c=c):
                                hb_ = Hb_b[hg]
                                uvb = UVu_b[hg][cl]
                                xss = []
                                for bank in range(2):
                                    xp, xpb = pc.next()
                                    for jj in range(4):
                                        h = hg * 8 + bank * 4 + jj
                                        hp, par, q = h // 2, h % 2, h * 2 + cl
                                        self.mm(xp[0:64, jj * 128:(jj + 1) * 128], Az[par][:, hp, c, :], Hb[:, hp, :],
                                                True, False, [AR_b[hp], hb_], [xpb])
                                        self.mm(xp[0:64, jj * 128:(jj + 1) * 128], Gm[:, q, 0:C], UV[:, q, :],
                                                False, True, [Gm_b[q // 4], UVv_b[hp // 2], uvb], [xpb])
                                    xs, xsb = Xs.next()
                                    self.cp(K.ACT if bank == 0 else K.DVE, xs[:], xp[0:64, :], [xpb], [xsb])
                                    xss.append((xs, xsb))
                                sst[hg]["xss"] = xss

                            def stage2(hg, cl=cl, c=c):
                                uvb = UVu_b[hg][cl]
                                for bank in range(2):
                                    xs, xsb = sst[hg]["xss"][bank]
                                    up, upb = pc.next()
                                    for jj in range(4):
                                        h = hg * 8 + bank * 4 + jj
                                        q = h * 2 + cl
                                        self.mm(up[0:64, jj * 128:(jj + 1) * 128], TTa[:, q, :],
                                                xs[:, jj * 128:(jj + 1) * 128], True, True, [TT_b[q // 8], xsb], [upb])
                                    hp0 = hg * 4 + bank * 2
                                    self.cp(K.DVE if bank == 0 else K.ACT, UV5[0:64, hp0:hp0 + 2, :, cl, :],
                                            up[0:64, :].rearrange("p (h r v) -> p h r v", h=2, r=2), [upb], [uvb])

                            def stage3(hg, cl=cl, c=c):
                                hb_, hf_ = Hb_b[hg], Hf_b[hg]
                                uvb = UVu_b[hg][cl]
                                op_, opb = pa.next()
                                for hl in range(4):
                                    hp = hg * 4 + hl
                                    qa, qb_ = (2 * hp) * 2 + cl, (2 * hp + 1) * 2 + cl
                                    o_ap = op_[:, hl * C:(hl + 1) * C]
                                    self.mm(o_ap, Hb[:, hp, :], AR[:, hp, c, 1, :], True, False, [hb_, AR_b[hp]], [opb])
                                    self.mm(o_ap, UV[:, qa, :], Gm[:, qa, C:2 * C], False, False,
                                            [uvb, UVv_b[hp // 2], Gm_b[qa // 4]], [opb])
                                    self.mm(o_ap, UV[:, qb_, :], Gm[:, qb_, C:2 * C], False, True,
                                            [uvb, UVv_b[hp // 2], Gm_b[qb_ // 4]], [opb])
                                self.cp(K.ACT, osb[:, hg * 4:(hg + 1) * 4, c * C:(c + 1) * C],
                                        op_[:, 0:4 * C].rearrange("p (a t) -> p a t", t=C), [opb], [osb_b[hg][c]])
                                hp_, hpb = pa.next()
                                for hl in range(4):
                                    hp = hg * 4 + hl
                                    qa, qb_ = (2 * hp) * 2 + cl, (2 * hp + 1) * 2 + cl
                                    h_ap = hp_[:, hl * 128:(hl + 1) * 128]
                                    self.mm(h_ap, BKt[:, qa, :], UV[:, qa, :], True, False,
                                            [BKt_b[qa // 4], uvb, UVv_b[hp // 2]], [hpb])
                                    self.mm(h_ap, BKt[:, qb_, :], UV[:, qb_, :], False, True,
                                            [BKt_b[qb_ // 4], uvb, UVv_b[hp // 2]], [hpb])
                                hs = Hf[:, hg * 4:(hg + 1) * 4, :]
                                self.tt(K.DVE, hs, hp_[:].rearrange("p (a v) -> p a v", v=128), hs, ALU.add,
                                        [hpb, hf_], [hf_])
                                pcb = PC[:, hg * 4:(hg + 1) * 4, c:c + 1].to_broadcast([P, 4, 128])
                                self.tt(K.DVE, hs, hs, pcb, ALU.mult, [hf_] + PC_b[hg * 4:(hg + 1) * 4], [hf_])
                                self.cp(K.ACT, Hb[:, hg * 4:(hg + 1) * 4, :], hs, [hf_], [hb_])

                            for stage in (stage1, stage2, stage3):
                                for hg in range(2):
                                    stage(hg)
                    if getattr(cfg, 'rstop', 99) < 5:
                        continue
                    for n in range(8):
                        hg = n // 4
                        o_n = osb[:, n, :]
                        o_bufs = osb_b[hg]
                        ob_, ob_b = tbf.next()
                        self.cp(K.ACT, ob_[:], o_n, o_bufs, [ob_b])
                        ps, psb = pa.next()
                        self.mm(ps[:, :RT], self.bones_bf[:], ob_[:], True, True, [self.bones_b, ob_b], [psb])
                        d, db = tmp.next()
                        self.stt(d[:], ps[:, :RT], -1.0 / 64.0, o_n, ALU.mult, ALU.add, [psb] + o_bufs, [db])
                        d2, d2b = tbf.next()
                        self.act(d2[:], d[:], AF.Square, [db], [d2b])
                        ps2, ps2b = pa.next()
                        self.mm(ps2[:, :RT], self.bones_bf[:], d2[:], True, True, [self.bones_b, d2b], [ps2b])
                        sd, sdb = tmp.next()
                        self.act(sd[:], ps2[:, :RT], AF.Sqrt, [ps2b, eps2_b], [sdb], scale=1.0 / 64.0, bias=eps2[:])
                        self.recip(sd[:], sd[:], [sdb], [sdb])
                        self.tt(K.DVE, d[:], d[:], sd[:], ALU.mult, [db, sdb], [db])
                        self.ts(K.DVE, d[:], d[:], V['lnx_w'][:, n:n + 1], V['lnx_b'][:, n:n + 1], ALU.mult, ALU.add,
                                [db, Vb['lnx_w'], Vb['lnx_b']], [db])
                        self.tt(K.POOL, d[:], d[:], bonus[:, n, :], ALU.add, [db, bonus_b[n]], [db])
                        ps3, ps3b = pa.next()
                        self.mm(ps3[:, :RT], wg2[:, n * P:(n + 1) * P], sgt[:], True, True, [wg2_b, sgt_b], [ps3b])
                        self.tt(K.DVE, og[:, n, :], d[:], ps3[:, :RT], ALU.mult, [db, ps3b], [og_b[n]])
                    for m_ in range(16):
                        ps, psb = pa.next()
                        for n in range(8):
                            self.mm(ps[:, :RT], wob[:, n, m_ * P:(m_ + 1) * P], og[:, n, :], n == 0, n == 7,
                                    [wob_b, og_b[n]], [psb])
                        st, stb = yst.next()
                        self.cp(K.ACT if m_ % 2 == 0 else K.DVE, st[:], ps[:, :RT], [psb], [stb])
                        K.dma(K.ACT, self.scr["yb"][m_ * P:(m_ + 1) * P, t0:t0 + RT], st[:], reads=[stb],
                              writes=self.sbufs("yb", t0, RT))
        K.barrier()

    def dump(self, name, srcap, bufs):
        K = self.K
        b = Buf()
        K.dma(K.SP, self.dbg[name][:, :], srcap, reads=bufs, writes=[b])
        self.out_bufs.append(b)

    def build(self):
        K, cfg = self.K, self.cfg
        self.out_bufs = []
        ph = cfg.phases
        with contextlib.ExitStack() as es:
            self.setup_consts(es)
            names = []
            for p_ in ph:
                names += PHASE_W[p_]
            self.cast_weights(names)
            self.phase_in()
            cur = 0
            if "ffn1" in ph:
                self.phase_ffn("f1", cur, 1 - cur, 'n_ffn1_pre', 'n_ffn1_post',
                               'w_ffn1_gate', 'w_ffn1_up', 'w_ffn1_down')
                cur = 1 - cur
            if "proj" in ph:
                self.phase_proj(cur)
            if "mla" in ph:
                self.phase_mla()
            if "rwkv" in ph:
                self.phase_rwkv()
            if "merge" in ph:
                self.phase_merge(cur, 1 - cur, use_b=("rwkv" in ph))
                cur = 1 - cur
            if "xattn" in ph:
                self.phase_xattn(cur, 1 - cur)
                cur = 1 - cur
            if "ffn2" in ph:
                self.phase_ffn("f2", cur, 1 - cur, 'n_ffn2_pre', 'n_ffn2_post',
                               'w_ffn2_gate', 'w_ffn2_up', 'w_ffn2_down')
                cur = 1 - cur
            for nm, rows in cfg.debug_out:
                self.dump(nm, self.scr[nm][:, :], self.scr_b[nm])
            self.phase_out(cur)
            K.barrier()
        return self.nc


def _consts():
    ident = np.eye(P, dtype=np.float32)
    cst = np.zeros((P, 8), np.float32)
    half = 32
    inv = (10000.0 ** (-np.arange(half, dtype=np.float32) / half)).astype(np.float32)
    cst[0:32, 0] = inv / (2 * np.pi)
    cst[32:64, 0] = inv / (2 * np.pi)
    cst[0:32, 1] = -1.0
    cst[32:64, 1] = 1.0
    s_ = np.arange(64)[:, None]
    t_ = np.arange(64)[None, :]
    su = (s_ < t_).astype(np.float32)
    ui = (s_ <= t_).astype(np.float32)
    blk = np.block([[np.zeros_like(su), ui], [su, ui]])
    msk = np.tile(blk, (1, 4)).astype(np.float32)
    msk2 = np.zeros((P, 1536), np.float32)
    sl = (t_ < s_).astype(np.float32)
    msk2[0:64, 0:512] = np.tile(sl, (1, 8))
    msk2[0:64, 512:1024] = np.tile(np.eye(64, dtype=np.float32), (1, 8))
    rs = np.ones(256, np.float32)
    rs[0::64] = 0.0
    msk2[:, 1024:1280] = rs[None, :]
    msk2[0:64, 1280:1536] = np.tile(su, (1, 4))
    z = np.zeros_like(su)
    bdu = np.block([[su, z], [z, su]])
    bdl = np.block([[sl, z], [z, sl]])
    msk3 = np.concatenate([np.tile(bdu, (1, 4)), np.tile(bdl, (1, 4)), np.tile(np.eye(128, dtype=np.float32), (1, 4))],
                          axis=1).astype(np.float32)
    return ident, cst, msk, msk2, msk3


_IDENT, _CST, _MSK, _MSK2, _MSK3 = _consts()


def make_in_map(cfg, inputs, core):
    NB, S = cfg.NB, cfg.S
    b0 = core * NB
    m = {
        "x": np.ascontiguousarray(np.asarray(inputs["x"])[b0:b0 + NB].reshape(NB * S, D)),
        "mem": np.ascontiguousarray(np.asarray(inputs["mem"])[b0:b0 + NB].reshape(NB * N_MEM, D)),
        "pos": np.ascontiguousarray(np.asarray(inputs["positions"])[b0:b0 + NB].reshape(NB * S)).astype(np.int32),
        "ident": _IDENT, "cst": _CST, "msk": _MSK, "msk2": _MSK2, "msk3": _MSK3,
    }
    for nm in W_SHAPES:
        if getattr(cfg, 'lite', False) and not any(nm in PHASE_W[p_] for p_ in cfg.phases):
            continue
        m[nm] = np.ascontiguousarray(np.asarray(inputs[nm])[0])
    for nm in V_SHAPES:
        m[nm] = np.ascontiguousarray(np.asarray(inputs[nm])[0].reshape(-1))
    return m


def kernel(**inputs):
    cfg = Cfg()
    prog = Prog(cfg)
    nc = prog.build()
    n_cores = 8
    in_maps = [make_in_map(cfg, inputs, c) for c in range(n_cores)]
    res = run_bass_kernel_spmd(nc, in_maps, core_ids=list(range(n_cores)))
    outs = [np.asarray(res.results[c]["out"]).reshape(cfg.NB, cfg.S, D) for c in range(n_cores)]
    return np.concatenate(outs, axis=0).astype(np.float32)
```

```python
import contextlib
import numpy as np
import concourse.bass as bass
import concourse.mybir as mybir
from concourse.bass_utils import run_bass_kernel_spmd

F32 = mybir.dt.float32
BF16 = mybir.dt.bfloat16
I32 = mybir.dt.int32
AF = mybir.ActivationFunctionType
ALU = mybir.AluOpType

D = 2048
D_FF = 5504
N_MEM = 256
EPS = 1e-6
P = 128
TT = 512


class Buf:
    __slots__ = ("name", "w", "r")

    def __init__(self, name=""):
        self.name = name
        self.w = None
        self.r = {}


class Eng:
    def __init__(self, K, name, h, sem, is_pe=False):
        self.K = K
        self.name = name
        self.h = h
        self.sem = sem
        self.sid = id(sem)
        self.n = 0
        self.seen = {}
        self.is_pe = is_pe
        K.sems[self.sid] = sem

    def wait_for(self, events):
        for sid, val in events:
            if sid == self.sid and self.is_pe:
                continue
            if self.seen.get(sid, 0) >= val:
                continue
            self.h.wait_ge(self.K.sems[sid], val)
            self.seen[sid] = val


class Kern:
    def __init__(self, nc, n_dma_sems=40):
        self.nc = nc
        self.sems = {}
        self.es = contextlib.ExitStack()
        mk = lambda nm: self.es.enter_context(nc.semaphore(nm))
        self.PE = Eng(self, "pe", nc.tensor, mk("s_pe"), is_pe=True)
        self.ACT = Eng(self, "act", nc.scalar, mk("s_act"))
        self.DVE = Eng(self, "dve", nc.vector, mk("s_dve"))
        self.POOL = Eng(self, "pool", nc.gpsimd, mk("s_pool"))
        self.SP = Eng(self, "sp", nc.sync, mk("s_sp"))
        self.engs = [self.PE, self.ACT, self.DVE, self.POOL, self.SP]
        self.dsems = []
        self.qpool = {}
        for e, cnt in ((self.SP, 28), (self.POOL, 12), (self.ACT, 8)):
            pool = []
            for i in range(cnt):
                s = mk("s_dma_%s%d" % (e.name, i))
                self.sems[id(s)] = s
                slot = [s, 0]
                pool.append(slot)
                self.dsems.append(slot)
            self.qpool[e.sid] = [pool, 0]

    @staticmethod
    def _deps(reads, writes):
        ev = set()
        for b in reads:
            if b.w is not None:
                ev.add(b.w)
        for b in writes:
            if b.w is not None:
                ev.add(b.w)
            ev.update(b.r.items())
        return ev

    def op(self, eng, reads, writes, fn):
        eng.wait_for(self._deps(reads, writes))
        ins = fn(eng.h)
        eng.n += 1
        ins.then_inc(eng.sem, 1)
        me = (eng.sid, eng.n)
        for b in reads:
            b.r[eng.sid] = eng.n
        for b in writes:
            b.w = me
            b.r = {}
        return ins

    def dma(self, q, out, in_, reads, writes, **kw):
        qp = self.qpool[q.sid]
        slot = qp[0][qp[1]]
        qp[1] = (qp[1] + 1) % len(qp[0])
        sem, prev = slot
        ev = self._deps(reads, writes)
        if prev > 0:
            ev.add((id(sem), prev))
        q.wait_for(ev)
        q.h.dma_start(out=out, in_=in_, **kw).then_inc(sem, 16)
        slot[1] = prev + 16
        me = (id(sem), prev + 16)
        for b in reads:
            b.r[me[0]] = me[1]
        for b in writes:
            b.w = me
            b.r = {}

    def barrier(self):
        ev = [(e.sid, e.n) for e in self.engs if e.n > 0]
        ev += [(id(s), v) for s, v in self.dsems if v > 0]
        for e in self.engs:
            e.wait_for([x for x in ev if x[0] != e.sid])

    def close(self):
        self.es.close()


class Cfg:
    def __init__(self, S=2048, NB=2, phases=("ffn1", "proj", "mla", "rwkv", "merge", "xattn", "ffn2"),
                 debug_out=()):
        self.S = S
        self.NB = NB
        self.NT = S * NB
        self.phases = phases
        self.debug_out = debug_out


Q_LORA, KV_LORA, QK_ROPE, QK_NOPE, V_HEAD, MLA_H = 512, 256, 64, 128, 128, 8
RW, RW_H, RW_N = 1024, 16, 64
RW_COLS = 3328
MLA_COLS = 832
GATE0 = MLA_COLS + RW_COLS
IN_COLS = 8256
MEM_H, MEM_D = 4, 256

W_SHAPES = {
    'w_ffn1_gate': (D, D_FF), 'w_ffn1_up': (D, D_FF), 'w_ffn1_down': (D_FF, D),
    'w_in': (D, IN_COLS), 'w_uq': (Q_LORA, 1536), 'w_ukv': (KV_LORA, 2048), 'w_oa': (1024, D),
    'w_w2': (64, RW), 'w_a2': (64, RW), 'w_g2': (128, RW), 'w_ob': (RW, D), 'w_o': (D, D),
    'w_cq': (D, 1024), 'w_ckv': (D, 2048), 'w_co': (1024, D),
    'w_ffn2_gate': (D, D_FF), 'w_ffn2_up': (D, D_FF), 'w_ffn2_down': (D_FF, D),
}
V_SHAPES = {
    'n_ffn1_pre': D, 'n_ffn1_post': D, 'n_mix_pre': D, 'n_mix_post': D, 'b_gate': 4096,
    'n_q_lat': Q_LORA, 'n_kv_lat': KV_LORA, 'mu_shift': RW_COLS, 'w0': RW, 'a0': RW, 'k_k': RW, 'k_a': RW,
    'r_k': RW, 'lnx_w': RW, 'lnx_b': RW, 'n_x_pre': D, 'n_x_post': D, 'n_mem': D,
    'n_ffn2_pre': D, 'n_ffn2_post': D,
}
PHASE_W = {
    "ffn1": ['w_ffn1_gate', 'w_ffn1_up', 'w_ffn1_down'],
    "proj": ['w_in'], "mla": ['w_uq', 'w_ukv', 'w_oa'],
    "rwkv": ['w_w2', 'w_a2', 'w_g2', 'w_ob'], "merge": ['w_in', 'w_o'],
    "xattn": ['w_cq', 'w_ckv', 'w_co'],
    "ffn2": ['w_ffn2_gate', 'w_ffn2_up', 'w_ffn2_down'],
}


def _largest_div(n, cap=2048):
    for c in range(cap, 0, -1):
        if n % c == 0:
            return c


class Ring:
    def __init__(self, tiles):
        self.t = tiles
        self.b = [Buf() for _ in tiles]
        self.i = 0

    def next(self):
        k = self.i % len(self.t)
        self.i += 1
        return self.t[k], self.b[k]


class Prog:
    def __init__(self, cfg):
        self.cfg = cfg
        nc = self.nc = bass.Bass("TRN2", target_bir_lowering=False)
        self.K = Kern(nc)
        NT = cfg.NT
        dt = nc.dram_tensor
        self.x_in = dt("x", [NT, D], F32, kind="ExternalInput").ap()
        self.mem_in = dt("mem", [cfg.NB * N_MEM, D], F32, kind="ExternalInput").ap()
        self.pos_in = dt("pos", [NT], I32, kind="ExternalInput").ap()
        self.ident_in = dt("ident", [P, P], F32, kind="ExternalInput").ap()
        self.cst_in = dt("cst", [P, 8], F32, kind="ExternalInput").ap()
        self.msk_in = dt("msk", [P, 512], F32, kind="ExternalInput").ap()
        self.msk2_in = dt("msk2", [P, 1536], F32, kind="ExternalInput").ap()
        self.msk3_in = dt("msk3", [P, 1536], F32, kind="ExternalInput").ap()
        self.out = dt("out", [NT, D], F32, kind="ExternalOutput").ap()
        self.w, self.wb, self.wbuf = {}, {}, {}
        self.wnames = [nm for nm in W_SHAPES if (not getattr(cfg, 'lite', False)) or any(nm in PHASE_W[p_] for p_ in cfg.phases)]
        for nm in W_SHAPES:
            k, n = W_SHAPES[nm]
            if nm in self.wnames:
                self.w[nm] = dt(nm, [k, n], F32, kind="ExternalInput").ap()
            self.wb[nm] = dt(nm + "_b", [k, n], BF16, kind="Internal").ap()
            self.wbuf[nm] = [Buf(nm) for _ in range(4)]
        self.v = {}
        for nm, n in V_SHAPES.items():
            self.v[nm] = dt(nm, [n], F32, kind="ExternalInput").ap()
        self.xT = [dt("xT%d" % i, [D, NT], F32, kind="Internal").ap() for i in range(2)]
        self.xT_buf = [[Buf() for t in range(NT // TT)] for i in range(2)]
        self.scr, self.scr_b = {}, {}
        for nm, rows in (("cq", Q_LORA), ("ckv", KV_LORA), ("kr", 64), ("cs", 64), ("sn", 64),
                         ("rw", RW_COLS), ("ya", D), ("yb", D)):
            kind = "ExternalInput" if (nm == "rw" and getattr(cfg, 'rw_input', False)) else "Internal"
            self.scr[nm] = dt("scr_" + nm, [rows, NT], F32, kind=kind).ap()
            self.scr_b[nm] = [Buf() for t in range(NT // 256)]
        self.dbg = {}
        for nm, rows in cfg.debug_out:
            self.dbg[nm] = dt("dbg_" + nm, [rows, NT], F32, kind="ExternalOutput").ap()

    def sbufs(self, nm, t0, n):
        return self.scr_b[nm][t0 // 256:(t0 + n + 255) // 256]

    def sb(self, es, name, shape, dtype):
        return es.enter_context(self.nc.sbuf_tensor("sb_" + name, shape, dtype))

    def ps(self, es, name, shape, dtype):
        return es.enter_context(self.nc.psum_tensor("ps_" + name, shape, dtype))

    def sring(self, es, name, n, shape, dtype):
        return Ring([self.sb(es, "%s%d" % (name, i), shape, dtype) for i in range(n)])

    def pring(self, es, name, n, shape=None, dtype=F32):
        return Ring([self.ps(es, "%s%d" % (name, i), shape or [P, 512], dtype) for i in range(n)])

    def mm(self, out, lhsT, rhs, start, stop, reads, writes):
        return self.K.op(self.K.PE, reads, writes,
                         lambda e: e.matmul(out, lhsT=lhsT, rhs=rhs, start=start, stop=stop))

    def tp(self, out, in_, ident, reads, writes):
        return self.K.op(self.K.PE, reads, writes, lambda e: e.transpose(out=out, in_=in_, identity=ident))

    def act(self, out, in_, func, reads, writes, scale=1.0, bias=None):
        if bias is None:
            return self.K.op(self.K.ACT, reads, writes,
                             lambda e: e.activation(out=out, in_=in_, func=func, scale=scale))
        return self.K.op(self.K.ACT, reads, writes,
                         lambda e: e.activation(out=out, in_=in_, func=func, scale=scale, bias=bias))

    def cp(self, eng, out, in_, reads, writes):
        if eng is self.K.ACT:
            return self.K.op(eng, reads, writes, lambda e: e.copy(out=out, in_=in_))
        return self.K.op(eng, reads, writes, lambda e: e.tensor_copy(out=out, in_=in_))

    def tt(self, eng, out, in0, in1, op, reads, writes):
        return self.K.op(eng, reads, writes, lambda e: e.tensor_tensor(out=out, in0=in0, in1=in1, op=op))

    def ts(self, eng, out, in0, s1, s2, op0, op1, reads, writes):
        if op1 is None:
            return self.K.op(eng, reads, writes,
                             lambda e: e.tensor_scalar(out=out, in0=in0, scalar1=s1, scalar2=None, op0=op0))
        return self.K.op(eng, reads, writes,
                         lambda e: e.tensor_scalar(out=out, in0=in0, scalar1=s1, scalar2=s2, op0=op0, op1=op1))

    def stt(self, out, in0, scalar, in1, op0, op1, reads, writes):
        return self.K.op(self.K.DVE, reads, writes, lambda e: e.scalar_tensor_tensor(
            out=out, in0=in0, scalar=scalar, in1=in1, op0=op0, op1=op1))

    def mset(self, eng, ap, val, reads, writes):
        return self.K.op(eng, reads, writes, lambda e: e.memset(ap, val))

    def recip(self, out, in_, reads, writes):
        return self.K.op(self.K.DVE, reads, writes, lambda e: e.reciprocal(out=out, in_=in_))

    def cast_weights(self, names):
        K = self.K
        done = set()
        for nm in names:
            if nm in done:
                continue
            done.add(nm)
            k, n = W_SHAPES[nm]
            c = _largest_div(n)
            src = self.w[nm].rearrange("k (a c) -> (k a) c", c=c)
            dst = self.wb[nm].rearrange("k (a c) -> (k a) c", c=c)
            rows = k * (n // c)
            nblk = 4 if rows >= 2048 else 1
            rb = rows // nblk
            for i in range(nblk):
                K.dma(K.POOL, dst[i * rb:(i + 1) * rb, :], src[i * rb:(i + 1) * rb, :],
                      reads=[], writes=[self.wbuf[nm][i]])

    def setup_consts(self, es):
        K, nc = self.K, self.nc
        self.ident = self.sb(es, "ident", [P, P], F32)
        self.ident_b = Buf("ident")
        K.dma(K.SP, self.ident[:], self.ident_in[:, :], reads=[], writes=[self.ident_b])
        self.ident_bf = self.sb(es, "ident_bf", [P, P], BF16)
        self.identbf_b = Buf()
        self.cp(K.DVE, self.ident_bf[:], self.ident[:], [self.ident_b], [self.identbf_b])
        self.cst = self.sb(es, "cst", [P, 8], F32)
        self.cst_b = Buf()
        K.dma(K.SP, self.cst[:], self.cst_in[:, :], reads=[], writes=[self.cst_b])
        self.msk = self.sb(es, "msk", [P, 512], F32)
        self.msk_b = Buf()
        K.dma(K.SP, self.msk[:], self.msk_in[:, :], reads=[], writes=[self.msk_b])
        self.ones_f32 = self.sb(es, "ones_f32", [P, P], F32)
        self.onesf_b = Buf()
        self.mset(K.DVE, self.ones_f32[:], 1.0, [], [self.onesf_b])
        self.ones_bf = self.sb(es, "ones_bf", [P, P], BF16)
        self.ones_b = Buf("ones")
        self.mset(K.DVE, self.ones_bf[:], 1.0, [], [self.ones_b])
        self.bones_bf = self.sb(es, "bones_bf", [P, P], BF16)
        self.bones_b = Buf()
        self.mset(K.DVE, self.bones_bf[:], 0.0, [], [self.bones_b])
        self.mset(K.DVE, self.bones_bf[0:64, 0:64], 1.0, [], [self.bones_b])
        self.mset(K.DVE, self.bones_bf[64:128, 64:128], 1.0, [], [self.bones_b])
        self.eps_t = self.sb(es, "eps_t", [P, 1], F32)
        self.eps_b = Buf("eps")
        self.mset(K.DVE, self.eps_t[:], EPS, [], [self.eps_b])
        self.vec, self.vec_b = {}, {}
        for nm, n in V_SHAPES.items():
            t = self.sb(es, "v_" + nm, [P, n // P], F32)
            b = Buf(nm)
            with nc.allow_non_contiguous_dma(reason="tiny per-feature vector load"):
                K.dma(K.SP, t[:], self.v[nm].rearrange("(c p) -> p c", p=P), reads=[], writes=[b])
            self.vec[nm] = t
            self.vec_b[nm] = b

    def phase_in(self):
        K, nc, cfg = self.K, self.nc, self.cfg
        with contextlib.ExitStack() as es:
            NBUF = 2
            xin = [self.sb(es, "pin_x%d" % i, [P, D], F32) for i in range(NBUF)]
            xin_b = [Buf() for _ in range(NBUF)]
            xo = [self.sb(es, "pin_o%d" % i, [P, 16, TT], F32) for i in range(2)]
            xo_b = [Buf() for _ in range(2)]
            pst = [self.ps(es, "pin_ps%d" % i, [P, 512], F32) for i in range(4)]
            pst_b = [Buf() for _ in range(4)]
            nsub = TT // P
            blk = 0
            pidx = 0
            for t in range(cfg.NT // TT):
                o, ob = xo[t % 2], xo_b[t % 2]
                for s in range(nsub):
                    xi, xib = xin[blk % NBUF], xin_b[blk % NBUF]
                    r0 = t * TT + s * P
                    K.dma(K.SP, xi[:], self.x_in[r0:r0 + P, :], reads=[], writes=[xib])
                    for g in range(4):
                        pt, ptb = pst[pidx % 4], pst_b[pidx % 4]
                        pidx += 1
                        for j in range(4):
                            c = g * 4 + j
                            K.op(K.PE, [xib, self.ident_b], [ptb],
                                 lambda e, c=c, j=j, pt=pt, xi=xi: e.transpose(
                                     out=pt[:, j * P:(j + 1) * P], in_=xi[:, c * P:(c + 1) * P],
                                     identity=self.ident[:]))
                        eng = K.ACT if (g % 2 == 0) else K.DVE
                        if eng is K.ACT:
                            K.op(eng, [ptb], [ob], lambda e, g=g, pt=pt, o=o, s=s: e.copy(
                                out=o[:, g * 4:(g + 1) * 4, s * P:(s + 1) * P],
                                in_=pt[:].rearrange("p (j q) -> p j q", j=4)))
                        else:
                            K.op(eng, [ptb], [ob], lambda e, g=g, pt=pt, o=o, s=s: e.tensor_copy(
                                out=o[:, g * 4:(g + 1) * 4, s * P:(s + 1) * P],
                                in_=pt[:].rearrange("p (j q) -> p j q", j=4)))
                    blk += 1
                K.dma(K.SP, self.xT[0].rearrange("(c p) t -> p c t", p=P)[:, :, t * TT:(t + 1) * TT],
                      o[:], reads=[ob], writes=[self.xT_buf[0][t]])
        K.barrier()

    def phase_out(self, src):
        K, nc, cfg = self.K, self.nc, self.cfg
        with contextlib.ExitStack() as es:
            xi_t = [self.sb(es, "pout_x%d" % i, [P, 16, TT], F32) for i in range(2)]
            xi_b = [Buf() for _ in range(2)]
            xo = [self.sb(es, "pout_o%d" % i, [P, D], F32) for i in range(2)]
            xo_b = [Buf() for _ in range(2)]
            pst = [self.ps(es, "pout_ps%d" % i, [P, 512], F32) for i in range(4)]
            pst_b = [Buf() for _ in range(4)]
            self.out_bufs = []
            nsub = TT // P
            blk = 0
            pidx = 0
            for t in range(cfg.NT // TT):
                xi, xib = xi_t[t % 2], xi_b[t % 2]
                K.dma(K.SP, xi[:], self.xT[src].rearrange("(c p) t -> p c t", p=P)[:, :, t * TT:(t + 1) * TT],
                      reads=[self.xT_buf[src][t]], writes=[xib])
                for s in range(nsub):
                    o, ob = xo[blk % 2], xo_b[blk % 2]
                    for g in range(4):
                        pt, ptb = pst[pidx % 4], pst_b[pidx % 4]
                        pidx += 1
                        for j in range(4):
                            c = g * 4 + j
                            K.op(K.PE, [xib, self.ident_b], [ptb],
                                 lambda e, c=c, j=j, pt=pt, xi=xi, s=s: e.transpose(
                                     out=pt[:, j * P:(j + 1) * P], in_=xi[:, c, s * P:(s + 1) * P],
                                     identity=self.ident[:]))
                        if g % 2 == 0:
                            K.op(K.ACT, [ptb], [ob], lambda e, g=g, pt=pt, o=o: e.copy(
                                out=o[:, g * 512:(g + 1) * 512], in_=pt[:]))
                        else:
                            K.op(K.DVE, [ptb], [ob], lambda e, g=g, pt=pt, o=o: e.tensor_copy(
                                out=o[:, g * 512:(g + 1) * 512], in_=pt[:]))
                    r0 = t * TT + s * P
                    fin = Buf("out")
                    K.dma(K.SP, self.out[r0:r0 + P, :], o[:], reads=[ob], writes=[fin])
                    self.out_bufs.append(fin)
                    blk += 1
        K.barrier()

    def rms_rstd(self, src, src_b, nchunks, width, dfeat, R, rows=P):
        K = self.K
        pstat, pstat_b = R["pstat"]
        for c in range(nchunks):
            q, qb = R["sq"].next()
            self.act(q[:, :width], src(c), AF.Square, [src_b(c)], [qb])
            self.mm(pstat[:, :width], self.ones_bf[:], q[:, :width], c == 0, c == nchunks - 1,
                    [qb, self.ones_b], [pstat_b])
        tmp, tmp_b = R["tmp"]
        rstd, rstd_b = R["rstd"]
        self.act(tmp[:, :width], pstat[:, :width], AF.Sqrt, [pstat_b, self.eps_b], [tmp_b],
                 scale=1.0 / dfeat, bias=self.eps_t[:])
        self.recip(rstd[:, :width], tmp[:, :width], [tmp_b], [rstd_b])

    def norm_res(self, es, tag, width=TT):
        R = {}
        R["pstat"] = (self.ps(es, tag + "pstat", [P, 512], F32), Buf())
        R["sq"] = self.sring(es, tag + "sq", 2, [P, width], BF16)
        R["tmp"] = (self.sb(es, tag + "tmp", [P, width], F32), Buf())
        R["rstd"] = (self.sb(es, tag + "rstd", [P, width], F32), Buf())
        return R

    def wslab(self, ring, wname, r0, nkc, c0, w, q=None):
        K = self.K
        t, b = ring.next()
        view = self.wb[wname][r0:r0 + nkc * P, :].rearrange("(c p) n -> p c n", p=P)
        for ca in range(0, nkc, 16):
            cb = min(nkc, ca + 16)
            K.dma(q or K.SP, t[:, ca:cb, :w], view[:, ca:cb, c0:c0 + w], reads=self.wbuf[wname], writes=[b])
        return t, b

    def tile_phase(self, tag, src, dst, n_pre, n_post, half, setup, body, post=True):
        K, nc, cfg = self.K, self.nc, self.cfg
        KC = D // P
        with contextlib.ExitStack() as es:
            ctx = setup(es)
            xt = self.sb(es, tag + "xt", [P, KC, TT], F32)
            xt_b = [Buf() for _ in range(KC)]
            hTs = [self.sb(es, tag + "hT%d" % i, [P, KC, TT], BF16) for i in range(2)]
            hT_bs = [Buf(), Buf()]
            pstat, pstat_b = self.ps(es, tag + "pstat", [P, 512], F32), Buf()
            sq = self.sring(es, tag + "sq", 2, [P, TT], F32)
            xs = self.sring(es, tag + "xs", 3, [P, TT], F32)
            acc = {k: (self.sb(es, tag + "acc" + k, [P, TT], F32), Buf()) for k in "PE"}
            tmp = {k: (self.sb(es, tag + "tmp" + k, [P, TT], F32), Buf()) for k in "PE"}
            rst = {k: (self.sb(es, tag + "rst" + k, [P, TT], F32), Buf()) for k in "PE"}
            gph = self.sb(es, tag + "gph", [P, KC], F32)
            gph_b = Buf()
            gpre, gpre_b = self.vec[n_pre], self.vec_b[n_pre]
            if post:
                gpost, gpost_b = self.vec[n_post], self.vec_b[n_post]
                self.ts(K.DVE, gph[:], gpost[:], 0.5 if half else 1.0, None, ALU.mult, None, [gpost_b], [gph_b])
                dstT = self.xT[dst].rearrange("(c p) t -> p c t", p=P)
            srcT = self.xT[src].rearrange("(c p) t -> p c t", p=P)
            ntiles = cfg.NT // TT

            def sumsq(k, chunk_ap, chunk_b):
                a, ab = acc[k]
                for c in range(KC):
                    ap_, b_ = chunk_ap(c), chunk_b(c)
                    q, qb = sq.next()
                    self.act(q[:], ap_, AF.Square, [b_], [qb])
                    if c == 0:
                        self.cp(K.POOL, a[:], q[:], [qb], [ab])
                    else:
                        self.tt(K.POOL, a[:], a[:], q[:], ALU.add, [ab, qb], [ab])

            def rstd_of(k):
                a, ab = acc[k]
                tm, tmb = tmp[k]
                rs, rsb = rst[k]
                self.mm(pstat[:], self.ones_f32[:], a[:], True, True, [ab, self.onesf_b], [pstat_b])
                self.act(tm[:], pstat[:], AF.Sqrt, [pstat_b, self.eps_b], [tmb], scale=1.0 / D, bias=self.eps_t[:])
                self.recip(rs[:], tm[:], [tmb], [rsb])
                return rs, rsb

            def load_chunk(t, c):
                r, rb = xs.next()
                K.dma(K.SP, r[:], srcT[:, c, t * TT:(t + 1) * TT], reads=[self.xT_buf[src][t]], writes=[rb])
                return r, rb

            def P_a(t):
                loaded = {}

                def ap_(c):
                    loaded[c] = load_chunk(t, c)
                    return loaded[c][0][:]
                sumsq("P", ap_, lambda c: loaded[c][1])

            def P_b(t):
                rs, rsb = rstd_of("P")
                hT, hT_b = hTs[t % 2], hT_bs[t % 2]
                for c in range(KC):
                    r, rb = load_chunk(t, c)
                    self.stt(hT[:, c, :], r[:], gpre[:, c:c + 1], rs[:], ALU.mult, ALU.mult,
                             [rb, gpre_b, rsb], [hT_b])

            def E_a(t):
                sumsq("E", lambda c: xt[:, c, :], lambda c: xt_b[c])

            def E_b(t):
                rs, rsb = rstd_of("E")
                for c in range(KC):
                    r, rb = load_chunk(t, c)
                    self.stt(xt[:, c, :], xt[:, c, :], gph[:, c:c + 1], rs[:], ALU.mult, ALU.mult,
                             [xt_b[c], gph_b, rsb], [xt_b[c]])
                    self.tt(K.DVE, xt[:, c, :], xt[:, c, :], r[:], ALU.add, [xt_b[c], rb], [xt_b[c]])
                K.dma(K.ACT, dstT[:, :, t * TT:(t + 1) * TT], xt[:], reads=xt_b, writes=[self.xT_buf[dst][t]])

            def emit(n, y_ps, y_b):
                self.cp(K.DVE, xt[:, n, :], y_ps, [y_b], [xt_b[n]])

            P_a(0)
            P_b(0)
            for t in range(ntiles):
                if t + 1 < ntiles:
                    P_a(t + 1)
                called = []

                def early(t=t):
                    called.append("e")
                    if post and t > 0:
                        E_b(t - 1)

                def mid(t=t):
                    called.append("m")
                    if t + 1 < ntiles:
                        P_b(t + 1)

                body(ctx, t, t * TT, hTs[t % 2], hT_bs[t % 2], emit, early, mid)
                assert called == ["e", "m"], called
                if post:
                    E_a(t)
            if post:
                E_b(ntiles - 1)
        K.barrier()

    def phase_ffn(self, tag, src, dst, n_pre, n_post, wg, wu, wd):
        K = self.K
        KC, FC, SW = D // P, D_FF // P, 256

        def setup(es):
            c = {}
            c["it"] = (self.sb(es, tag + "it", [P, FC, TT], BF16), Buf())
            c["wg"] = self.sring(es, tag + "wg", 2, [P, KC, SW], BF16)
            c["wu"] = self.sring(es, tag + "wu", 2, [P, KC, SW], BF16)
            c["wd"] = self.sring(es, tag + "wd", 3, [P, FC, P], BF16)
            c["sg"] = self.sring(es, tag + "sg", 2, [P, TT], BF16)
            c["pg"] = self.pring(es, tag + "pg", 2)
            c["pu"] = self.pring(es, tag + "pu", 2)
            c["py"] = self.pring(es, tag + "py", 2)
            return c

        def body(c, t, t0, hT, hT_b, emit, early, mid):
            it, it_b = c["it"]
            for s in range((D_FF + SW - 1) // SW):
                c0 = s * SW
                w = min(SW, D_FF - c0)
                a, ab = self.wslab(c["wg"], wg, 0, KC, c0, w)
                u, ub = self.wslab(c["wu"], wu, 0, KC, c0, w)
                for j in range(w // P):
                    n = c0 // P + j
                    g_ps, g_b = c["pg"].next()
                    u_ps, u_b = c["pu"].next()
                    sgt, sgb = c["sg"].next()
                    for kc in range(KC):
                        self.mm(g_ps[:], a[:, kc, j * P:(j + 1) * P], hT[:, kc, :], kc == 0, kc == KC - 1,
                                [ab, hT_b], [g_b])
                    for kc in range(KC):
                        self.mm(u_ps[:], u[:, kc, j * P:(j + 1) * P], hT[:, kc, :], kc == 0, kc == KC - 1,
                                [ub, hT_b], [u_b])
                    self.act(sgt[:], g_ps[:], AF.Silu, [g_b], [sgb])
                    self.tt(K.DVE, it[:, n, :], sgt[:], u_ps[:], ALU.mult, [sgb, u_b], [it_b])
                if s == 1:
                    early()
            mid()
            for n in range(D // P):
                dw, dwb = self.wslab(c["wd"], wd, 0, FC, n * P, P)
                y_ps, y_b = c["py"].next()
                for kc in range(FC):
                    self.mm(y_ps[:], dw[:, kc, :], it[:, kc, :], kc == 0, kc == FC - 1, [dwb, it_b], [y_b])
                emit(n, y_ps[:], y_b)

        self.tile_phase(tag, src, dst, n_pre, n_post, True, setup, body)

    def rope_tables(self, es, t0, cs, cs_b, sn, sn_b, W):
        K = self.K
        pos_i, pos_f, tq, ti, tf, m = (W[k] for k in ("pos_i", "pos_f", "tq", "ti", "tf", "m"))
        wb = W["b"]
        K.dma(K.SP, pos_i[:], self.pos_in[t0:t0 + TT].partition_broadcast(64), reads=[], writes=[wb])
        self.cp(K.DVE, pos_f[:], pos_i[:], [wb], [wb])
        for which, out, out_b in ((0, sn, sn_b), (1, cs, cs_b)):
            self.ts(K.DVE, tq[:], pos_f[:], self.cst[0:64, 0:1], 0.25 * which, ALU.mult, ALU.add,
                    [wb, self.cst_b], [wb])
            self.cp(K.DVE, ti[:], tq[:], [wb], [wb])
            self.cp(K.DVE, tf[:], ti[:], [wb], [wb])
            self.tt(K.DVE, tq[:], tq[:], tf[:], ALU.subtract, [wb], [wb])
            self.ts(K.DVE, m[:], tq[:], 0.5, None, ALU.is_gt, None, [wb], [wb])
            self.tt(K.DVE, tq[:], tq[:], m[:], ALU.subtract, [wb], [wb])
            self.ts(K.DVE, m[:], tq[:], -0.5, None, ALU.is_lt, None, [wb], [wb])
            self.tt(K.DVE, tq[:], tq[:], m[:], ALU.add, [wb], [wb])
            self.act(out[:], tq[:], AF.Sin, [wb], [out_b], scale=2.0 * np.pi * (1.0 - 1e-6))
        self.ts(K.DVE, sn[:], sn[:], self.cst[0:64, 1:2], None, ALU.mult, None, [sn_b, self.cst_b], [sn_b])

    def rope_work(self, es, tag):
        W = {"b": Buf()}
        W["pos_i"] = self.sb(es, tag + "pos_i", [64, TT], I32)
        W["ti"] = self.sb(es, tag + "ti", [64, TT], I32)
        for k in ("pos_f", "tq", "tf", "m"):
            W[k] = self.sb(es, tag + k, [64, TT], F32)
        return W

    def phase_proj(self, src):
        K = self.K
        KC, SW = D // P, 256
        tag = "pj"
        segs = [("cq", 0, Q_LORA), ("ckv", Q_LORA, KV_LORA), ("rw", MLA_COLS, RW_COLS)]

        def setup(es):
            c = {}
            c["w"] = self.sring(es, tag + "w", 3, [P, KC, SW], BF16)
            c["wr"] = (self.sb(es, tag + "wr", [P, KC, 64], BF16), Buf())
            c["wrs"] = (self.sb(es, tag + "wrs", [P, KC, 64], BF16), Buf())
            c["pp"] = self.pring(es, tag + "pp", 4)
            c["st"] = self.sring(es, tag + "st", 4, [P, TT], F32)
            c["cs"] = (self.sb(es, tag + "cs", [64, TT], F32), Buf())
            c["sn"] = (self.sb(es, tag + "sn", [64, TT], F32), Buf())
            c["rt"] = self.sring(es, tag + "rt", 2, [64, TT], F32)
            c["W"] = self.rope_work(es, tag)
            wr, wrb = c["wr"]
            wrs, wrsb = c["wrs"]
            view = self.wb['w_in'].rearrange("(c p) n -> p c n", p=P)
            kr0 = Q_LORA + KV_LORA
            with self.nc.allow_non_contiguous_dma(reason="64-col rope weight slab"):
                K.dma(K.SP, wr[:], view[:, :, kr0:kr0 + 64], reads=self.wbuf['w_in'], writes=[wrb])
                K.dma(K.SP, wrs[:, :, 0:32], view[:, :, kr0 + 32:kr0 + 64], reads=self.wbuf['w_in'], writes=[wrsb])
                K.dma(K.SP, wrs[:, :, 32:64], view[:, :, kr0:kr0 + 32], reads=self.wbuf['w_in'], writes=[wrsb])
            return c

        def body(c, t, t0, hT, hT_b, emit, early, mid):
            cs, cs_b = c["cs"]
            sn, sn_b = c["sn"]
            self.rope_tables(None, t0, cs, cs_b, sn, sn_b, c["W"])
            K.dma(K.ACT, self.scr["cs"][:, t0:t0 + TT], cs[:], reads=[cs_b], writes=self.sbufs("cs", t0, TT))
            K.dma(K.ACT, self.scr["sn"][:, t0:t0 + TT], sn[:], reads=[sn_b], writes=self.sbufs("sn", t0, TT))
            wr, wrb = c["wr"]
            wrs, wrsb = c["wrs"]
            p1, p1b = c["pp"].next()
            p2, p2b = c["pp"].next()
            for kc in range(KC):
                self.mm(p1[0:64, :], wr[:, kc, :], hT[:, kc, :], kc == 0, kc == KC - 1, [wrb, hT_b], [p1b])
            for kc in range(KC):
                self.mm(p2[0:64, :], wrs[:, kc, :], hT[:, kc, :], kc == 0, kc == KC - 1, [wrsb, hT_b], [p2b])
            r1, r1b = c["rt"].next()
            r2, r2b = c["rt"].next()
            self.tt(K.DVE, r1[:], p1[0:64, :], cs[:], ALU.mult, [p1b, cs_b], [r1b])
            self.tt(K.DVE, r2[:], p2[0:64, :], sn[:], ALU.mult, [p2b, sn_b], [r2b])
            self.tt(K.POOL, r1[:], r1[:], r2[:], ALU.add, [r1b, r2b], [r1b])
            K.dma(K.ACT, self.scr["kr"][:, t0:t0 + TT], r1[:], reads=[r1b], writes=self.sbufs("kr", t0, TT))
            early()
            for nm, col0, rows in segs:
                for s in range(rows // SW):
                    if nm == "rw" and s == 5:
                        mid()
                    wt, wtb = self.wslab(c["w"], 'w_in', 0, KC, col0 + s * SW, SW)
                    for j in range(SW // P):
                        n = s * (SW // P) + j
                        ps, psb = c["pp"].next()
                        for kc in range(KC):
                            self.mm(ps[:], wt[:, kc, j * P:(j + 1) * P], hT[:, kc, :], kc == 0, kc == KC - 1,
                                    [wtb, hT_b], [psb])
                        st, stb = c["st"].next()
                        self.cp(K.ACT if n % 2 == 0 else K.DVE, st[:], ps[:], [psb], [stb])
                        K.dma(K.ACT, self.scr[nm][n * P:(n + 1) * P, t0:t0 + TT], st[:], reads=[stb],
                              writes=self.sbufs(nm, t0, TT))

        self.tile_phase(tag, src, None, 'n_mix_pre', None, False, setup, body, post=False)

    def phase_mla(self):
        K, cfg = self.K, self.cfg
        S, NB = cfg.S, cfg.NB
        tag = "ml"
        NKB = S // P
        scale = float((QK_NOPE + QK_ROPE) ** -0.5)
        with contextlib.ExitStack() as es:
            wuq = self.sb(es, tag + "wuq", [P, 4, 1536], BF16)
            wuqs = self.sb(es, tag + "wuqs", [P, 4, 8, 64], BF16)
            wukv = self.sb(es, tag + "wukv", [P, 2, 2048], BF16)
            woa = self.sb(es, tag + "woa", [P, 8, D], BF16)
            wuq_b, wuqs_b, wukv_b, woa_b = Buf(), Buf(), Buf(), Buf()
            vq = self.wb['w_uq'].rearrange("(c p) n -> p c n", p=P)
            K.dma(K.SP, wuq[:], vq, reads=self.wbuf['w_uq'], writes=[wuq_b])
            vq4 = self.wb['w_uq'].rearrange("(c p) (h d) -> p c h d", p=P, h=8)
            with self.nc.allow_non_contiguous_dma(reason="rope weight half-swap (64B runs)"):
                for kc in range(4):
                    K.dma(K.SP, wuqs[:, kc, :, 0:32], vq4[:, kc, :, 160:192], reads=self.wbuf['w_uq'], writes=[wuqs_b])
                    K.dma(K.SP, wuqs[:, kc, :, 32:64], vq4[:, kc, :, 128:160], reads=self.wbuf['w_uq'], writes=[wuqs_b])
            K.dma(K.SP, wukv[:], self.wb['w_ukv'].rearrange("(c p) n -> p c n", p=P),
                  reads=self.wbuf['w_ukv'], writes=[wukv_b])
            K.dma(K.SP, woa[:], self.wb['w_oa'].rearrange("(c p) n -> p c n", p=P),
                  reads=self.wbuf['w_oa'], writes=[woa_b])
            wukv4 = wukv[:].rearrange("p c (h two d) -> p c h two d", h=8, two=2)

            Kn = self.sb(es, tag + "Kn", [P, 8, S], BF16)
            Kn_b = [Buf() for _ in range(S // TT)]
            Vs = self.sb(es, tag + "Vs", [P, NKB, 1024], BF16)
            Vs_b = [Buf() for _ in range(NKB)]
            Kr = self.sb(es, tag + "Kr", [64, S], BF16)
            Kr_b = [Buf() for _ in range(S // TT)]
            pt = self.sb(es, tag + "pt", [P, NKB, TT], BF16)
            pt_b = [Buf() for _ in range(NKB)]
            lat = self.sb(es, tag + "lat", [P, 4, TT], F32)
            lat_b = [Buf() for _ in range(4)]
            latn = self.sb(es, tag + "latn", [P, 4, TT], BF16)
            latn_b = Buf()
            krl = self.sb(es, tag + "krl", [64, TT], F32)
            krl_b = Buf()
            cs = self.sb(es, tag + "cs", [64, TT], F32)
            sn = self.sb(es, tag + "sn", [64, TT], F32)
            cs_b, sn_b = Buf(), Buf()
            qn_r = self.sring(es, tag + "qn", 2, [P, TT], BF16)
            qr_r = self.sring(es, tag + "qr", 2, [64, TT], BF16)
            rt = self.sring(es, tag + "rt", 4, [64, TT], F32)
            rec = self.sring(es, tag + "rec", 2, [P, TT], F32)
            oT = self.sb(es, tag + "oT", [P, 8, TT], BF16)
            oT_b = [Buf() for _ in range(8)]
            yst = self.sring(es, tag + "yst", 4, [P, TT], F32)
            R = self.norm_res(es, tag)
            rstd, rstd_b = R["rstd"]
            sps = self.pring(es, tag + "sps", 2)
            ops_, ops_b = self.ps(es, tag + "ops", [P, 512], F32), Buf()
            rps, rps_b = self.ps(es, tag + "rps", [P, 512], F32), Buf()
            mps = self.pring(es, tag + "mps", 3)
            nq, nq_b = self.vec['n_q_lat'], self.vec_b['n_q_lat']
            nkv, nkv_b = self.vec['n_kv_lat'], self.vec_b['n_kv_lat']

            for b in range(NB):
                for tt_ in range(S // TT):
                    t0 = b * S + tt_ * TT
                    K.dma(K.SP, lat[:, 0:2, :], self.scr["ckv"].rearrange("(c p) t -> p c t", p=P)[:, :, t0:t0 + TT],
                          reads=self.sbufs("ckv", t0, TT), writes=lat_b[0:2])
                    self.rms_rstd(lambda c: lat[:, c, :], lambda c: lat_b[c], 2, TT, KV_LORA, R)
                    for c in range(2):
                        self.stt(latn[:, c, :], lat[:, c, :], nkv[:, c:c + 1], rstd[:], ALU.mult, ALU.mult,
                                 [lat_b[c], nkv_b, rstd_b], [latn_b])
                    for h in range(8):
                        ps, psb = mps.next()
                        for kc in range(2):
                            self.mm(ps[:], wukv[:, kc, h * 256:h * 256 + 128], latn[:, kc, :], kc == 0, kc == 1,
                                    [wukv_b, latn_b], [psb])
                        self.cp(K.ACT if h % 2 == 0 else K.DVE, Kn[:, h, tt_ * TT:(tt_ + 1) * TT], ps[:],
                                [psb], [Kn_b[tt_]])
                    for sb_ in range(TT // P):
                        blk = tt_ * (TT // P) + sb_
                        for g in range(2):
                            ps, psb = mps.next()
                            for kc in range(2):
                                self.mm(ps[:].rearrange("p (h d) -> p h d", h=4),
                                        latn[:, kc, sb_ * P:(sb_ + 1) * P], wukv4[:, kc, g * 4:(g + 1) * 4, 1, :],
                                        kc == 0, kc == 1, [wukv_b, latn_b], [psb])
                            self.cp(K.ACT if g == 0 else K.DVE, Vs[:, blk, g * 512:(g + 1) * 512], ps[:],
                                    [psb], [Vs_b[blk]])
                    K.dma(K.SP, krl[:], self.scr["kr"][:, t0:t0 + TT], reads=self.sbufs("kr", t0, TT), writes=[krl_b])
                    self.cp(K.POOL, Kr[:, tt_ * TT:(tt_ + 1) * TT], krl[:], [krl_b], [Kr_b[tt_]])
                for qt in range(S // TT):
                    t0 = b * S + qt * TT
                    K.dma(K.SP, lat[:], self.scr["cq"].rearrange("(c p) t -> p c t", p=P)[:, :, t0:t0 + TT],
                          reads=self.sbufs("cq", t0, TT), writes=lat_b)
                    K.dma(K.SP, cs[:], self.scr["cs"][:, t0:t0 + TT], reads=self.sbufs("cs", t0, TT), writes=[cs_b])
                    K.dma(K.SP, sn[:], self.scr["sn"][:, t0:t0 + TT], reads=self.sbufs("sn", t0, TT), writes=[sn_b])
                    self.rms_rstd(lambda c: lat[:, c, :], lambda c: lat_b[c], 4, TT, Q_LORA, R)
                    for c in range(4):
                        self.stt(latn[:, c, :], lat[:, c, :], nq[:, c:c + 1], rstd[:], ALU.mult, ALU.mult,
                                 [lat_b[c], nq_b, rstd_b], [latn_b])
                    nkb = 4 * (qt + 1)
                    for h in range(8):
                        ps, psb = mps.next()
                        for kc in range(4):
                            self.mm(ps[:], wuq[:, kc, h * 192:h * 192 + 128], latn[:, kc, :], kc == 0, kc == 3,
                                    [wuq_b, latn_b], [psb])
                        qn, qnb = qn_r.next()
                        self.cp(K.ACT, qn[:], ps[:], [psb], [qnb])
                        p1, p1b = mps.next()
                        p2, p2b = mps.next()
                        for kc in range(4):
                            self.mm(p1[0:64, :], wuq[:, kc, h * 192 + 128:h * 192 + 192], latn[:, kc, :],
                                    kc == 0, kc == 3, [wuq_b, latn_b], [p1b])
                        for kc in range(4):
                            self.mm(p2[0:64, :], wuqs[:, kc, h, :], latn[:, kc, :], kc == 0, kc == 3,
                                    [wuqs_b, latn_b], [p2b])
                        r1, r1b = rt.next()
                        r2, r2b = rt.next()
                        self.tt(K.DVE, r1[:], p1[0:64, :], cs[:], ALU.mult, [p1b, cs_b], [r1b])
                        self.tt(K.DVE, r2[:], p2[0:64, :], sn[:], ALU.mult, [p2b, sn_b], [r2b])
                        qr, qrb = qr_r.next()
                        self.tt(K.POOL, qr[:], r1[:], r2[:], ALU.add, [r1b, r2b], [qrb])
                        for kb in range(nkb):
                            qlo = max(0, kb * P - qt * TT)
                            sp, spb = sps.next()
                            self.mm(sp[:, qlo:], Kn[:, h, kb * P:(kb + 1) * P], qn[:, qlo:], True, False,
                                    [Kn_b[kb // 4], qnb], [spb])
                            self.mm(sp[:, qlo:], Kr[:, kb * P:(kb + 1) * P], qr[:, qlo:], False, True,
                                    [Kr_b[kb // 4], qrb], [spb])
                            self.act(pt[:, kb, qlo:], sp[:, qlo:], AF.Exp, [spb], [pt_b[kb]], scale=scale)
                            if kb * P >= qt * TT:
                                self.mset(K.POOL, pt[64:128, kb, qlo:qlo + 64], 0.0, [], [pt_b[kb]])
                        for kb in range(nkb):
                            qlo = max(0, kb * P - qt * TT)
                            self.mm(ops_[:, qlo:], Vs[:, kb, h * P:(h + 1) * P], pt[:, kb, qlo:], kb == 0, kb == nkb - 1,
                                    [Vs_b[kb], pt_b[kb]], [ops_b])
                        for kb in range(nkb):
                            qlo = max(0, kb * P - qt * TT)
                            self.mm(rps[:, qlo:], self.ones_bf[:], pt[:, kb, qlo:], kb == 0, kb == nkb - 1,
                                    [self.ones_b, pt_b[kb]], [rps_b])
                        rc, rcb = rec.next()
                        self.recip(rc[:], rps[:], [rps_b], [rcb])
                        self.tt(K.DVE, oT[:, h, :], ops_[:], rc[:], ALU.mult, [ops_b, rcb], [oT_b[h]])
                    for n in range(16):
                        ps, psb = mps.next()
                        for h in range(8):
                            self.mm(ps[:], woa[:, h, n * P:(n + 1) * P], oT[:, h, :], h == 0, h == 7,
                                    [woa_b, oT_b[h]], [psb])
                        st, stb = yst.next()
                        self.cp(K.ACT if n % 2 == 0 else K.DVE, st[:], ps[:], [psb], [stb])
                        K.dma(K.ACT, self.scr["ya"][n * P:(n + 1) * P, t0:t0 + TT], st[:], reads=[stb],
                              writes=self.sbufs("ya", t0, TT))
        K.barrier()

    def phase_merge(self, src, dst, use_b=True):
        K = self.K
        KC, SW = D // P, 256
        tag = "mg"

        def setup(es):
            c = {}
            c["wga"] = self.sring(es, tag + "wga", 2, [P, KC, SW], BF16)
            c["wgb"] = self.sring(es, tag + "wgb", 2, [P, KC, SW], BF16)
            c["wo"] = self.sring(es, tag + "wo", 2, [P, KC, SW], BF16)
            c["mT"] = (self.sb(es, tag + "mT", [P, KC, TT], BF16), Buf())
            c["ya"] = self.sring(es, tag + "ya", 4, [P, TT], F32)
            c["yb"] = self.sring(es, tag + "yb", 4, [P, TT], F32)
            c["sa"] = self.sring(es, tag + "sa", 2, [P, TT], F32)
            c["sbb"] = self.sring(es, tag + "sbb", 2, [P, TT], F32)
            c["pa"] = self.pring(es, tag + "pa", 2)
            c["pb"] = self.pring(es, tag + "pb", 2)
            c["py"] = self.pring(es, tag + "py", 2)
            return c

        bg, bg_b = self.vec['b_gate'], self.vec_b['b_gate']

        def body(c, t, t0, hT, hT_b, emit, early, mid):
            mT, mT_b = c["mT"]
            for s in range(D // SW):
                wa, wab = self.wslab(c["wga"], 'w_in', 0, KC, GATE0 + s * SW, SW)
                wb_, wbb = self.wslab(c["wgb"], 'w_in', 0, KC, GATE0 + D + s * SW, SW)
                for j in range(SW // P):
                    n = s * (SW // P) + j
                    pa, pab = c["pa"].next()
                    pb, pbb = c["pb"].next()
                    for kc in range(KC):
                        self.mm(pa[:], wa[:, kc, j * P:(j + 1) * P], hT[:, kc, :], kc == 0, kc == KC - 1,
                                [wab, hT_b], [pab])
                    for kc in range(KC):
                        self.mm(pb[:], wb_[:, kc, j * P:(j + 1) * P], hT[:, kc, :], kc == 0, kc == KC - 1,
                                [wbb, hT_b], [pbb])
                    ya, yab = c["ya"].next()
                    yb, ybb = c["yb"].next()
                    K.dma(K.SP, ya[:], self.scr["ya"][n * P:(n + 1) * P, t0:t0 + TT],
                          reads=self.sbufs("ya", t0, TT), writes=[yab])
                    sa, sab = c["sa"].next()
                    self.act(sa[:], pa[:], AF.Sigmoid, [pab, bg_b], [sab], bias=bg[:, n:n + 1])
                    self.tt(K.DVE, sa[:], sa[:], ya[:], ALU.mult, [sab, yab], [sab])
                    if use_b:
                        K.dma(K.SP, yb[:], self.scr["yb"][n * P:(n + 1) * P, t0:t0 + TT],
                              reads=self.sbufs("yb", t0, TT), writes=[ybb])
                        sbb, sbbb = c["sbb"].next()
                        self.act(sbb[:], pb[:], AF.Sigmoid, [pbb, bg_b], [sbbb], bias=bg[:, 16 + n:17 + n])
                        self.tt(K.POOL, sbb[:], sbb[:], yb[:], ALU.mult, [sbbb, ybb], [sbbb])
                        self.tt(K.DVE, mT[:, n, :], sa[:], sbb[:], ALU.add, [sab, sbbb], [mT_b])
                    else:
                        self.cp(K.DVE, mT[:, n, :], sa[:], [sab], [mT_b])
                if s == 1:
                    early()
            mid()
            for s in range(D // SW):
                wo, wob = self.wslab(c["wo"], 'w_o', 0, KC, s * SW, SW)
                for j in range(SW // P):
                    n = s * (SW // P) + j
                    y_ps, y_b = c["py"].next()
                    for kc in range(KC):
                        self.mm(y_ps[:], wo[:, kc, j * P:(j + 1) * P], mT[:, kc, :], kc == 0, kc == KC - 1,
                                [wob, mT_b], [y_b])
                    emit(n, y_ps[:], y_b)

        self.tile_phase(tag, src, dst, 'n_mix_pre', 'n_mix_post', False, setup, body)

    def phase_xattn(self, src, dst):
        K, cfg = self.K, self.cfg
        KC, SW = D // P, 256
        tag = "xa"
        scale = float(MEM_D ** -0.5)
        NMB = N_MEM // P

        def setup(es):
            c = {}
            c["wq"] = self.sring(es, tag + "wq", 2, [P, KC, SW], BF16)
            c["wkv"] = self.sring(es, tag + "wkv", 2, [P, KC, SW], BF16)
            c["wco"] = self.sring(es, tag + "wco", 2, [P, 8, SW], BF16)
            c["qT"] = (self.sb(es, tag + "qT", [P, 8, TT], BF16), [Buf() for _ in range(8)])
            c["oT"] = (self.sb(es, tag + "oT", [P, 8, TT], BF16), [Buf() for _ in range(8)])
            c["KT"] = (self.sb(es, tag + "KT", [P, cfg.NB, 8, N_MEM], BF16), Buf())
            c["Vm"] = (self.sb(es, tag + "Vm", [P, cfg.NB, NMB, 1024], BF16), Buf())
            c["pt"] = self.sring(es, tag + "pt", 4, [P, TT], BF16)
            c["rec"] = self.sring(es, tag + "rec", 2, [P, TT], F32)
            c["pm"] = self.pring(es, tag + "pm", 3)
            c["po"] = (self.ps(es, tag + "po", [P, 512], F32), Buf())
            c["pr"] = (self.ps(es, tag + "pr", [P, 512], F32), Buf())
            KT, KT_b = c["KT"]
            Vm, Vm_b = c["Vm"]
            with contextlib.ExitStack() as es2:
                mt = self.sring(es2, tag + "mt", 2, [P, D], F32)
                mT = self.sb(es2, tag + "mTm", [P, KC, N_MEM], F32)
                mT_b = [Buf() for _ in range(KC)]
                mn = self.sb(es2, tag + "mn", [P, KC, N_MEM], BF16)
                mn_b = Buf()
                R = self.norm_res(es2, tag + "m", width=N_MEM)
                rstd, rstd_b = R["rstd"]
                gm, gm_b = self.vec['n_mem'], self.vec_b['n_mem']
                for b in range(cfg.NB):
                    for mb in range(NMB):
                        m_, m_b = mt.next()
                        r0 = b * N_MEM + mb * P
                        K.dma(K.SP, m_[:], self.mem_in[r0:r0 + P, :], reads=[], writes=[m_b])
                        for g in range(4):
                            ps, psb = c["pm"].next()
                            for j in range(4):
                                self.tp(ps[:, j * P:(j + 1) * P], m_[:, (g * 4 + j) * P:(g * 4 + j + 1) * P],
                                        self.ident[:], [m_b, self.ident_b], [psb])
                            self.cp(K.ACT if g % 2 == 0 else K.DVE, mT[:, g * 4:(g + 1) * 4, mb * P:(mb + 1) * P],
                                    ps[:].rearrange("p (j q) -> p j q", j=4), [psb], mT_b[g * 4:(g + 1) * 4])
                    self.rms_rstd(lambda cc: mT[:, cc, :], lambda cc: mT_b[cc], KC, N_MEM, D, R)
                    for cc in range(KC):
                        self.stt(mn[:, cc, :], mT[:, cc, :], gm[:, cc:cc + 1], rstd[:, :N_MEM], ALU.mult, ALU.mult,
                                 [mT_b[cc], gm_b, rstd_b], [mn_b])
                    for s in range(2048 // SW):
                        wt, wtb = self.wslab(c["wkv"], 'w_ckv', 0, KC, s * SW, SW)
                        h, part = s // 2, s % 2
                        if part == 0:
                            for j in range(2):
                                ps, psb = c["pm"].next()
                                for kc in range(KC):
                                    self.mm(ps[:, :N_MEM], wt[:, kc, j * P:(j + 1) * P], mn[:, kc, :], kc == 0,
                                            kc == KC - 1, [wtb, mn_b], [psb])
                                self.cp(K.ACT, KT[:, b, h * 2 + j, :], ps[:, :N_MEM], [psb], [KT_b])
                        else:
                            for mb in range(NMB):
                                ps, psb = c["pm"].next()
                                for kc in range(KC):
                                    self.mm(ps[:, :SW], mn[:, kc, mb * P:(mb + 1) * P], wt[:, kc, :], kc == 0,
                                            kc == KC - 1, [wtb, mn_b], [psb])
                                self.cp(K.DVE, Vm[:, b, mb, h * 256:(h + 1) * 256], ps[:, :SW], [psb], [Vm_b])
                K.barrier()
            c["py"] = self.pring(es, tag + "py", 2)
            return c

        def body(c, t, t0, hT, hT_b, emit, early, mid):
            b = t0 // cfg.S
            qT, qT_b = c["qT"]
            oT, oT_b = c["oT"]
            KT, KT_b = c["KT"]
            Vm, Vm_b = c["Vm"]
            for s in range(1024 // SW):
                wt, wtb = self.wslab(c["wq"], 'w_cq', 0, KC, s * SW, SW)
                for j in range(SW // P):
                    n = s * (SW // P) + j
                    ps, psb = c["pm"].next()
                    for kc in range(KC):
                        self.mm(ps[:], wt[:, kc, j * P:(j + 1) * P], hT[:, kc, :], kc == 0, kc == KC - 1,
                                [wtb, hT_b], [psb])
                    self.cp(K.ACT, qT[:, n, :], ps[:], [psb], [qT_b[n]])
                if s == 1:
                    early()
            po, po_b = c["po"]
            pr, pr_b = c["pr"]
            for h in range(MEM_H):
                pts = []
                for mb in range(NMB):
                    ps, psb = c["pm"].next()
                    for dc in range(2):
                        self.mm(ps[:], KT[:, b, h * 2 + dc, mb * P:(mb + 1) * P], qT[:, h * 2 + dc, :], dc == 0, dc == 1,
                                [KT_b, qT_b[h * 2 + dc]], [psb])
                    p_, p_b = c["pt"].next()
                    self.act(p_[:], ps[:], AF.Exp, [psb], [p_b], scale=scale)
                    pts.append((p_, p_b))
                for mb in range(NMB):
                    self.mm(pr[:], self.ones_bf[:], pts[mb][0][:], mb == 0, mb == NMB - 1,
                            [self.ones_b, pts[mb][1]], [pr_b])
                rc, rcb = c["rec"].next()
                self.recip(rc[:], pr[:], [pr_b], [rcb])
                for dc in range(2):
                    for mb in range(NMB):
                        self.mm(po[:], Vm[:, b, mb, h * 256 + dc * P:h * 256 + (dc + 1) * P], pts[mb][0][:],
                                mb == 0, mb == NMB - 1, [Vm_b, pts[mb][1]], [po_b])
                    self.tt(K.DVE, oT[:, h * 2 + dc, :], po[:], rc[:], ALU.mult, [po_b, rcb], [oT_b[h * 2 + dc]])
            mid()
            for s in range(D // SW):
                wt, wtb = self.wslab(c["wco"], 'w_co', 0, 8, s * SW, SW)
                for j in range(SW // P):
                    n = s * (SW // P) + j
                    y_ps, y_b = c["py"].next()
                    for kc in range(8):
                        self.mm(y_ps[:], wt[:, kc, j * P:(j + 1) * P], oT[:, kc, :], kc == 0, kc == 7,
                                [wtb, oT_b[kc]], [y_b])
                    emit(n, y_ps[:], y_b)

        self.tile_phase(tag, src, dst, 'n_x_pre', 'n_x_post', False, setup, body)

    def phase_rwkv(self):
        K, cfg = self.K, self.cfg
        S, NB = cfg.S, cfg.NB
        RT, C = 256, 64
        NCH = RT // C
        NPAIR = 16 * NCH
        tag = "rk"
        with contextlib.ExitStack() as es:
            lww = self.sb(es, tag + "lww", [P, RW], BF16)
            wg2 = self.sb(es, tag + "wg2", [P, RW], BF16)
            wob = self.sb(es, tag + "wob", [P, 8, D], BF16)
            lww_b, wg2_b, wob_b = Buf(), Buf(), Buf()
            K.dma(K.SP, lww[0:64, :], self.wb['w_w2'][:, :], reads=self.wbuf['w_w2'], writes=[lww_b])
            K.dma(K.SP, lww[64:128, :], self.wb['w_a2'][:, :], reads=self.wbuf['w_a2'], writes=[lww_b])
            K.dma(K.SP, wg2[:], self.wb['w_g2'][:, :], reads=self.wbuf['w_g2'], writes=[wg2_b])
            K.dma(K.SP, wob[:], self.wb['w_ob'].rearrange("(c p) n -> p c n", p=P),
                  reads=self.wbuf['w_ob'], writes=[wob_b])
            m2 = self.sb(es, tag + "m2", [P, 1280], F32)
            m2_b = Buf()
            K.dma(K.SP, m2[:], self.msk2_in[:, 0:1280], reads=[], writes=[m2_b])
            lmask = m2[0:64, 0:512]
            imask = m2[0:64, 512:1024]
            rsm = m2[:, 1024:1280]
            eps2 = self.sb(es, tag + "eps2", [P, 1], F32)
            eps2_b = Buf()
            self.mset(K.DVE, eps2[:], 64e-5, [], [eps2_b])
            omk = self.sb(es, tag + "omk", [P, 8], F32)
            omk_b = Buf()
            self.ts(K.DVE, omk[:], self.vec['k_a'][:], -1.0, 1.0, ALU.mult, ALU.add, [self.vec_b['k_a']], [omk_b])
            V = self.vec
            Vb = self.vec_b

            yt = self.sring(es, tag + "yt", 3, [P, RT + 1], F32)
            tmp = self.sring(es, tag + "tmp", 21, [P, RT], F32)
            tbf = self.sring(es, tag + "tbf", 4, [P, RT], BF16)
            dwa = self.sb(es, tag + "dwa", [P, RT], BF16)
            dwa_b = Buf()
            sgt = self.sb(es, tag + "sgt", [P, RT], BF16)
            sgt_b = Buf()
            NP2 = 16 * 2
            AR = self.sb(es, tag + "AR", [P, 8, NCH, 2, C], BF16)
            VV = self.sb(es, tag + "VV", [P, 8, NCH, 2, C], BF16)
            BKz = [self.sb(es, tag + "BKz%d" % i, [P, 8, NCH, 2, C], BF16) for i in range(2)]
            Az = [self.sb(es, tag + "Az%d" % i, [P, 8, NCH, C], BF16) for i in range(2)]
            AR_b = [Buf() for _ in range(8)]
            BK_b = [Buf() for _ in range(8)]
            VV_b = [Buf() for _ in range(8)]
            self.mset(K.POOL, VV[:], 0.0, [], VV_b)
            for i in range(2):
                self.mset(K.POOL, BKz[i][:], 0.0, [], BK_b)
                self.mset(K.POOL, Az[i][:], 0.0, [], AR_b)
            bonus = self.sb(es, tag + "bonus", [P, 8, RT], F32)
            bonus_b = [Buf() for _ in range(8)]
            PC = self.sb(es, tag + "PC", [P, 8, NCH], F32)
            PC_b = [Buf() for _ in range(8)]
            Gm = self.sb(es, tag + "Gm", [P, NP2, 128], BF16)
            Gm_b = [Buf() for _ in range(NP2 // 4)]
            Ltb = self.sb(es, tag + "Ltb", [P, 16, 128], BF16)
            Lnb = self.sb(es, tag + "Lnb", [P, 16, 128], BF16)
            Gb = self.sb(es, tag + "Gb", [P, 16, 128], BF16)
            Ltb_b = [Buf() for _ in range(4)]
            Lnb_b = [Buf() for _ in range(4)]
            Gb_b = [Buf() for _ in range(4)]
            m3 = self.sb(es, tag + "m3", [P, 1536], F32)
            m3_b = Buf()
            K.dma(K.SP, m3[:], self.msk3_in[:, :], reads=[], writes=[m3_b])
            TTa = self.sb(es, tag + "TTa", [64, NP2, C], BF16)
            TT_b = [Buf() for _ in range(NP2 // 8)]
            lt_r = self.sring(es, tag + "ltr", 7, [P, 4, 128], BF16)
            ln_r = self.sring(es, tag + "lnr", 7, [P, 4, 128], BF16)
            BKt = self.sb(es, tag + "BKt", [P, NP2, 128], BF16)
            BKt_b = [Buf() for _ in range(NP2 // 4)]
            UV = self.sb(es, tag + "UV", [P, NP2, 128], BF16)
            UVv_b = [Buf() for _ in range(4)]
            UVu_b = [[Buf() for _ in range(2)] for _ in range(2)]
            self.mset(K.POOL, UV[:], 0.0, [], UVv_b + UVu_b[0] + UVu_b[1])
            UV5 = UV[:].rearrange("p (h r c) v -> p h r c v", r=2, c=2)
            Hf = self.sb(es, tag + "Hf", [P, 8, 128], F32)
            Hb = self.sb(es, tag + "Hb", [P, 8, 128], BF16)
            Hf_b = [Buf(), Buf()]
            Hb_b = [Buf(), Buf()]
            Xs = self.sring(es, tag + "Xs", 4, [64, 512], BF16)
            osb = self.sb(es, tag + "osb", [P, 8, RT], F32)
            osb_b = [[Buf() for _ in range(NCH)] for _ in range(2)]
            og = self.sb(es, tag + "og", [P, 8, RT], BF16)
            og_b = [Buf() for _ in range(8)]
            yst = self.sring(es, tag + "yst", 2, [P, RT], F32)
            pa = self.pring(es, tag + "pa", 4)
            ptr = self.pring(es, tag + "ptr", 2, [P, 512], BF16)
            pc = self.pring(es, tag + "pc", 2)
            rwT = self.scr["rw"]

            def v3(ap):
                return ap.rearrange("p (c t) -> p c t", t=C)

            def load_shift(ci, t0, first):
                y, yb = yt.next()
                if first:
                    self.mset(K.POOL, y[:, 0:1], 0.0, [], [yb])
                    K.dma(K.SP, y[:, 1:RT + 1], rwT[ci * P:(ci + 1) * P, t0:t0 + RT],
                          reads=self.sbufs("rw", t0, RT), writes=[yb])
                else:
                    K.dma(K.SP, y[:, :], rwT[ci * P:(ci + 1) * P, t0 - 1:t0 + RT],
                          reads=self.sbufs("rw", t0 - 1, RT + 1), writes=[yb])
                d, db = tmp.next()
                self.tt(K.DVE, d[:], y[:, 0:RT], y[:, 1:RT + 1], ALU.subtract, [yb], [db])
                o, ob = tmp.next()
                self.stt(o[:], d[:], V['mu_shift'][:, ci:ci + 1], y[:, 1:RT + 1], ALU.mult, ALU.add,
                         [db, yb, Vb['mu_shift']], [ob])
                return o, ob

            for b in range(NB):
                self.mset(K.DVE, Hf[:], 0.0, [], Hf_b)
                self.mset(K.DVE, Hb[:], 0.0, [], Hb_b)
                for ti in range(S // RT):
                    t0 = b * S + ti * RT
                    first = (ti == 0)
                    p24, p24b = load_shift(24, t0, first)
                    self.act(dwa[0:64, :], p24[0:64, :], AF.Tanh, [p24b], [dwa_b])
                    self.cp(K.DVE, dwa[64:128, :], p24[64:128, :], [p24b], [dwa_b])
                    p25, p25b = load_shift(25, t0, first)
                    self.act(sgt[:], p25[:], AF.Sigmoid, [p25b], [sgt_b])
                    for n in range(8):
                        r, rb = load_shift(n, t0, first)
                        k, kb = load_shift(8 + n, t0, first)
                        v, vb = load_shift(16 + n, t0, first)
                        ps, psb = pa.next()
                        self.mm(ps[:, :RT], lww[0:64, n * P:(n + 1) * P], dwa[0:64, :], True, True,
                                [lww_b, dwa_b], [psb])
                        lw, lwb = tmp.next()
                        self.act(lw[:], ps[:, :RT], AF.Sigmoid, [psb, Vb['w0']], [lwb], bias=V['w0'][:, n:n + 1])
                        self.ts(K.DVE, lw[:], lw[:], -0.6065306597126334, None, ALU.mult, None, [lwb], [lwb])
                        ps2, ps2b = pa.next()
                        self.mm(ps2[:, :RT], lww[64:128, n * P:(n + 1) * P], dwa[64:128, :], True, True,
                                [lww_b, dwa_b], [ps2b])
                        a, ab = tmp.next()
                        self.act(a[:], ps2[:, :RT], AF.Sigmoid, [ps2b, Vb['a0']], [ab], bias=V['a0'][:, n:n + 1])
                        cs, csb = tmp.next()
                        K.op(K.DVE, [lwb, m2_b], [csb], lambda e, cs=cs, lw=lw: e.tensor_tensor_scan(
                            out=cs[:], data0=rsm, data1=lw[:], initial=0.0, op0=ALU.mult, op1=ALU.add))
                        e_in, e_inb = tmp.next()
                        self.act(e_in[:], cs[:], AF.Exp, [csb], [e_inb])
                        e_ng, e_ngb = tmp.next()
                        self.act(e_ng[:], cs[:], AF.Exp, [csb], [e_ngb], scale=-1.0)
                        dx, dxb = tmp.next()
                        self.tt(K.POOL, dx[:], cs[:], lw[:], ALU.subtract, [csb, lwb], [dxb])
                        e_ex, e_exb = tmp.next()
                        self.act(e_ex[:], dx[:], AF.Exp, [dxb], [e_exb])
                        self.cp(K.POOL, PC[:, n, :], v3(e_in[:])[:, :, C - 1], [e_inb], [PC_b[n]])
                        kk, kkb = tmp.next()
                        self.ts(K.DVE, kk[:], k[:], V['k_k'][:, n:n + 1], None, ALU.mult, None, [kb, Vb['k_k']], [kkb])
                        q2, q2b = tbf.next()
                        self.act(q2[:], kk[:], AF.Square, [kkb], [q2b])
                        ps3, ps3b = pa.next()
                        self.mm(ps3[:, :RT], self.bones_bf[:], q2[:], True, True, [self.bones_b, q2b], [ps3b])
                        nr, nrb = tmp.next()
                        self.act(nr[:], ps3[:, :RT], AF.Sqrt, [ps3b], [nrb])
                        self.ts(K.DVE, nr[:], nr[:], 1e-12, None, ALU.max, None, [nrb], [nrb])
                        self.recip(nr[:], nr[:], [nrb], [nrb])
                        self.tt(K.DVE, kk[:], kk[:], nr[:], ALU.mult, [kkb, nrb], [kkb])
                        km, kmb = tmp.next()
                        self.ts(K.DVE, km[:], a[:], V['k_a'][:, n:n + 1], omk[:, n:n + 1], ALU.mult, ALU.add,
                                [ab, Vb['k_a'], omk_b], [kmb])
                        self.tt(K.DVE, km[:], km[:], k[:], ALU.mult, [kmb, kb], [kmb])
                        rk, rkb = tbf.next()
                        self.stt(rk[:], r[:], V['r_k'][:, n:n + 1], km[:], ALU.mult, ALU.mult,
                                 [rb, kmb, Vb['r_k']], [rkb])
                        ps4, ps4b = pa.next()
                        self.mm(ps4[:, :RT], self.bones_bf[:], rk[:], True, True, [self.bones_b, rkb], [ps4b])
                        self.tt(K.DVE, bonus[:, n, :], ps4[:, :RT], v[:], ALU.mult, [ps4b, vb], [bonus_b[n]])
                        at, atb = tmp.next()
                        self.stt(at[:], kk[:], -1.0, e_ex[:], ALU.mult, ALU.mult, [kkb, e_exb], [atb])
                        self.cp(K.ACT, AR[:, n, :, 0, :], v3(at[:]), [atb], [AR_b[n]])
                        self.cp(K.POOL, Az[0][0:64, n, :, :], v3(at[0:64, :]), [atb], [AR_b[n]])
                        self.cp(K.POOL, Az[1][64:128, n, :, :], v3(at[64:128, :]), [atb], [AR_b[n]])
                        self.tt(K.POOL, AR[:, n, :, 1, :], v3(r[:]), v3(e_in[:]), ALU.mult, [rb, e_inb], [AR_b[n]])
                        kb2, kb2b = tmp.next()
                        self.tt(K.POOL, kb2[:], kk[:], a[:], ALU.mult, [kkb, ab], [kb2b])
                        self.tt(K.DVE, kb2[:], kb2[:], e_ng[:], ALU.mult, [kb2b, e_ngb], [kb2b])
                        kt, ktb = tmp.next()
                        self.tt(K.POOL, kt[:], km[:], e_ng[:], ALU.mult, [kmb, e_ngb], [ktb])
                        self.cp(K.ACT, BKz[0][0:64, n, :, 0, :], v3(kb2[0:64, :]), [kb2b], [BK_b[n]])
                        self.cp(K.ACT, BKz[1][64:128, n, :, 0, :], v3(kb2[64:128, :]), [kb2b], [BK_b[n]])
                        self.cp(K.DVE, BKz[0][0:64, n, :, 1, :], v3(kt[0:64, :]), [ktb], [BK_b[n]])
                        self.cp(K.DVE, BKz[1][64:128, n, :, 1, :], v3(kt[64:128, :]), [ktb], [BK_b[n]])
                        self.cp(K.ACT, VV[:, n, :, 1, :], v3(v[:]), [vb], [VV_b[n]])

                    def f2(ap):
                        return ap.rearrange("p a t -> p (a t)")

                    if getattr(cfg, 'rstop', 99) < 2:
                        continue
                    for hf in range(2):
                        def pr(q, hf=hf):
                            h, cl = divmod(q, 2)
                            return h, h // 2, h % 2, 2 * hf + cl
                        for g4 in range(NP2 // 4):
                            ps, psb = pa.next()
                            for i in range(4):
                                h, hp, par, c = pr(g4 * 4 + i)
                                self.mm(ps[:, i * 128:(i + 1) * 128], f2(BKz[par][:, hp, c, :, :]), f2(AR[:, hp, c, :, :]),
                                        True, True, [BK_b[hp], AR_b[hp]], [psb])
                            self.tt(K.DVE, f2(Gm[:, g4 * 4:(g4 + 1) * 4, :]), ps[:], self.msk[:], ALU.mult,
                                    [psb, self.msk_b], [Gm_b[g4]])
                            pt1, pt1b = ptr.next()
                            for i in range(4):
                                h, hp, par, c = pr(g4 * 4 + i)
                                self.tp(pt1[:, i * 128:(i + 1) * 128], f2(BKz[par][:, hp, c, :, :]), self.ident_bf[:],
                                        [BK_b[hp], self.identbf_b], [pt1b])
                            self.cp(K.ACT, f2(BKt[:, g4 * 4:(g4 + 1) * 4, :]), pt1[:], [pt1b], [BKt_b[g4]])
                        for gv in range(4):
                            pt2, pt2b = ptr.next()
                            for i in range(4):
                                hpl, cl = divmod(i, 2)
                                hp, c = gv * 2 + hpl, 2 * hf + cl
                                self.tp(pt2[:, i * 128:(i + 1) * 128], f2(VV[:, hp, c, :, :]), self.ident_bf[:],
                                        [VV_b[hp], self.identbf_b], [pt2b])
                            pv = pt2[64:128, :].rearrange("p (h c v) -> p h c v", h=2, c=2)
                            self.cp(K.ACT, UV5[64:128, gv * 2:gv * 2 + 2, 0, :, 0:64], pv[:, :, :, 0:64], [pt2b], [UVv_b[gv]])
                            self.cp(K.DVE, UV5[64:128, gv * 2:gv * 2 + 2, 1, :, 64:128], pv[:, :, :, 64:128], [pt2b],
                                    [UVv_b[gv]])
                        c0 = 2 * hf
                        for g in range(4):
                            ps2, ps2b = pa.next()
                            for i in range(4):
                                h = g * 4 + i
                                hp, par = h // 2, h % 2
                                bk0 = BKz[par][:, hp, c0:c0 + 2, 0, :]
                                a0 = AR[:, hp, c0:c0 + 2, 0, :]
                                az0 = Az[par][:, hp, c0:c0 + 2, :]
                                o2 = ps2[:, i * 128:(i + 1) * 128].rearrange("p (a t) -> p a t", a=2)
                                self.mm(o2, az0.rearrange("p a t -> p (a t)"), bk0, True, True, [BK_b[hp], AR_b[hp]], [ps2b])
                            self.tt(K.DVE, f2(Lnb[:, g * 4:(g + 1) * 4, :]), ps2[:], m3[:, 512:1024], ALU.mult,
                                    [ps2b, m3_b], [Lnb_b[g]])
                            ptl, ptlb = ptr.next()
                            for i in range(4):
                                self.tp(ptl[:, i * 128:(i + 1) * 128], Lnb[:, g * 4 + i, :], self.ident_bf[:],
                                        [Lnb_b[g], self.identbf_b], [ptlb])
                            self.cp(K.ACT, f2(Ltb[:, g * 4:(g + 1) * 4, :]), ptl[:], [ptlb], [Ltb_b[g]])
                        if getattr(cfg, 'rstop', 99) < 3:
                            continue
                        ist = {}
                        for g in range(4):
                            G = Gb[:, g * 4:(g + 1) * 4, :]
                            gb = Gb_b[g]
                            self.tt(K.DVE, f2(G), f2(Ltb[:, g * 4:(g + 1) * 4, :]), m3[:, 1024:1536], ALU.add,
                                    [Ltb_b[g], m3_b], [gb])
                            ist[g] = dict(G=G, G2=f2(G), gb=gb, lt_bufs=[Ltb_b[g]], ln_bufs=[Lnb_b[g]],
                                          lt_cur=[Ltb[:, g * 4 + i, :] for i in range(4)],
                                          ln_cur=[Lnb[:, g * 4 + i, :] for i in range(4)])
                        for lvl in range(5):
                            last = (lvl == 4)
                            for g in range(4):
                                z = ist[g]
                                G, G2, gb = z["G"], z["G2"], z["gb"]
                                lt_cur, ln_cur, lt_bufs, ln_bufs = z["lt_cur"], z["ln_cur"], z["lt_bufs"], z["ln_bufs"]
                                if not last:
                                    p_lt, p_ltb = pa.next()
                                    for i in range(4):
                                        self.mm(p_lt[:, i * 128:(i + 1) * 128], ln_cur[i], lt_cur[i], True, True,
                                                lt_bufs + ln_bufs, [p_ltb])
                                p_ln, p_lnb = pa.next()
                                for i in range(4):
                                    self.mm(p_ln[:, i * 128:(i + 1) * 128], lt_cur[i], ln_cur[i], True, True,
                                            lt_bufs + ln_bufs, [p_lnb])
                                ln_n, ln_nb = ln_r.next()
                                self.cp(K.ACT, f2(ln_n[:]), p_ln[:], [p_lnb], [ln_nb])
                                if not last:
                                    lt_n, lt_nb = lt_r.next()
                                    self.cp(K.DVE, f2(lt_n[:]), p_lt[:], [p_ltb], [lt_nb])
                                p_g, p_gb = pa.next()
                                for i in range(4):
                                    self.mm(p_g[:, i * 128:(i + 1) * 128], ln_n[:, i, :], G[:, i, :], True, True,
                                            [ln_nb, gb], [p_gb])
                                self.tt(K.DVE, G2, p_g[:], G2, ALU.add, [p_gb, gb], [gb])
                                z["ln_cur"] = [ln_n[:, i, :] for i in range(4)]
                                z["ln_bufs"] = [ln_nb]
                                if not last:
                                    z["lt_cur"] = [lt_n[:, i, :] for i in range(4)]
                                    z["lt_bufs"] = [lt_nb]
                        TT4 = TTa[:].rearrange("p (h c) t -> p h c t", c=2)
                        for g in range(4):
                            gb = Gb_b[g]
                            self.cp(K.ACT, TT4[:, g * 4:(g + 1) * 4, 0, :], Gb[0:64, g * 4:(g + 1) * 4, 0:C], [gb], [TT_b[g]])
                            ps, psb = pa.next()
                            for i in range(4):
                                self.mm(ps[0:64, i * C:(i + 1) * C], self.ident_bf[:, 64:128], Gb[:, g * 4 + i, C:2 * C],
                                        True, True, [gb, self.identbf_b], [psb])
                            self.cp(K.DVE, TT4[:, g * 4:(g + 1) * 4, 1, :],
                                    ps[0:64, 0:4 * C].rearrange("p (a t) -> p a t", t=C), [psb], [TT_b[g]])
                        if getattr(cfg, 'rstop', 99) < 4:
                            continue
                        for cl in range(2):
                            c = 2 * hf + cl
                            sst = {0: {}, 1: {}}

                            def stage1(hg, cl=cl, c=c):
                                hb_ = Hb_b[hg]
                                uvb = UVu_b[hg][cl]
                                xss = []
                                for bank in range(2):
                                    xp, xpb = pc.next()
                                    for jj in range(4):
                                        h = hg * 8 + bank * 4 + jj
                                        hp, par, q = h // 2, h % 2, h * 2 + cl
                                        self.mm(xp[0:64, jj * 128:(jj + 1) * 128], Az[par][:, hp, c, :], Hb[:, hp, :],
                                                True, False, [AR_b[hp], hb_], [xpb])
                                        self.mm(xp[0:64, jj * 128:(jj + 1) * 128], Gm[:, q, 0:C], UV[:, q, :],
                                                False, True, [Gm_b[q // 4], UVv_b[hp // 2], uvb], [xpb])
                                    xs, xsb = Xs.next()
                                    self.cp(K.ACT if bank == 0 else K.DVE, xs[:], xp[0:64, :], [xpb], [xsb])
                                    xss.append((xs, xsb))
                                sst[hg]["xss"] = xss

                            def stage2(hg, cl=cl, c=c):
                                uvb = UVu_b[hg][cl]
                                for bank in range(2):
                                    xs, xsb = sst[hg]["xss"][bank]
                                    up, upb = pc.next()
                                    for jj in range(4):
                                        h = hg * 8 + bank * 4 + jj
                                        q = h * 2 + cl
                                        self.mm(up[0:64, jj * 128:(jj + 1) * 128], TTa[:, q, :],
                                                xs[:, jj * 128:(jj + 1) * 128], True, True, [TT_b[q // 8], xsb], [upb])
                                    hp0 = hg * 4 + bank * 2
                                    self.cp(K.DVE if bank == 0 else K.ACT, UV5[0:64, hp0:hp0 + 2, :, cl, :],
                                            up[0:64, :].rearrange("p (h r v) -> p h r v", h=2, r=2), [upb], [uvb])

                            def stage3(hg, cl=cl, c=c):
                                hb_, hf_ = Hb_b[hg], Hf_b[hg]
                                uvb = UVu_b[hg][cl]
                                op_, opb = pa.next()
                                for hl in range(4):
                                    hp = hg * 4 + hl
                                    qa, qb_ = (2 * hp) * 2 + cl, (2 * hp + 1) * 2 + cl
                                    o_ap = op_[:, hl * C:(hl + 1) * C]
                                    self.mm(o_ap, Hb[:, hp, :], AR[:, hp, c, 1, :], True, False, [hb_, AR_b[hp]], [opb])
                                    self.mm(o_ap, UV[:, qa, :], Gm[:, qa, C:2 * C], False, False,
                                            [uvb, UVv_b[hp // 2], Gm_b[qa // 4]], [opb])
                                    self.mm(o_ap, UV[:, qb_, :], Gm[:, qb_, C:2 * C], False, True,
                                            [uvb, UVv_b[hp // 2], Gm_b[qb_ // 4]], [opb])
                                self.cp(K.ACT, osb[:, hg * 4:(hg + 1) * 4, c * C:(c + 1) * C],
                                        op_[:, 0:4 * C].rearrange("p (a t) -> p a t", t=C), [opb], [osb_b[hg][c]])
                                hp_, hpb = pa.next()
                                for hl in range(4):
                                    hp = hg * 4 + hl
                                    qa, qb_ = (2 * hp) * 2 + cl, (2 * hp + 1) * 2 + cl
                                    h_ap = hp_[:, hl * 128:(hl + 1) * 128]
                                    self.mm(h_ap, BKt[:, qa, :], UV[:, qa, :], True, False,
                                            [BKt_b[qa // 4], uvb, UVv_b[hp // 2]], [hpb])
                                    self.mm(h_ap, BKt[:, qb_, :], UV[:, qb_, :], False, True,
                                            [BKt_b[qb_ // 4], uvb, UVv_b[hp // 2]], [hpb])
                                hs = Hf[:, hg * 4:(hg + 1) * 4, :]
                                self.tt(K.DVE, hs, hp_[:].rearrange("p (a v) -> p a v", v=128), hs, ALU.add,
                                        [hpb, hf_], [hf_])
                                pcb = PC[:, hg * 4:(hg + 1) * 4, c:c + 1].to_broadcast([P, 4, 128])
                                self.tt(K.DVE, hs, hs, pcb, ALU.mult, [hf_] + PC_b[hg * 4:(hg + 1) * 4], [hf_])
                                self.cp(K.ACT, Hb[:, hg * 4:(hg + 1) * 4, :], hs, [hf_], [hb_])

                            for stage in (stage1, stage2, stage3):
                                for hg in range(2):
                                    stage(hg)
                    if getattr(cfg, 'rstop', 99) < 5:
                        continue
                    for n in range(8):
                        hg = n // 4
                        o_n = osb[:, n, :]
                        o_bufs = osb_b[hg]
                        ob_, ob_b = tbf.next()
                        self.cp(K.ACT, ob_[:], o_n, o_bufs, [ob_b])
                        ps, psb = pa.next()
                        self.mm(ps[:, :RT], self.bones_bf[:], ob_[:], True, True, [self.bones_b, ob_b], [psb])
                        d, db = tmp.next()
                        self.stt(d[:], ps[:, :RT], -1.0 / 64.0, o_n, ALU.mult, ALU.add, [psb] + o_bufs, [db])
                        d2, d2b = tbf.next()
                        self.act(d2[:], d[:], AF.Square, [db], [d2b])
                        ps2, ps2b = pa.next()
                        self.mm(ps2[:, :RT], self.bones_bf[:], d2[:], True, True, [self.bones_b, d2b], [ps2b])
                        sd, sdb = tmp.next()
                        self.act(sd[:], ps2[:, :RT], AF.Sqrt, [ps2b, eps2_b], [sdb], scale=1.0 / 64.0, bias=eps2[:])
                        self.recip(sd[:], sd[:], [sdb], [sdb])
                        self.tt(K.DVE, d[:], d[:], sd[:], ALU.mult, [db, sdb], [db])
                        self.ts(K.DVE, d[:], d[:], V['lnx_w'][:, n:n + 1], V['lnx_b'][:, n:n + 1], ALU.mult, ALU.add,
                                [db, Vb['lnx_w'], Vb['lnx_b']], [db])
                        self.tt(K.POOL, d[:], d[:], bonus[:, n, :], ALU.add, [db, bonus_b[n]], [db])
                        ps3, ps3b = pa.next()
                        self.mm(ps3[:, :RT], wg2[:, n * P:(n + 1) * P], sgt[:], True, True, [wg2_b, sgt_b], [ps3b])
                        self.tt(K.DVE, og[:, n, :], d[:], ps3[:, :RT], ALU.mult, [db, ps3b], [og_b[n]])
                    for m_ in range(16):
                        ps, psb = pa.next()
                        for n in range(8):
                            self.mm(ps[:, :RT], wob[:, n, m_ * P:(m_ + 1) * P], og[:, n, :], n == 0, n == 7,
                                    [wob_b, og_b[n]], [psb])
                        st, stb = yst.next()
                        self.cp(K.ACT if m_ % 2 == 0 else K.DVE, st[:], ps[:, :RT], [psb], [stb])
                        K.dma(K.ACT, self.scr["yb"][m_ * P:(m_ + 1) * P, t0:t0 + RT], st[:], reads=[stb],
                              writes=self.sbufs("yb", t0, RT))
        K.barrier()

    def dump(self, name, srcap, bufs):
        K = self.K
        b = Buf()
        K.dma(K.SP, self.dbg[name][:, :], srcap, reads=bufs, writes=[b])
        self.out_bufs.append(b)

    def build(self):
        K, cfg = self.K, self.cfg
        self.out_bufs = []
        ph = cfg.phases
        with contextlib.ExitStack() as es:
            self.setup_consts(es)
            names = []
            for p_ in ph:
                names += PHASE_W[p_]
            self.cast_weights(names)
            self.phase_in()
            cur = 0
            if "ffn1" in ph:
                self.phase_ffn("f1", cur, 1 - cur, 'n_ffn1_pre', 'n_ffn1_post',
                               'w_ffn1_gate', 'w_ffn1_up', 'w_ffn1_down')
                cur = 1 - cur
            if "proj" in ph:
                self.phase_proj(cur)
            if "mla" in ph:
                self.phase_mla()
            if "rwkv" in ph:
                self.phase_rwkv()
            if "merge" in ph:
                self.phase_merge(cur, 1 - cur, use_b=("rwkv" in ph))
                cur = 1 - cur
            if "xattn" in ph:
                self.phase_xattn(cur, 1 - cur)
                cur = 1 - cur
            if "ffn2" in ph:
                self.phase_ffn("f2", cur, 1 - cur, 'n_ffn2_pre', 'n_ffn2_post',
                               'w_ffn2_gate', 'w_ffn2_up', 'w_ffn2_down')
                cur = 1 - cur
            for nm, rows in cfg.debug_out:
                self.dump(nm, self.scr[nm][:, :], self.scr_b[nm])
            self.phase_out(cur)
            K.barrier()
        return self.nc


def _consts():
    ident = np.eye(P, dtype=np.float32)
    cst = np.zeros((P, 8), np.float32)
    half = 32
    inv = (10000.0 ** (-np.arange(half, dtype=np.float32) / half)).astype(np.float32)
    cst[0:32, 0] = inv / (2 * np.pi)
    cst[32:64, 0] = inv / (2 * np.pi)
    cst[0:32, 1] = -1.0
    cst[32:64, 1] = 1.0
    s_ = np.arange(64)[:, None]
    t_ = np.arange(64)[None, :]
    su = (s_ < t_).astype(np.float32)
    ui = (s_ <= t_).astype(np.float32)
    blk = np.block([[np.zeros_like(su), ui], [su, ui]])
    msk = np.tile(blk, (1, 4)).astype(np.float32)
    msk2 = np.zeros((P, 1536), np.float32)
    sl = (t_ < s_).astype(np.float32)
    msk2[0:64, 0:512] = np.tile(sl, (1, 8))
    msk2[0:64, 512:1024] = np.tile(np.eye(64, dtype=np.float32), (1, 8))
    rs = np.ones(256, np.float32)
    rs[0::64] = 0.0
    msk2[:, 1024:1280] = rs[None, :]
    msk2[0:64, 1280:1536] = np.tile(su, (1, 4))
    z = np.zeros_like(su)
    bdu = np.block([[su, z], [z, su]])
    bdl = np.block([[sl, z], [z, sl]])
    msk3 = np.concatenate([np.tile(bdu, (1, 4)), np.tile(bdl, (1, 4)), np.tile(np.eye(128, dtype=np.float32), (1, 4))],
                          axis=1).astype(np.float32)
    return ident, cst, msk, msk2, msk3


_IDENT, _CST, _MSK, _MSK2, _MSK3 = _consts()


def make_in_map(cfg, inputs, core):
    NB, S = cfg.NB, cfg.S
    b0 = core * NB
    m = {
        "x": np.ascontiguousarray(np.asarray(inputs["x"])[b0:b0 + NB].reshape(NB * S, D)),
        "mem": np.ascontiguousarray(np.asarray(inputs["mem"])[b0:b0 + NB].reshape(NB * N_MEM, D)),
        "pos": np.ascontiguousarray(np.asarray(inputs["positions"])[b0:b0 + NB].reshape(NB * S)).astype(np.int32),
        "ident": _IDENT, "cst": _CST, "msk": _MSK, "msk2": _MSK2, "msk3": _MSK3,
    }
    for nm in W_SHAPES:
        if getattr(cfg, 'lite', False) and not any(nm in PHASE_W[p_] for p_ in cfg.phases):
            continue
        m[nm] = np.ascontiguousarray(np.asarray(inputs[nm])[0])
    for nm in V_SHAPES:
        m[nm] = np.ascontiguousarray(np.asarray(inputs[nm])[0].reshape(-1))
    return m


def kernel(**inputs):
    cfg = Cfg()
    prog = Prog(cfg)
    nc = prog.build()
    n_cores = 8
    in_maps = [make_in_map(cfg, inputs, c) for c in range(n_cores)]
    res = run_bass_kernel_spmd(nc, in_maps, core_ids=list(range(n_cores)))
    outs = [np.asarray(res.results[c]["out"]).reshape(cfg.NB, cfg.S, D) for c in range(n_cores)]
    return np.concatenate(outs, axis=0).astype(np.float32)
```

```python
import contextlib
import numpy as np
import concourse.bass as bass
import concourse.mybir as mybir
from concourse.bass_utils import run_bass_kernel_spmd

F32 = mybir.dt.float32
BF16 = mybir.dt.bfloat16
I32 = mybir.dt.int32
AF = mybir.ActivationFunctionType
ALU = mybir.AluOpType

D = 2048
D_FF = 5504
N_MEM = 256
EPS = 1e-6
P = 128
TT = 512


class Buf:
    __slots__ = ("name", "w", "r")

    def __init__(self, name=""):
        self.name = name
        self.w = None
        self.r = {}


class Eng:
    def __init__(self, K, name, h, sem, is_pe=False):
        self.K = K
        self.name = name
        self.h = h
        self.sem = sem
        self.sid = id(sem)
        self.n = 0
        self.seen = {}
        self.is_pe = is_pe
        K.sems[self.sid] = sem

    def wait_for(self, events):
        for sid, val in events:
            if sid == self.sid and self.is_pe:
                continue
            if self.seen.get(sid, 0) >= val:
                continue
            self.h.wait_ge(self.K.sems[sid], val)
            self.seen[sid] = val


class Kern:
    def __init__(self, nc, n_dma_sems=40):
        self.nc = nc
        self.sems = {}
        self.es = contextlib.ExitStack()
        mk = lambda nm: self.es.enter_context(nc.semaphore(nm))
        self.PE = Eng(self, "pe", nc.tensor, mk("s_pe"), is_pe=True)
        self.ACT = Eng(self, "act", nc.scalar, mk("s_act"))
        self.DVE = Eng(self, "dve", nc.vector, mk("s_dve"))
        self.POOL = Eng(self, "pool", nc.gpsimd, mk("s_pool"))
        self.SP = Eng(self, "sp", nc.sync, mk("s_sp"))
        self.engs = [self.PE, self.ACT, self.DVE, self.POOL, self.SP]
        self.dsems = []
        self.qpool = {}
        for e, cnt in ((self.SP, 28), (self.POOL, 12), (self.ACT, 8)):
            pool = []
            for i in range(cnt):
                s = mk("s_dma_%s%d" % (e.name, i))
                self.sems[id(s)] = s
                slot = [s, 0]
                pool.append(slot)
                self.dsems.append(slot)
            self.qpool[e.sid] = [pool, 0]

    @staticmethod
    def _deps(reads, writes):
        ev = set()
        for b in reads:
            if b.w is not None:
                ev.add(b.w)
        for b in writes:
            if b.w is not None:
                ev.add(b.w)
            ev.update(b.r.items())
        return ev

    def op(self, eng, reads, writes, fn):
        eng.wait_for(self._deps(reads, writes))
        ins = fn(eng.h)
        eng.n += 1
        ins.then_inc(eng.sem, 1)
        me = (eng.sid, eng.n)
        for b in reads:
            b.r[eng.sid] = eng.n
        for b in writes:
            b.w = me
            b.r = {}
        return ins

    def dma(self, q, out, in_, reads, writes, **kw):
        qp = self.qpool[q.sid]
        slot = qp[0][qp[1]]
        qp[1] = (qp[1] + 1) % len(qp[0])
        sem, prev = slot
        ev = self._deps(reads, writes)
        if prev > 0:
            ev.add((id(sem), prev))
        q.wait_for(ev)
        q.h.dma_start(out=out, in_=in_, **kw).then_inc(sem, 16)
        slot[1] = prev + 16
        me = (id(sem), prev + 16)
        for b in reads:
            b.r[me[0]] = me[1]
        for b in writes:
            b.w = me
            b.r = {}

    def barrier(self):
        ev = [(e.sid, e.n) for e in self.engs if e.n > 0]
        ev += [(id(s), v) for s, v in self.dsems if v > 0]
        for e in self.engs:
            e.wait_for([x for x in ev if x[0] != e.sid])

    def close(self):
        self.es.close()


class Cfg:
    def __init__(self, S=2048, NB=2, phases=("ffn1", "proj", "mla", "rwkv", "merge", "xattn", "ffn2"),
                 debug_out=()):
        self.S = S
        self.NB = NB
        self.NT = S * NB
        self.phases = phases
        self.debug_out = debug_out


Q_LORA, KV_LORA, QK_ROPE, QK_NOPE, V_HEAD, MLA_H = 512, 256, 64, 128, 128, 8
RW, RW_H, RW_N = 1024, 16, 64
RW_COLS = 3328
MLA_COLS = 832
GATE0 = MLA_COLS + RW_COLS
IN_COLS = 8256
MEM_H, MEM_D = 4, 256

W_SHAPES = {
    'w_ffn1_gate': (D, D_FF), 'w_ffn1_up': (D, D_FF), 'w_ffn1_down': (D_FF, D),
    'w_in': (D, IN_COLS), 'w_uq': (Q_LORA, 1536), 'w_ukv': (KV_LORA, 2048), 'w_oa': (1024, D),
    'w_w2': (64, RW), 'w_a2': (64, RW), 'w_g2': (128, RW), 'w_ob': (RW, D), 'w_o': (D, D),
    'w_cq': (D, 1024), 'w_ckv': (D, 2048), 'w_co': (1024, D),
    'w_ffn2_gate': (D, D_FF), 'w_ffn2_up': (D, D_FF), 'w_ffn2_down': (D_FF, D),
}
V_SHAPES = {
    'n_ffn1_pre': D, 'n_ffn1_post': D, 'n_mix_pre': D, 'n_mix_post': D, 'b_gate': 4096,
    'n_q_lat': Q_LORA, 'n_kv_lat': KV_LORA, 'mu_shift': RW_COLS, 'w0': RW, 'a0': RW, 'k_k': RW, 'k_a': RW,
    'r_k': RW, 'lnx_w': RW, 'lnx_b': RW, 'n_x_pre': D, 'n_x_post': D, 'n_mem': D,
    'n_ffn2_pre': D, 'n_ffn2_post': D,
}
PHASE_W = {
    "ffn1": ['w_ffn1_gate', 'w_ffn1_up', 'w_ffn1_down'],
    "proj": ['w_in'], "mla": ['w_uq', 'w_ukv', 'w_oa'],
    "rwkv": ['w_w2', 'w_a2', 'w_g2', 'w_ob'], "merge": ['w_in', 'w_o'],
    "xattn": ['w_cq', 'w_ckv', 'w_co'],
    "ffn2": ['w_ffn2_gate', 'w_ffn2_up', 'w_ffn2_down'],
}


def _largest_div(n, cap=2048):
    for c in range(cap, 0, -1):
        if n % c == 0:
            return c


class Ring:
    def __init__(self, tiles):
        self.t = tiles
        self.b = [Buf() for _ in tiles]
        self.i = 0

    def next(self):
        k = self.i % len(self.t)
        self.i += 1
        return self.t[k], self.b[k]


class Prog:
    def __init__(self, cfg):
        self.cfg = cfg
        nc = self.nc = bass.Bass("TRN2", target_bir_lowering=False)
        self.K = Kern(nc)
        NT = cfg.NT
        dt = nc.dram_tensor
        self.x_in = dt("x", [NT, D], F32, kind="ExternalInput").ap()
        self.mem_in = dt("mem", [cfg.NB * N_MEM, D], F32, kind="ExternalInput").ap()
        self.pos_in = dt("pos", [NT], I32, kind="ExternalInput").ap()
        self.ident_in = dt("ident", [P, P], F32, kind="ExternalInput").ap()
        self.cst_in = dt("cst", [P, 8], F32, kind="ExternalInput").ap()
        self.msk_in = dt("msk", [P, 512], F32, kind="ExternalInput").ap()
        self.msk2_in = dt("msk2", [P, 1536], F32, kind="ExternalInput").ap()
        self.msk3_in = dt("msk3", [P, 1536], F32, kind="ExternalInput").ap()
        self.out = dt("out", [NT, D], F32, kind="ExternalOutput").ap()
        self.w, self.wb, self.wbuf = {}, {}, {}
        self.wnames = [nm for nm in W_SHAPES if (not getattr(cfg, 'lite', False)) or any(nm in PHASE_W[p_] for p_ in cfg.phases)]
        for nm in W_SHAPES:
            k, n = W_SHAPES[nm]
            if nm in self.wnames:
                self.w[nm] = dt(nm, [k, n], F32, kind="ExternalInput").ap()
            self.wb[nm] = dt(nm + "_b", [k, n], BF16, kind="Internal").ap()
            self.wbuf[nm] = [Buf(nm) for _ in range(4)]
        self.v = {}
        for nm, n in V_SHAPES.items():
            self.v[nm] = dt(nm, [n], F32, kind="ExternalInput").ap()
        self.xT = [dt("xT%d" % i, [D, NT], F32, kind="Internal").ap() for i in range(2)]
        self.xT_buf = [[Buf() for t in range(NT // TT)] for i in range(2)]
        self.scr, self.scr_b = {}, {}
        for nm, rows in (("cq", Q_LORA), ("ckv", KV_LORA), ("kr", 64), ("cs", 64), ("sn", 64),
                         ("rw", RW_COLS), ("ya", D), ("yb", D)):
            kind = "ExternalInput" if (nm == "rw" and getattr(cfg, 'rw_input', False)) else "Internal"
            self.scr[nm] = dt("scr_" + nm, [rows, NT], F32, kind=kind).ap()
            self.scr_b[nm] = [Buf() for t in range(NT // 256)]
        self.dbg = {}
        for nm, rows in cfg.debug_out:
            self.dbg[nm] = dt("dbg_" + nm, [rows, NT], F32, kind="ExternalOutput").ap()

    def sbufs(self, nm, t0, n):
        return self.scr_b[nm][t0 // 256:(t0 + n + 255) // 256]

    def sb(self, es, name, shape, dtype):
        return es.enter_context(self.nc.sbuf_tensor("sb_" + name, shape, dtype))

    def ps(self, es, name, shape, dtype):
        return es.enter_context(self.nc.psum_tensor("ps_" + name, shape, dtype))

    def sring(self, es, name, n, shape, dtype):
        return Ring([self.sb(es, "%s%d" % (name, i), shape, dtype) for i in range(n)])

    def pring(self, es, name, n, shape=None, dtype=F32):
        return Ring([self.ps(es, "%s%d" % (name, i), shape or [P, 512], dtype) for i in range(n)])

    def mm(self, out, lhsT, rhs, start, stop, reads, writes):
        return self.K.op(self.K.PE, reads, writes,
                         lambda e: e.matmul(out, lhsT=lhsT, rhs=rhs, start=start, stop=stop))

    def tp(self, out, in_, ident, reads, writes):
        return self.K.op(self.K.PE, reads, writes, lambda e: e.transpose(out=out, in_=in_, identity=ident))

    def act(self, out, in_, func, reads, writes, scale=1.0, bias=None):
        if bias is None:
            return self.K.op(self.K.ACT, reads, writes,
                             lambda e: e.activation(out=out, in_=in_, func=func, scale=scale))
        return self.K.op(self.K.ACT, reads, writes,
                         lambda e: e.activation(out=out, in_=in_, func=func, scale=scale, bias=bias))

    def cp(self, eng, out, in_, reads, writes):
        if eng is self.K.ACT:
            return self.K.op(eng, reads, writes, lambda e: e.copy(out=out, in_=in_))
        return self.K.op(eng, reads, writes, lambda e: e.tensor_copy(out=out, in_=in_))

    def tt(self, eng, out, in0, in1, op, reads, writes):
        return self.K.op(eng, reads, writes, lambda e: e.tensor_tensor(out=out, in0=in0, in1=in1, op=op))

    def ts(self, eng, out, in0, s1, s2, op0, op1, reads, writes):
        if op1 is None:
            return self.K.op(eng, reads, writes,
                             lambda e: e.tensor_scalar(out=out, in0=in0, scalar1=s1, scalar2=None, op0=op0))
        return self.K.op(eng, reads, writes,
                         lambda e: e.tensor_scalar(out=out, in0=in0, scalar1=s1, scalar2=s2, op0=op0, op1=op1))

    def stt(self, out, in0, scalar, in1, op0, op1, reads, writes):
        return self.K.op(self.K.DVE, reads, writes, lambda e: e.scalar_tensor_tensor(
            out=out, in0=in0, scalar=scalar, in1=in1, op0=op0, op1=op1))

    def mset(self, eng, ap, val, reads, writes):
        return self.K.op(eng, reads, writes, lambda e: e.memset(ap, val))

    def recip(self, out, in_, reads, writes):
        return self.K.op(self.K.DVE, reads, writes, lambda e: e.reciprocal(out=out, in_=in_))

    def cast_weights(self, names):
        K = self.K
        done = set()
        for nm in names:
            if nm in done:
                continue
            done.add(nm)
            k, n = W_SHAPES[nm]
            c = _largest_div(n)
            src = self.w[nm].rearrange("k (a c) -> (k a) c", c=c)
            dst = self.wb[nm].rearrange("k (a c) -> (k a) c", c=c)
            rows = k * (n // c)
            nblk = 4 if rows >= 2048 else 1
            rb = rows // nblk
            for i in range(nblk):
                K.dma(K.POOL, dst[i * rb:(i + 1) * rb, :], src[i * rb:(i + 1) * rb, :],
                      reads=[], writes=[self.wbuf[nm][i]])

    def setup_consts(self, es):
        K, nc = self.K, self.nc
        self.ident = self.sb(es, "ident", [P, P], F32)
        self.ident_b = Buf("ident")
        K.dma(K.SP, self.ident[:], self.ident_in[:, :], reads=[], writes=[self.ident_b])
        self.ident_bf = self.sb(es, "ident_bf", [P, P], BF16)
        self.identbf_b = Buf()
        self.cp(K.DVE, self.ident_bf[:], self.ident[:], [self.ident_b], [self.identbf_b])
        self.cst = self.sb(es, "cst", [P, 8], F32)
        self.cst_b = Buf()
        K.dma(K.SP, self.cst[:], self.cst_in[:, :], reads=[], writes=[self.cst_b])
        self.msk = self.sb(es, "msk", [P, 512], F32)
        self.msk_b = Buf()
        K.dma(K.SP, self.msk[:], self.msk_in[:, :], reads=[], writes=[self.msk_b])
        self.ones_f32 = self.sb(es, "ones_f32", [P, P], F32)
        self.onesf_b = Buf()
        self.mset(K.DVE, self.ones_f32[:], 1.0, [], [self.onesf_b])
        self.ones_bf = self.sb(es, "ones_bf", [P, P], BF16)
        self.ones_b = Buf("ones")
        self.mset(K.DVE, self.ones_bf[:], 1.0, [], [self.ones_b])
        self.bones_bf = self.sb(es, "bones_bf", [P, P], BF16)
        self.bones_b = Buf()
        self.mset(K.DVE, self.bones_bf[:], 0.0, [], [self.bones_b])
        self.mset(K.DVE, self.bones_bf[0:64, 0:64], 1.0, [], [self.bones_b])
        self.mset(K.DVE, self.bones_bf[64:128, 64:128], 1.0, [], [self.bones_b])
        self.eps_t = self.sb(es, "eps_t", [P, 1], F32)
        self.eps_b = Buf("eps")
        self.mset(K.DVE, self.eps_t[:], EPS, [], [self.eps_b])
        self.vec, self.vec_b = {}, {}
        for nm, n in V_SHAPES.items():
            t = self.sb(es, "v_" + nm, [P, n // P], F32)
            b = Buf(nm)
            with nc.allow_non_contiguous_dma(reason="tiny per-feature vector load"):
                K.dma(K.SP, t[:], self.v[nm].rearrange("(c p) -> p c", p=P), reads=[], writes=[b])
            self.vec[nm] = t
            self.vec_b[nm] = b

    def phase_in(self):
        K, nc, cfg = self.K, self.nc, self.cfg
        with contextlib.ExitStack() as es:
            NBUF = 2
            xin = [self.sb(es, "pin_x%d" % i, [P, D], F32) for i in range(NBUF)]
            xin_b = [Buf() for _ in range(NBUF)]
            xo = [self.sb(es, "pin_o%d" % i, [P, 16, TT], F32) for i in range(2)]
            xo_b = [Buf() for _ in range(2)]
            pst = [self.ps(es, "pin_ps%d" % i, [P, 512], F32) for i in range(4)]
            pst_b = [Buf() for _ in range(4)]
            nsub = TT // P
            blk = 0
            pidx = 0
            for t in range(cfg.NT // TT):
                o, ob = xo[t % 2], xo_b[t % 2]
                for s in range(nsub):
                    xi, xib = xin[blk % NBUF], xin_b[blk % NBUF]
                    r0 = t * TT + s * P
                    K.dma(K.SP, xi[:], self.x_in[r0:r0 + P, :], reads=[], writes=[xib])
                    for g in range(4):
                        pt, ptb = pst[pidx % 4], pst_b[pidx % 4]
                        pidx += 1
                        for j in range(4):
                            c = g * 4 + j
                            K.op(K.PE, [xib, self.ident_b], [ptb],
                                 lambda e, c=c, j=j, pt=pt, xi=xi: e.transpose(
                                     out=pt[:, j * P:(j + 1) * P], in_=xi[:, c * P:(c + 1) * P],
                                     identity=self.ident[:]))
                        eng = K.ACT if (g % 2 == 0) else K.DVE
                        if eng is K.ACT:
                            K.op(eng, [ptb], [ob], lambda e, g=g, pt=pt, o=o, s=s: e.copy(
                                out=o[:, g * 4:(g + 1) * 4, s * P:(s + 1) * P],
                                in_=pt[:].rearrange("p (j q) -> p j q", j=4)))
                        else:
                            K.op(eng, [ptb], [ob], lambda e, g=g, pt=pt, o=o, s=s: e.tensor_copy(
                                out=o[:, g * 4:(g + 1) * 4, s * P:(s + 1) * P],
                                in_=pt[:].rearrange("p (j q) -> p j q", j=4)))
                    blk += 1
                K.dma(K.SP, self.xT[0].rearrange("(c p) t -> p c t", p=P)[:, :, t * TT:(t + 1) * TT],
                      o[:], reads=[ob], writes=[self.xT_buf[0][t]])
        K.barrier()

    def phase_out(self, src):
        K, nc, cfg = self.K, self.nc, self.cfg
        with contextlib.ExitStack() as es:
            xi_t = [self.sb(es, "pout_x%d" % i, [P, 16, TT], F32) for i in range(2)]
            xi_b = [Buf() for _ in range(2)]
            xo = [self.sb(es, "pout_o%d" % i, [P, D], F32) for i in range(2)]
            xo_b = [Buf() for _ in range(2)]
            pst = [self.ps(es, "pout_ps%d" % i, [P, 512], F32) for i in range(4)]
            pst_b = [Buf() for _ in range(4)]
            self.out_bufs = []
            nsub = TT // P
            blk = 0
            pidx = 0
            for t in range(cfg.NT // TT):
                xi, xib = xi_t[t % 2], xi_b[t % 2]
                K.dma(K.SP, xi[:], self.xT[src].rearrange("(c p) t -> p c t", p=P)[:, :, t * TT:(t + 1) * TT],
                      reads=[self.xT_buf[src][t]], writes=[xib])
                for s in range(nsub):
                    o, ob = xo[blk % 2], xo_b[blk % 2]
                    for g in range(4):
                        pt, ptb = pst[pidx % 4], pst_b[pidx % 4]
                        pidx += 1
                        for j in range(4):
                            c = g * 4 + j
                            K.op(K.PE, [xib, self.ident_b], [ptb],
                                 lambda e, c=c, j=j, pt=pt, xi=xi, s=s: e.transpose(
                                     out=pt[:, j * P:(j + 1) * P], in_=xi[:, c, s * P:(s + 1) * P],
                                     identity=self.ident[:]))
                        if g % 2 == 0:
                            K.op(K.ACT, [ptb], [ob], lambda e, g=g, pt=pt, o=o: e.copy(
                                out=o[:, g * 512:(g + 1) * 512], in_=pt[:]))
                        else:
                            K.op(K.DVE, [ptb], [ob], lambda e, g=g, pt=pt, o=o: e.tensor_copy(
                                out=o[:, g * 512:(g + 1) * 512], in_=pt[:]))
                    r0 = t * TT + s * P
                    fin = Buf("out")
                    K.dma(K.SP, self.out[r0:r0 + P, :], o[:], reads=[ob], writes=[fin])
                    self.out_bufs.append(fin)
                    blk += 1
        K.barrier()

    def rms_rstd(self, src, src_b, nchunks, width, dfeat, R, rows=P):
        K = self.K
        pstat, pstat_b = R["pstat"]
        for c in range(nchunks):
            q, qb = R["sq"].next()
            self.act(q[:, :width], src(c), AF.Square, [src_b(c)], [qb])
            self.mm(pstat[:, :width], self.ones_bf[:], q[:, :width], c == 0, c == nchunks - 1,
                    [qb, self.ones_b], [pstat_b])
        tmp, tmp_b = R["tmp"]
        rstd, rstd_b = R["rstd"]
        self.act(tmp[:, :width], pstat[:, :width], AF.Sqrt, [pstat_b, self.eps_b], [tmp_b],
                 scale=1.0 / dfeat, bias=self.eps_t[:])
        self.recip(rstd[:, :width], tmp[:, :width], [tmp_b], [rstd_b])

    def norm_res(self, es, tag, width=TT):
        R = {}
        R["pstat"] = (self.ps(es, tag + "pstat", [P, 512], F32), Buf())
        R["sq"] = self.sring(es, tag + "sq", 2, [P, width], BF16)
        R["tmp"] = (self.sb(es, tag + "tmp", [P, width], F32), Buf())
        R["rstd"] = (self.sb(es, tag + "rstd", [P, width], F32), Buf())
        return R

    def wslab(self, ring, wname, r0, nkc, c0, w, q=None):
        K = self.K
        t, b = ring.next()
        view = self.wb[wname][r0:r0 + nkc * P, :].rearrange("(c p) n -> p c n", p=P)
        for ca in range(0, nkc, 16):
            cb = min(nkc, ca + 16)
            K.dma(q or K.SP, t[:, ca:cb, :w], view[:, ca:cb, c0:c0 + w], reads=self.wbuf[wname], writes=[b])
        return t, b

    def tile_phase(self, tag, src, dst, n_pre, n_post, half, setup, body, post=True):
        K, nc, cfg = self.K, self.nc, self.cfg
        KC = D // P
        with contextlib.ExitStack() as es:
            ctx = setup(es)
            xt = self.sb(es, tag + "xt", [P, KC, TT], F32)
            xt_b = [Buf() for _ in range(KC)]
            hTs = [self.sb(es, tag + "hT%d" % i, [P, KC, TT], BF16) for i in range(2)]
            hT_bs = [Buf(), Buf()]
            pstat, pstat_b = self.ps(es, tag + "pstat", [P, 512], F32), Buf()
            sq = self.sring(es, tag + "sq", 2, [P, TT], F32)
            xs = self.sring(es, tag + "xs", 3, [P, TT], F32)
            acc = {k: (self.sb(es, tag + "acc" + k, [P, TT], F32), Buf()) for k in "PE"}
            tmp = {k: (self.sb(es, tag + "tmp" + k, [P, TT], F32), Buf()) for k in "PE"}
            rst = {k: (self.sb(es, tag + "rst" + k, [P, TT], F32), Buf()) for k in "PE"}
            gph = self.sb(es, tag + "gph", [P, KC], F32)
            gph_b = Buf()
            gpre, gpre_b = self.vec[n_pre], self.vec_b[n_pre]
            if post:
                gpost, gpost_b = self.vec[n_post], self.vec_b[n_post]
                self.ts(K.DVE, gph[:], gpost[:], 0.5 if half else 1.0, None, ALU.mult, None, [gpost_b], [gph_b])
                dstT = self.xT[dst].rearrange("(c p) t -> p c t", p=P)
            srcT = self.xT[src].rearrange("(c p) t -> p c t", p=P)
            ntiles = cfg.NT // TT

            def sumsq(k, chunk_ap, chunk_b):
                a, ab = acc[k]
                for c in range(KC):
                    ap_, b_ = chunk_ap(c), chunk_b(c)
                    q, qb = sq.next()
                    self.act(q[:], ap_, AF.Square, [b_], [qb])
                    if c == 0:
                        self.cp(K.DVE, a[:], q[:], [qb], [ab])
                    else:
                        self.tt(K.DVE, a[:], a[:], q[:], ALU.add, [ab, qb], [ab])

            def rstd_of(k):
                a, ab = acc[k]
                tm, tmb = tmp[k]
                rs, rsb = rst[k]
                self.mm(pstat[:], self.ones_f32[:], a[:], True, True, [ab, self.onesf_b], [pstat_b])
                self.act(tm[:], pstat[:], AF.Sqrt, [pstat_b, self.eps_b], [tmb], scale=1.0 / D, bias=self.eps_t[:])
                self.recip(rs[:], tm[:], [tmb], [rsb])
                return rs, rsb

            def load_chunk(t, c):
                r, rb = xs.next()
                K.dma(K.SP, r[:], srcT[:, c, t * TT:(t + 1) * TT], reads=[self.xT_buf[src][t]], writes=[rb])
                return r, rb

            def P_a(t):
                loaded = {}

                def ap_(c):
                    loaded[c] = load_chunk(t, c)
                    return loaded[c][0][:]
                sumsq("P", ap_, lambda c: loaded[c][1])

            def P_b(t):
                rs, rsb = rstd_of("P")
                hT, hT_b = hTs[t % 2], hT_bs[t % 2]
                for c in range(KC):
                    r, rb = load_chunk(t, c)
                    self.stt(hT[:, c, :], r[:], gpre[:, c:c + 1], rs[:], ALU.mult, ALU.mult,
                             [rb, gpre_b, rsb], [hT_b])

            def E_a(t):
                sumsq("E", lambda c: xt[:, c, :], lambda c: xt_b[c])

            def E_b(t):
                rs, rsb = rstd_of("E")
                for c in range(KC):
                    r, rb = load_chunk(t, c)
                    self.stt(xt[:, c, :], xt[:, c, :], gph[:, c:c + 1], rs[:], ALU.mult, ALU.mult,
                             [xt_b[c], gph_b, rsb], [xt_b[c]])
                    self.tt(K.DVE, xt[:, c, :], xt[:, c, :], r[:], ALU.add, [xt_b[c], rb], [xt_b[c]])
                K.dma(K.ACT, dstT[:, :, t * TT:(t + 1) * TT], xt[:], reads=xt_b, writes=[self.xT_buf[dst][t]])

            def emit(n, y_ps, y_b):
                self.cp(K.DVE, xt[:, n, :], y_ps, [y_b], [xt_b[n]])

            P_a(0)
            P_b(0)
            for t in range(ntiles):
                called = []

                def early(t=t):
                    called.append("e")
                    if post and t > 0:
                        E_b(t - 1)

                def mid(t=t):
                    called.append("m")
                    if t + 1 < ntiles:
                        P_a(t + 1)

                def late(t=t):
                    called.append("l")
                    if t + 1 < ntiles:
                        P_b(t + 1)

                body(ctx, t, t * TT, hTs[t % 2], hT_bs[t % 2], emit, early, mid, late)
                assert called == ["e", "m", "l"], called
                if post:
                    E_a(t)
            if post:
                E_b(ntiles - 1)
        K.barrier()

    def phase_ffn(self, tag, src, dst, n_pre, n_post, wg, wu, wd):
        K = self.K
        KC, FC, SW = D // P, D_FF // P, 256

        def setup(es):
            c = {}
            c["it"] = (self.sb(es, tag + "it", [P, FC, TT], BF16), Buf())
            c["wg"] = self.sring(es, tag + "wg", 2, [P, KC, SW], BF16)
            c["wu"] = self.sring(es, tag + "wu", 2, [P, KC, SW], BF16)
            c["wd"] = self.sring(es, tag + "wd", 3, [P, FC, P], BF16)
            c["sg"] = self.sring(es, tag + "sg", 2, [P, TT], BF16)
            c["pg"] = self.pring(es, tag + "pg", 2)
            c["pu"] = self.pring(es, tag + "pu", 2)
            c["py"] = self.pring(es, tag + "py", 3)
            return c

        def body(c, t, t0, hT, hT_b, emit, early, mid, late):
            it, it_b = c["it"]
            for s in range((D_FF + SW - 1) // SW):
                c0 = s * SW
                w = min(SW, D_FF - c0)
                a, ab = self.wslab(c["wg"], wg, 0, KC, c0, w)
                u, ub = self.wslab(c["wu"], wu, 0, KC, c0, w)
                for j in range(w // P):
                    n = c0 // P + j
                    g_ps, g_b = c["pg"].next()
                    u_ps, u_b = c["pu"].next()
                    sgt, sgb = c["sg"].next()
                    for kc in range(KC):
                        self.mm(g_ps[:], a[:, kc, j * P:(j + 1) * P], hT[:, kc, :], kc == 0, kc == KC - 1,
                                [ab, hT_b], [g_b])
                    for kc in range(KC):
                        self.mm(u_ps[:], u[:, kc, j * P:(j + 1) * P], hT[:, kc, :], kc == 0, kc == KC - 1,
                                [ub, hT_b], [u_b])
                    self.act(sgt[:], g_ps[:], AF.Silu, [g_b], [sgb])
                    self.tt(K.DVE, it[:, n, :], sgt[:], u_ps[:], ALU.mult, [sgb, u_b], [it_b])
                if s == 1:
                    early()
            mid()
            for n in range(D // P):
                dw, dwb = self.wslab(c["wd"], wd, 0, FC, n * P, P)
                y_ps, y_b = c["py"].next()
                for kc in range(FC):
                    self.mm(y_ps[:], dw[:, kc, :], it[:, kc, :], kc == 0, kc == FC - 1, [dwb, it_b], [y_b])
                emit(n, y_ps[:], y_b)
                if n == 2:
                    late()

        self.tile_phase(tag, src, dst, n_pre, n_post, True, setup, body)

    def rope_tables(self, es, t0, cs, cs_b, sn, sn_b, W):
        K = self.K
        pos_i, pos_f, tq, ti, tf, m = (W[k] for k in ("pos_i", "pos_f", "tq", "ti", "tf", "m"))
        wb = W["b"]
        K.dma(K.SP, pos_i[:], self.pos_in[t0:t0 + TT].partition_broadcast(64), reads=[], writes=[wb])
        self.cp(K.DVE, pos_f[:], pos_i[:], [wb], [wb])
        for which, out, out_b in ((0, sn, sn_b), (1, cs, cs_b)):
            self.ts(K.DVE, tq[:], pos_f[:], self.cst[0:64, 0:1], 0.25 * which, ALU.mult, ALU.add,
                    [wb, self.cst_b], [wb])
            self.cp(K.DVE, ti[:], tq[:], [wb], [wb])
            self.cp(K.DVE, tf[:], ti[:], [wb], [wb])
            self.tt(K.DVE, tq[:], tq[:], tf[:], ALU.subtract, [wb], [wb])
            self.ts(K.DVE, m[:], tq[:], 0.5, None, ALU.is_gt, None, [wb], [wb])
            self.tt(K.DVE, tq[:], tq[:], m[:], ALU.subtract, [wb], [wb])
            self.ts(K.DVE, m[:], tq[:], -0.5, None, ALU.is_lt, None, [wb], [wb])
            self.tt(K.DVE, tq[:], tq[:], m[:], ALU.add, [wb], [wb])
            self.act(out[:], tq[:], AF.Sin, [wb], [out_b], scale=2.0 * np.pi * (1.0 - 1e-6))
        self.ts(K.DVE, sn[:], sn[:], self.cst[0:64, 1:2], None, ALU.mult, None, [sn_b, self.cst_b], [sn_b])

    def rope_work(self, es, tag):
        W = {"b": Buf()}
        W["pos_i"] = self.sb(es, tag + "pos_i", [64, TT], I32)
        W["ti"] = self.sb(es, tag + "ti", [64, TT], I32)
        for k in ("pos_f", "tq", "tf", "m"):
            W[k] = self.sb(es, tag + k, [64, TT], F32)
        return W

    def phase_proj(self, src):
        K = self.K
        KC, SW = D // P, 256
        tag = "pj"
        segs = [("cq", 0, Q_LORA), ("ckv", Q_LORA, KV_LORA), ("rw", MLA_COLS, RW_COLS)]

        def setup(es):
            c = {}
            c["w"] = self.sring(es, tag + "w", 3, [P, KC, SW], BF16)
            c["wr"] = (self.sb(es, tag + "wr", [P, KC, 64], BF16), Buf())
            c["wrs"] = (self.sb(es, tag + "wrs", [P, KC, 64], BF16), Buf())
            c["pp"] = self.pring(es, tag + "pp", 4)
            c["st"] = self.sring(es, tag + "st", 4, [P, TT], F32)
            c["cs"] = (self.sb(es, tag + "cs", [64, TT], F32), Buf())
            c["sn"] = (self.sb(es, tag + "sn", [64, TT], F32), Buf())
            c["rt"] = self.sring(es, tag + "rt", 2, [64, TT], F32)
            c["W"] = self.rope_work(es, tag)
            wr, wrb = c["wr"]
            wrs, wrsb = c["wrs"]
            view = self.wb['w_in'].rearrange("(c p) n -> p c n", p=P)
            kr0 = Q_LORA + KV_LORA
            with self.nc.allow_non_contiguous_dma(reason="64-col rope weight slab"):
                K.dma(K.SP, wr[:], view[:, :, kr0:kr0 + 64], reads=self.wbuf['w_in'], writes=[wrb])
                K.dma(K.SP, wrs[:, :, 0:32], view[:, :, kr0 + 32:kr0 + 64], reads=self.wbuf['w_in'], writes=[wrsb])
                K.dma(K.SP, wrs[:, :, 32:64], view[:, :, kr0:kr0 + 32], reads=self.wbuf['w_in'], writes=[wrsb])
            return c

        def body(c, t, t0, hT, hT_b, emit, early, mid, late):
            cs, cs_b = c["cs"]
            sn, sn_b = c["sn"]
            self.rope_tables(None, t0, cs, cs_b, sn, sn_b, c["W"])
            K.dma(K.ACT, self.scr["cs"][:, t0:t0 + TT], cs[:], reads=[cs_b], writes=self.sbufs("cs", t0, TT))
            K.dma(K.ACT, self.scr["sn"][:, t0:t0 + TT], sn[:], reads=[sn_b], writes=self.sbufs("sn", t0, TT))
            wr, wrb = c["wr"]
            wrs, wrsb = c["wrs"]
            p1, p1b = c["pp"].next()
            p2, p2b = c["pp"].next()
            for kc in range(KC):
                self.mm(p1[0:64, :], wr[:, kc, :], hT[:, kc, :], kc == 0, kc == KC - 1, [wrb, hT_b], [p1b])
            for kc in range(KC):
                self.mm(p2[0:64, :], wrs[:, kc, :], hT[:, kc, :], kc == 0, kc == KC - 1, [wrsb, hT_b], [p2b])
            r1, r1b = c["rt"].next()
            r2, r2b = c["rt"].next()
            self.tt(K.DVE, r1[:], p1[0:64, :], cs[:], ALU.mult, [p1b, cs_b], [r1b])
            self.tt(K.DVE, r2[:], p2[0:64, :], sn[:], ALU.mult, [p2b, sn_b], [r2b])
            self.tt(K.POOL, r1[:], r1[:], r2[:], ALU.add, [r1b, r2b], [r1b])
            K.dma(K.ACT, self.scr["kr"][:, t0:t0 + TT], r1[:], reads=[r1b], writes=self.sbufs("kr", t0, TT))
            early()
            for nm, col0, rows in segs:
                for s in range(rows // SW):
                    if nm == "rw" and s == 5:
                        mid()
                    if nm == "rw" and s == 8:
                        late()
                    wt, wtb = self.wslab(c["w"], 'w_in', 0, KC, col0 + s * SW, SW)
                    for j in range(SW // P):
                        n = s * (SW // P) + j
                        ps, psb = c["pp"].next()
                        for kc in range(KC):
                            self.mm(ps[:], wt[:, kc, j * P:(j + 1) * P], hT[:, kc, :], kc == 0, kc == KC - 1,
                                    [wtb, hT_b], [psb])
                        st, stb = c["st"].next()
                        self.cp(K.ACT if n % 2 == 0 else K.DVE, st[:], ps[:], [psb], [stb])
                        K.dma(K.ACT, self.scr[nm][n * P:(n + 1) * P, t0:t0 + TT], st[:], reads=[stb],
                              writes=self.sbufs(nm, t0, TT))

        self.tile_phase(tag, src, None, 'n_mix_pre', None, False, setup, body, post=False)

    def phase_mla(self):
        K, cfg = self.K, self.cfg
        S, NB = cfg.S, cfg.NB
        tag = "ml"
        NKB = S // P
        scale = float((QK_NOPE + QK_ROPE) ** -0.5)
        with contextlib.ExitStack() as es:
            wuq = self.sb(es, tag + "wuq", [P, 4, 1536], BF16)
            wuqs = self.sb(es, tag + "wuqs", [P, 4, 8, 64], BF16)
            wukv = self.sb(es, tag + "wukv", [P, 2, 2048], BF16)
            woa = self.sb(es, tag + "woa", [P, 8, D], BF16)
            wuq_b, wuqs_b, wukv_b, woa_b = Buf(), Buf(), Buf(), Buf()
            vq = self.wb['w_uq'].rearrange("(c p) n -> p c n", p=P)
            K.dma(K.SP, wuq[:], vq, reads=self.wbuf['w_uq'], writes=[wuq_b])
            vq4 = self.wb['w_uq'].rearrange("(c p) (h d) -> p c h d", p=P, h=8)
            with self.nc.allow_non_contiguous_dma(reason="rope weight half-swap (64B runs)"):
                for kc in range(4):
                    K.dma(K.SP, wuqs[:, kc, :, 0:32], vq4[:, kc, :, 160:192], reads=self.wbuf['w_uq'], writes=[wuqs_b])
                    K.dma(K.SP, wuqs[:, kc, :, 32:64], vq4[:, kc, :, 128:160], reads=self.wbuf['w_uq'], writes=[wuqs_b])
            K.dma(K.SP, wukv[:], self.wb['w_ukv'].rearrange("(c p) n -> p c n", p=P),
                  reads=self.wbuf['w_ukv'], writes=[wukv_b])
            K.dma(K.SP, woa[:], self.wb['w_oa'].rearrange("(c p) n -> p c n", p=P),
                  reads=self.wbuf['w_oa'], writes=[woa_b])
            wukv4 = wukv[:].rearrange("p c (h two d) -> p c h two d", h=8, two=2)

            Kn = self.sb(es, tag + "Kn", [P, 8, S], BF16)
            Kn_b = [Buf() for _ in range(S // TT)]
            Vs = self.sb(es, tag + "Vs", [P, NKB, 1024], BF16)
            Vs_b = [Buf() for _ in range(NKB)]
            Kr = self.sb(es, tag + "Kr", [64, S], BF16)
            Kr_b = [Buf() for _ in range(S // TT)]
            pt = self.sb(es, tag + "pt", [P, NKB, TT], BF16)
            pt_b = [Buf() for _ in range(NKB)]
            lat = self.sb(es, tag + "lat", [P, 4, TT], F32)
            lat_b = [Buf() for _ in range(4)]
            latn = self.sb(es, tag + "latn", [P, 4, TT], BF16)
            latn_b = Buf()
            krl = self.sb(es, tag + "krl", [64, TT], F32)
            krl_b = Buf()
            cs = self.sb(es, tag + "cs", [64, TT], F32)
            sn = self.sb(es, tag + "sn", [64, TT], F32)
            cs_b, sn_b = Buf(), Buf()
            qn_r = self.sring(es, tag + "qn", 2, [P, TT], BF16)
            qr_r = self.sring(es, tag + "qr", 2, [64, TT], BF16)
            rt = self.sring(es, tag + "rt", 4, [64, TT], F32)
            rec = self.sring(es, tag + "rec", 2, [P, TT], F32)
            oT = self.sb(es, tag + "oT", [P, 8, TT], BF16)
            oT_b = [Buf() for _ in range(8)]
            yst = self.sring(es, tag + "yst", 4, [P, TT], F32)
            R = self.norm_res(es, tag)
            rstd, rstd_b = R["rstd"]
            sps = self.pring(es, tag + "sps", 2)
            ops_, ops_b = self.ps(es, tag + "ops", [P, 512], F32), Buf()
            rps, rps_b = self.ps(es, tag + "rps", [P, 512], F32), Buf()
            mps = self.pring(es, tag + "mps", 3)
            nq, nq_b = self.vec['n_q_lat'], self.vec_b['n_q_lat']
            nkv, nkv_b = self.vec['n_kv_lat'], self.vec_b['n_kv_lat']

            for b in range(NB):
                for tt_ in range(S // TT):
                    t0 = b * S + tt_ * TT
                    K.dma(K.SP, lat[:, 0:2, :], self.scr["ckv"].rearrange("(c p) t -> p c t", p=P)[:, :, t0:t0 + TT],
                          reads=self.sbufs("ckv", t0, TT), writes=lat_b[0:2])
                    self.rms_rstd(lambda c: lat[:, c, :], lambda c: lat_b[c], 2, TT, KV_LORA, R)
                    for c in range(2):
                        self.stt(latn[:, c, :], lat[:, c, :], nkv[:, c:c + 1], rstd[:], ALU.mult, ALU.mult,
                                 [lat_b[c], nkv_b, rstd_b], [latn_b])
                    for h in range(8):
                        ps, psb = mps.next()
                        for kc in range(2):
                            self.mm(ps[:], wukv[:, kc, h * 256:h * 256 + 128], latn[:, kc, :], kc == 0, kc == 1,
                                    [wukv_b, latn_b], [psb])
                        self.cp(K.ACT if h % 2 == 0 else K.DVE, Kn[:, h, tt_ * TT:(tt_ + 1) * TT], ps[:],
                                [psb], [Kn_b[tt_]])
                    for sb_ in range(TT // P):
                        blk = tt_ * (TT // P) + sb_
                        for g in range(2):
                            ps, psb = mps.next()
                            for kc in range(2):
                                self.mm(ps[:].rearrange("p (h d) -> p h d", h=4),
                                        latn[:, kc, sb_ * P:(sb_ + 1) * P], wukv4[:, kc, g * 4:(g + 1) * 4, 1, :],
                                        kc == 0, kc == 1, [wukv_b, latn_b], [psb])
                            self.cp(K.ACT if g == 0 else K.DVE, Vs[:, blk, g * 512:(g + 1) * 512], ps[:],
                                    [psb], [Vs_b[blk]])
                    K.dma(K.SP, krl[:], self.scr["kr"][:, t0:t0 + TT], reads=self.sbufs("kr", t0, TT), writes=[krl_b])
                    self.cp(K.POOL, Kr[:, tt_ * TT:(tt_ + 1) * TT], krl[:], [krl_b], [Kr_b[tt_]])
                for qt in range(S // TT):
                    t0 = b * S + qt * TT
                    K.dma(K.SP, lat[:], self.scr["cq"].rearrange("(c p) t -> p c t", p=P)[:, :, t0:t0 + TT],
                          reads=self.sbufs("cq", t0, TT), writes=lat_b)
                    K.dma(K.SP, cs[:], self.scr["cs"][:, t0:t0 + TT], reads=self.sbufs("cs", t0, TT), writes=[cs_b])
                    K.dma(K.SP, sn[:], self.scr["sn"][:, t0:t0 + TT], reads=self.sbufs("sn", t0, TT), writes=[sn_b])
                    self.rms_rstd(lambda c: lat[:, c, :], lambda c: lat_b[c], 4, TT, Q_LORA, R)
                    for c in range(4):
                        self.stt(latn[:, c, :], lat[:, c, :], nq[:, c:c + 1], rstd[:], ALU.mult, ALU.mult,
                                 [lat_b[c], nq_b, rstd_b], [latn_b])
                    nkb = 4 * (qt + 1)
                    for h in range(8):
                        ps, psb = mps.next()
                        for kc in range(4):
                            self.mm(ps[:], wuq[:, kc, h * 192:h * 192 + 128], latn[:, kc, :], kc == 0, kc == 3,
                                    [wuq_b, latn_b], [psb])
                        qn, qnb = qn_r.next()
                        self.cp(K.ACT, qn[:], ps[:], [psb], [qnb])
                        p1, p1b = mps.next()
                        p2, p2b = mps.next()
                        for kc in range(4):
                            self.mm(p1[0:64, :], wuq[:, kc, h * 192 + 128:h * 192 + 192], latn[:, kc, :],
                                    kc == 0, kc == 3, [wuq_b, latn_b], [p1b])
                        for kc in range(4):
                            self.mm(p2[0:64, :], wuqs[:, kc, h, :], latn[:, kc, :], kc == 0, kc == 3,
                                    [wuqs_b, latn_b], [p2b])
                        r1, r1b = rt.next()
                        r2, r2b = rt.next()
                        self.tt(K.DVE, r1[:], p1[0:64, :], cs[:], ALU.mult, [p1b, cs_b], [r1b])
                        self.tt(K.DVE, r2[:], p2[0:64, :], sn[:], ALU.mult, [p2b, sn_b], [r2b])
                        qr, qrb = qr_r.next()
                        self.tt(K.POOL, qr[:], r1[:], r2[:], ALU.add, [r1b, r2b], [qrb])
                        for kb in range(nkb):
                            qlo = max(0, kb * P - qt * TT)
                            sp, spb = sps.next()
                            self.mm(sp[:, qlo:], Kn[:, h, kb * P:(kb + 1) * P], qn[:, qlo:], True, False,
                                    [Kn_b[kb // 4], qnb], [spb])
                            self.mm(sp[:, qlo:], Kr[:, kb * P:(kb + 1) * P], qr[:, qlo:], False, True,
                                    [Kr_b[kb // 4], qrb], [spb])
                            self.act(pt[:, kb, qlo:], sp[:, qlo:], AF.Exp, [spb], [pt_b[kb]], scale=scale)
                            if kb * P >= qt * TT:
                                self.mset(K.POOL, pt[64:128, kb, qlo:qlo + 64], 0.0, [], [pt_b[kb]])
                        for kb in range(nkb):
                            qlo = max(0, kb * P - qt * TT)
                            self.mm(ops_[:, qlo:], Vs[:, kb, h * P:(h + 1) * P], pt[:, kb, qlo:], kb == 0, kb == nkb - 1,
                                    [Vs_b[kb], pt_b[kb]], [ops_b])
                        for kb in range(nkb):
                            qlo = max(0, kb * P - qt * TT)
                            self.mm(rps[:, qlo:], self.ones_bf[:], pt[:, kb, qlo:], kb == 0, kb == nkb - 1,
                                    [self.ones_b, pt_b[kb]], [rps_b])
                        rc, rcb = rec.next()
                        self.recip(rc[:], rps[:], [rps_b], [rcb])
                        self.tt(K.DVE, oT[:, h, :], ops_[:], rc[:], ALU.mult, [ops_b, rcb], [oT_b[h]])
                    for n in range(16):
                        ps, psb = mps.next()
                        for h in range(8):
                            self.mm(ps[:], woa[:, h, n * P:(n + 1) * P], oT[:, h, :], h == 0, h == 7,
                                    [woa_b, oT_b[h]], [psb])
                        st, stb = yst.next()
                        self.cp(K.ACT if n % 2 == 0 else K.DVE, st[:], ps[:], [psb], [stb])
                        K.dma(K.ACT, self.scr["ya"][n * P:(n + 1) * P, t0:t0 + TT], st[:], reads=[stb],
                              writes=self.sbufs("ya", t0, TT))
        K.barrier()

    def phase_merge(self, src, dst, use_b=True):
        K = self.K
        KC, SW = D // P, 256
        tag = "mg"

        def setup(es):
            c = {}
            c["wga"] = self.sring(es, tag + "wga", 2, [P, KC, SW], BF16)
            c["wgb"] = self.sring(es, tag + "wgb", 2, [P, KC, SW], BF16)
            c["wo"] = self.sring(es, tag + "wo", 2, [P, KC, SW], BF16)
            c["mT"] = (self.sb(es, tag + "mT", [P, KC, TT], BF16), Buf())
            c["ya"] = self.sring(es, tag + "ya", 4, [P, TT], F32)
            c["yb"] = self.sring(es, tag + "yb", 4, [P, TT], F32)
            c["sa"] = self.sring(es, tag + "sa", 2, [P, TT], F32)
            c["sbb"] = self.sring(es, tag + "sbb", 2, [P, TT], F32)
            c["pa"] = self.pring(es, tag + "pa", 2)
            c["pb"] = self.pring(es, tag + "pb", 2)
            c["py"] = self.pring(es, tag + "py", 2)
            return c

        bg, bg_b = self.vec['b_gate'], self.vec_b['b_gate']

        def body(c, t, t0, hT, hT_b, emit, early, mid, late):
            mT, mT_b = c["mT"]
            for s in range(D // SW):
                wa, wab = self.wslab(c["wga"], 'w_in', 0, KC, GATE0 + s * SW, SW)
                wb_, wbb = self.wslab(c["wgb"], 'w_in', 0, KC, GATE0 + D + s * SW, SW)
                for j in range(SW // P):
                    n = s * (SW // P) + j
                    pa, pab = c["pa"].next()
                    pb, pbb = c["pb"].next()
                    for kc in range(KC):
                        self.mm(pa[:], wa[:, kc, j * P:(j + 1) * P], hT[:, kc, :], kc == 0, kc == KC - 1,
                                [wab, hT_b], [pab])
                    for kc in range(KC):
                        self.mm(pb[:], wb_[:, kc, j * P:(j + 1) * P], hT[:, kc, :], kc == 0, kc == KC - 1,
                                [wbb, hT_b], [pbb])
                    ya, yab = c["ya"].next()
                    yb, ybb = c["yb"].next()
                    K.dma(K.SP, ya[:], self.scr["ya"][n * P:(n + 1) * P, t0:t0 + TT],
                          reads=self.sbufs("ya", t0, TT), writes=[yab])
                    sa, sab = c["sa"].next()
                    self.act(sa[:], pa[:], AF.Sigmoid, [pab, bg_b], [sab], bias=bg[:, n:n + 1])
                    self.tt(K.DVE, sa[:], sa[:], ya[:], ALU.mult, [sab, yab], [sab])
                    if use_b:
                        K.dma(K.SP, yb[:], self.scr["yb"][n * P:(n + 1) * P, t0:t0 + TT],
                              reads=self.sbufs("yb", t0, TT), writes=[ybb])
                        sbb, sbbb = c["sbb"].next()
                        self.act(sbb[:], pb[:], AF.Sigmoid, [pbb, bg_b], [sbbb], bias=bg[:, 16 + n:17 + n])
                        self.tt(K.POOL, sbb[:], sbb[:], yb[:], ALU.mult, [sbbb, ybb], [sbbb])
                        self.tt(K.DVE, mT[:, n, :], sa[:], sbb[:], ALU.add, [sab, sbbb], [mT_b])
                    else:
                        self.cp(K.DVE, mT[:, n, :], sa[:], [sab], [mT_b])
                if s == 1:
                    early()
            mid()
            for s in range(D // SW):
                if s == 2:
                    late()
                wo, wob = self.wslab(c["wo"], 'w_o', 0, KC, s * SW, SW)
                for j in range(SW // P):
                    n = s * (SW // P) + j
                    y_ps, y_b = c["py"].next()
                    for kc in range(KC):
                        self.mm(y_ps[:], wo[:, kc, j * P:(j + 1) * P], mT[:, kc, :], kc == 0, kc == KC - 1,
                                [wob, mT_b], [y_b])
                    emit(n, y_ps[:], y_b)

        self.tile_phase(tag, src, dst, 'n_mix_pre', 'n_mix_post', False, setup, body)

    def phase_xattn(self, src, dst):
        K, cfg = self.K, self.cfg
        KC, SW = D // P, 256
        tag = "xa"
        scale = float(MEM_D ** -0.5)
        NMB = N_MEM // P

        def setup(es):
            c = {}
            c["wq"] = self.sring(es, tag + "wq", 2, [P, KC, SW], BF16)
            c["wkv"] = self.sring(es, tag + "wkv", 2, [P, KC, SW], BF16)
            c["wco"] = self.sring(es, tag + "wco", 2, [P, 8, SW], BF16)
            c["qT"] = (self.sb(es, tag + "qT", [P, 8, TT], BF16), [Buf() for _ in range(8)])
            c["oT"] = (self.sb(es, tag + "oT", [P, 8, TT], BF16), [Buf() for _ in range(8)])
            c["KT"] = (self.sb(es, tag + "KT", [P, cfg.NB, 8, N_MEM], BF16), Buf())
            c["Vm"] = (self.sb(es, tag + "Vm", [P, cfg.NB, NMB, 1024], BF16), Buf())
            c["pt"] = self.sring(es, tag + "pt", 4, [P, TT], BF16)
            c["rec"] = self.sring(es, tag + "rec", 2, [P, TT], F32)
            c["pm"] = self.pring(es, tag + "pm", 3)
            c["po"] = (self.ps(es, tag + "po", [P, 512], F32), Buf())
            c["pr"] = (self.ps(es, tag + "pr", [P, 512], F32), Buf())
            KT, KT_b = c["KT"]
            Vm, Vm_b = c["Vm"]
            with contextlib.ExitStack() as es2:
                mt = self.sring(es2, tag + "mt", 2, [P, D], F32)
                mT = self.sb(es2, tag + "mTm", [P, KC, N_MEM], F32)
                mT_b = [Buf() for _ in range(KC)]
                mn = self.sb(es2, tag + "mn", [P, KC, N_MEM], BF16)
                mn_b = Buf()
                R = self.norm_res(es2, tag + "m", width=N_MEM)
                rstd, rstd_b = R["rstd"]
                gm, gm_b = self.vec['n_mem'], self.vec_b['n_mem']
                for b in range(cfg.NB):
                    for mb in range(NMB):
                        m_, m_b = mt.next()
                        r0 = b * N_MEM + mb * P
                        K.dma(K.SP, m_[:], self.mem_in[r0:r0 + P, :], reads=[], writes=[m_b])
                        for g in range(4):
                            ps, psb = c["pm"].next()
                            for j in range(4):
                                self.tp(ps[:, j * P:(j + 1) * P], m_[:, (g * 4 + j) * P:(g * 4 + j + 1) * P],
                                        self.ident[:], [m_b, self.ident_b], [psb])
                            self.cp(K.ACT if g % 2 == 0 else K.DVE, mT[:, g * 4:(g + 1) * 4, mb * P:(mb + 1) * P],
                                    ps[:].rearrange("p (j q) -> p j q", j=4), [psb], mT_b[g * 4:(g + 1) * 4])
                    self.rms_rstd(lambda cc: mT[:, cc, :], lambda cc: mT_b[cc], KC, N_MEM, D, R)
                    for cc in range(KC):
                        self.stt(mn[:, cc, :], mT[:, cc, :], gm[:, cc:cc + 1], rstd[:, :N_MEM], ALU.mult, ALU.mult,
                                 [mT_b[cc], gm_b, rstd_b], [mn_b])
                    for s in range(2048 // SW):
                        wt, wtb = self.wslab(c["wkv"], 'w_ckv', 0, KC, s * SW, SW)
                        h, part = s // 2, s % 2
                        if part == 0:
                            for j in range(2):
                                ps, psb = c["pm"].next()
                                for kc in range(KC):
                                    self.mm(ps[:, :N_MEM], wt[:, kc, j * P:(j + 1) * P], mn[:, kc, :], kc == 0,
                                            kc == KC - 1, [wtb, mn_b], [psb])
                                self.cp(K.ACT, KT[:, b, h * 2 + j, :], ps[:, :N_MEM], [psb], [KT_b])
                        else:
                            for mb in range(NMB):
                                ps, psb = c["pm"].next()
                                for kc in range(KC):
                                    self.mm(ps[:, :SW], mn[:, kc, mb * P:(mb + 1) * P], wt[:, kc, :], kc == 0,
                                            kc == KC - 1, [wtb, mn_b], [psb])
                                self.cp(K.DVE, Vm[:, b, mb, h * 256:(h + 1) * 256], ps[:, :SW], [psb], [Vm_b])
                K.barrier()
            c["py"] = self.pring(es, tag + "py", 2)
            return c

        def body(c, t, t0, hT, hT_b, emit, early, mid, late):
            b = t0 // cfg.S
            qT, qT_b = c["qT"]
            oT, oT_b = c["oT"]
            KT, KT_b = c["KT"]
            Vm, Vm_b = c["Vm"]
            for s in range(1024 // SW):
                wt, wtb = self.wslab(c["wq"], 'w_cq', 0, KC, s * SW, SW)
                for j in range(SW // P):
                    n = s * (SW // P) + j
                    ps, psb = c["pm"].next()
                    for kc in range(KC):
                        self.mm(ps[:], wt[:, kc, j * P:(j + 1) * P], hT[:, kc, :], kc == 0, kc == KC - 1,
                                [wtb, hT_b], [psb])
                    self.cp(K.ACT, qT[:, n, :], ps[:], [psb], [qT_b[n]])
                if s == 1:
                    early()
            po, po_b = c["po"]
            pr, pr_b = c["pr"]
            for h in range(MEM_H):
                pts = []
                for mb in range(NMB):
                    ps, psb = c["pm"].next()
                    for dc in range(2):
                        self.mm(ps[:], KT[:, b, h * 2 + dc, mb * P:(mb + 1) * P], qT[:, h * 2 + dc, :], dc == 0, dc == 1,
                                [KT_b, qT_b[h * 2 + dc]], [psb])
                    p_, p_b = c["pt"].next()
                    self.act(p_[:], ps[:], AF.Exp, [psb], [p_b], scale=scale)
                    pts.append((p_, p_b))
                for mb in range(NMB):
                    self.mm(pr[:], self.ones_bf[:], pts[mb][0][:], mb == 0, mb == NMB - 1,
                            [self.ones_b, pts[mb][1]], [pr_b])
                rc, rcb = c["rec"].next()
                self.recip(rc[:], pr[:], [pr_b], [rcb])
                for dc in range(2):
                    for mb in range(NMB):
                        self.mm(po[:], Vm[:, b, mb, h * 256 + dc * P:h * 256 + (dc + 1) * P], pts[mb][0][:],
                                mb == 0, mb == NMB - 1, [Vm_b, pts[mb][1]], [po_b])
                    self.tt(K.DVE, oT[:, h * 2 + dc, :], po[:], rc[:], ALU.mult, [po_b, rcb], [oT_b[h * 2 + dc]])
            mid()
            for s in range(D // SW):
                if s == 2:
                    late()
                wt, wtb = self.wslab(c["wco"], 'w_co', 0, 8, s * SW, SW)
                for j in range(SW // P):
                    n = s * (SW // P) + j
                    y_ps, y_b = c["py"].next()
                    for kc in range(8):
                        self.mm(y_ps[:], wt[:, kc, j * P:(j + 1) * P], oT[:, kc, :], kc == 0, kc == 7,
                                [wtb, oT_b[kc]], [y_b])
                    emit(n, y_ps[:], y_b)

        self.tile_phase(tag, src, dst, 'n_x_pre', 'n_x_post', False, setup, body)

    def phase_rwkv(self):
        K, cfg = self.K, self.cfg
        S, NB = cfg.S, cfg.NB
        RT, C = 256, 64
        NCH = RT // C
        NPAIR = 16 * NCH
        tag = "rk"
        with contextlib.ExitStack() as es:
            lww = self.sb(es, tag + "lww", [P, RW], BF16)
            wg2 = self.sb(es, tag + "wg2", [P, RW], BF16)
            wob = self.sb(es, tag + "wob", [P, 8, D], BF16)
            lww_b, wg2_b, wob_b = Buf(), Buf(), Buf()
            K.dma(K.SP, lww[0:64, :], self.wb['w_w2'][:, :], reads=self.wbuf['w_w2'], writes=[lww_b])
            K.dma(K.SP, lww[64:128, :], self.wb['w_a2'][:, :], reads=self.wbuf['w_a2'], writes=[lww_b])
            K.dma(K.SP, wg2[:], self.wb['w_g2'][:, :], reads=self.wbuf['w_g2'], writes=[wg2_b])
            K.dma(K.SP, wob[:], self.wb['w_ob'].rearrange("(c p) n -> p c n", p=P),
                  reads=self.wbuf['w_ob'], writes=[wob_b])
            m2 = self.sb(es, tag + "m2", [P, 1280], F32)
            m2_b = Buf()
            K.dma(K.SP, m2[:], self.msk2_in[:, 0:1280], reads=[], writes=[m2_b])
            lmask = m2[0:64, 0:512]
            imask = m2[0:64, 512:1024]
            rsm = m2[:, 1024:1280]
            eps2 = self.sb(es, tag + "eps2", [P, 1], F32)
            eps2_b = Buf()
            self.mset(K.DVE, eps2[:], 64e-5, [], [eps2_b])
            omk = self.sb(es, tag + "omk", [P, 8], F32)
            omk_b = Buf()
            self.ts(K.DVE, omk[:], self.vec['k_a'][:], -1.0, 1.0, ALU.mult, ALU.add, [self.vec_b['k_a']], [omk_b])
            V = self.vec
            Vb = self.vec_b

            yt = self.sring(es, tag + "yt", 3, [P, RT + 1], F32)
            tmp = self.sring(es, tag + "tmp", 21, [P, RT], F32)
            tbf = self.sring(es, tag + "tbf", 4, [P, RT], BF16)
            dwa = self.sb(es, tag + "dwa", [P, RT], BF16)
            dwa_b = Buf()
            sgt = self.sb(es, tag + "sgt", [P, RT], BF16)
            sgt_b = Buf()
            NP2 = 16 * 2
            AR = self.sb(es, tag + "AR", [P, 8, NCH, 2, C], BF16)
            VV = self.sb(es, tag + "VV", [P, 8, NCH, 2, C], BF16)
            BKz = [self.sb(es, tag + "BKz%d" % i, [P, 8, NCH, 2, C], BF16) for i in range(2)]
            Az = [self.sb(es, tag + "Az%d" % i, [P, 8, NCH, C], BF16) for i in range(2)]
            AR_b = [Buf() for _ in range(8)]
            BK_b = [Buf() for _ in range(8)]
            VV_b = [Buf() for _ in range(8)]
            self.mset(K.POOL, VV[:], 0.0, [], VV_b)
            for i in range(2):
                self.mset(K.POOL, BKz[i][:], 0.0, [], BK_b)
                self.mset(K.POOL, Az[i][:], 0.0, [], AR_b)
            bonus = self.sb(es, tag + "bonus", [P, 8, RT], F32)
            bonus_b = [Buf() for _ in range(8)]
            PC = self.sb(es, tag + "PC", [P, 8, NCH], F32)
            PC_b = [Buf() for _ in range(8)]
            Gm = self.sb(es, tag + "Gm", [P, NP2, 128], BF16)
            Gm_b = [Buf() for _ in range(NP2 // 4)]
            Ltb = self.sb(es, tag + "Ltb", [P, 16, 128], BF16)
            Lnb = self.sb(es, tag + "Lnb", [P, 16, 128], BF16)
            Gb = self.sb(es, tag + "Gb", [P, 16, 128], BF16)
            Ltb_b = [Buf() for _ in range(4)]
            Lnb_b = [Buf() for _ in range(4)]
            Gb_b = [Buf() for _ in range(4)]
            m3 = self.sb(es, tag + "m3", [P, 1536], F32)
            m3_b = Buf()
            K.dma(K.SP, m3[:], self.msk3_in[:, :], reads=[], writes=[m3_b])
            TTa = self.sb(es, tag + "TTa", [64, NP2, C], BF16)
            TT_b = [Buf() for _ in range(NP2 // 8)]
            lt_r = self.sring(es, tag + "ltr", 7, [P, 4, 128], BF16)
            ln_r = self.sring(es, tag + "lnr", 7, [P, 4, 128], BF16)
            BKt = self.sb(es, tag + "BKt", [P, NP2, 128], BF16)
            BKt_b = [Buf() for _ in range(NP2 // 4)]
            UV = self.sb(es, tag + "UV", [P, NP2, 128], BF16)
            UVv_b = [Buf() for _ in range(4)]
            UVu_b = [[Buf() for _ in range(2)] for _ in range(2)]
            self.mset(K.POOL, UV[:], 0.0, [], UVv_b + UVu_b[0] + UVu_b[1])
            UV5 = UV[:].rearrange("p (h r c) v -> p h r c v", r=2, c=2)
            Hf = self.sb(es, tag + "Hf", [P, 8, 128], F32)
            Hb = self.sb(es, tag + "Hb", [P, 8, 128], BF16)
            Hf_b = [Buf(), Buf()]
            Hb_b = [Buf(), Buf()]
            Xs = self.sring(es, tag + "Xs", 4, [64, 512], BF16)
            osb = self.sb(es, tag + "osb", [P, 8, RT], F32)
            osb_b = [[Buf() for _ in range(NCH)] for _ in range(2)]
            og = self.sb(es, tag + "og", [P, 8, RT], BF16)
            og_b = [Buf() for _ in range(8)]
            yst = self.sring(es, tag + "yst", 2, [P, RT], F32)
            pa = self.pring(es, tag + "pa", 4)
            ptr = self.pring(es, tag + "ptr", 2, [P, 512], BF16)
            pc = self.pring(es, tag + "pc", 2)
            rwT = self.scr["rw"]

            def v3(ap):
                return ap.rearrange("p (c t) -> p c t", t=C)

            def load_shift(ci, t0, first):
                y, yb = yt.next()
                if first:
                    self.mset(K.POOL, y[:, 0:1], 0.0, [], [yb])
                    K.dma(K.SP, y[:, 1:RT + 1], rwT[ci * P:(ci + 1) * P, t0:t0 + RT],
                          reads=self.sbufs("rw", t0, RT), writes=[yb])
                else:
                    K.dma(K.SP, y[:, :], rwT[ci * P:(ci + 1) * P, t0 - 1:t0 + RT],
                          reads=self.sbufs("rw", t0 - 1, RT + 1), writes=[yb])
                d, db = tmp.next()
                self.tt(K.DVE, d[:], y[:, 0:RT], y[:, 1:RT + 1], ALU.subtract, [yb], [db])
                o, ob = tmp.next()
                self.stt(o[:], d[:], V['mu_shift'][:, ci:ci + 1], y[:, 1:RT + 1], ALU.mult, ALU.add,
                         [db, yb, Vb['mu_shift']], [ob])
                return o, ob

            for b in range(NB):
                self.mset(K.DVE, Hf[:], 0.0, [], Hf_b)
                self.mset(K.DVE, Hb[:], 0.0, [], Hb_b)
                for ti in range(S // RT):
                    t0 = b * S + ti * RT
                    first = (ti == 0)
                    p24, p24b = load_shift(24, t0, first)
                    self.act(dwa[0:64, :], p24[0:64, :], AF.Tanh, [p24b], [dwa_b])
                    self.cp(K.DVE, dwa[64:128, :], p24[64:128, :], [p24b], [dwa_b])
                    p25, p25b = load_shift(25, t0, first)
                    self.act(sgt[:], p25[:], AF.Sigmoid, [p25b], [sgt_b])
                    for n in range(8):
                        r, rb = load_shift(n, t0, first)
                        k, kb = load_shift(8 + n, t0, first)
                        v, vb = load_shift(16 + n, t0, first)
                        ps, psb = pa.next()
                        self.mm(ps[:, :RT], lww[0:64, n * P:(n + 1) * P], dwa[0:64, :], True, True,
                                [lww_b, dwa_b], [psb])
                        lw, lwb = tmp.next()
                        self.act(lw[:], ps[:, :RT], AF.Sigmoid, [psb, Vb['w0']], [lwb], bias=V['w0'][:, n:n + 1])
                        self.ts(K.DVE, lw[:], lw[:], -0.6065306597126334, None, ALU.mult, None, [lwb], [lwb])
                        ps2, ps2b = pa.next()
                        self.mm(ps2[:, :RT], lww[64:128, n * P:(n + 1) * P], dwa[64:128, :], True, True,
                                [lww_b, dwa_b], [ps2b])
                        a, ab = tmp.next()
                        self.act(a[:], ps2[:, :RT], AF.Sigmoid, [ps2b, Vb['a0']], [ab], bias=V['a0'][:, n:n + 1])
                        cs, csb = tmp.next()
                        K.op(K.DVE, [lwb, m2_b], [csb], lambda e, cs=cs, lw=lw: e.tensor_tensor_scan(
                            out=cs[:], data0=rsm, data1=lw[:], initial=0.0, op0=ALU.mult, op1=ALU.add))
                        e_in, e_inb = tmp.next()
                        self.act(e_in[:], cs[:], AF.Exp, [csb], [e_inb])
                        e_ng, e_ngb = tmp.next()
                        self.act(e_ng[:], cs[:], AF.Exp, [csb], [e_ngb], scale=-1.0)
                        dx, dxb = tmp.next()
                        self.tt(K.POOL, dx[:], cs[:], lw[:], ALU.subtract, [csb, lwb], [dxb])
                        e_ex, e_exb = tmp.next()
                        self.act(e_ex[:], dx[:], AF.Exp, [dxb], [e_exb])
                        self.cp(K.POOL, PC[:, n, :], v3(e_in[:])[:, :, C - 1], [e_inb], [PC_b[n]])
                        kk, kkb = tmp.next()
                        self.ts(K.DVE, kk[:], k[:], V['k_k'][:, n:n + 1], None, ALU.mult, None, [kb, Vb['k_k']], [kkb])
                        q2, q2b = tbf.next()
                        self.act(q2[:], kk[:], AF.Square, [kkb], [q2b])
                        ps3, ps3b = pa.next()
                        self.mm(ps3[:, :RT], self.bones_bf[:], q2[:], True, True, [self.bones_b, q2b], [ps3b])
                        nr, nrb = tmp.next()
                        self.act(nr[:], ps3[:, :RT], AF.Sqrt, [ps3b], [nrb])
                        self.ts(K.DVE, nr[:], nr[:], 1e-12, None, ALU.max, None, [nrb], [nrb])
                        self.recip(nr[:], nr[:], [nrb], [nrb])
                        self.tt(K.DVE, kk[:], kk[:], nr[:], ALU.mult, [kkb, nrb], [kkb])
                        km, kmb = tmp.next()
                        self.ts(K.DVE, km[:], a[:], V['k_a'][:, n:n + 1], omk[:, n:n + 1], ALU.mult, ALU.add,
                                [ab, Vb['k_a'], omk_b], [kmb])
                        self.tt(K.DVE, km[:], km[:], k[:], ALU.mult, [kmb, kb], [kmb])
                        rk, rkb = tbf.next()
                        self.stt(rk[:], r[:], V['r_k'][:, n:n + 1], km[:], ALU.mult, ALU.mult,
                                 [rb, kmb, Vb['r_k']], [rkb])
                        ps4, ps4b = pa.next()
                        self.mm(ps4[:, :RT], self.bones_bf[:], rk[:], True, True, [self.bones_b, rkb], [ps4b])
                        self.tt(K.DVE, bonus[:, n, :], ps4[:, :RT], v[:], ALU.mult, [ps4b, vb], [bonus_b[n]])
                        at, atb = tmp.next()
                        self.stt(at[:], kk[:], -1.0, e_ex[:], ALU.mult, ALU.mult, [kkb, e_exb], [atb])
                        self.cp(K.ACT, AR[:, n, :, 0, :], v3(at[:]), [atb], [AR_b[n]])
                        self.cp(K.POOL, Az[0][0:64, n, :, :], v3(at[0:64, :]), [atb], [AR_b[n]])
                        self.cp(K.POOL, Az[1][64:128, n, :, :], v3(at[64:128, :]), [atb], [AR_b[n]])
                        self.tt(K.POOL, AR[:, n, :, 1, :], v3(r[:]), v3(e_in[:]), ALU.mult, [rb, e_inb], [AR_b[n]])
                        kb2, kb2b = tmp.next()
                        self.tt(K.POOL, kb2[:], kk[:], a[:], ALU.mult, [kkb, ab], [kb2b])
                        self.tt(K.DVE, kb2[:], kb2[:], e_ng[:], ALU.mult, [kb2b, e_ngb], [kb2b])
                        kt, ktb = tmp.next()
                        self.tt(K.POOL, kt[:], km[:], e_ng[:], ALU.mult, [kmb, e_ngb], [ktb])
                        self.cp(K.ACT, BKz[0][0:64, n, :, 0, :], v3(kb2[0:64, :]), [kb2b], [BK_b[n]])
                        self.cp(K.ACT, BKz[1][64:128, n, :, 0, :], v3(kb2[64:128, :]), [kb2b], [BK_b[n]])
                        self.cp(K.DVE, BKz[0][0:64, n, :, 1, :], v3(kt[0:64, :]), [ktb], [BK_b[n]])
                        self.cp(K.DVE, BKz[1][64:128, n, :, 1, :], v3(kt[64:128, :]), [ktb], [BK_b[n]])
                        self.cp(K.ACT, VV[:, n, :, 1, :], v3(v[:]), [vb], [VV_b[n]])

                    def f2(ap):
                        return ap.rearrange("p a t -> p (a t)")

                    if getattr(cfg, 'rstop', 99) < 2:
                        continue
                    for hf in range(2):
                        def pr(q, hf=hf):
                            h, cl = divmod(q, 2)
                            return h, h // 2, h % 2, 2 * hf + cl
                        for g4 in range(NP2 // 4):
                            ps, psb = pa.next()
                            for i in range(4):
                                h, hp, par, c = pr(g4 * 4 + i)
                                self.mm(ps[:, i * 128:(i + 1) * 128], f2(BKz[par][:, hp, c, :, :]), f2(AR[:, hp, c, :, :]),
                                        True, True, [BK_b[hp], AR_b[hp]], [psb])
                            self.tt(K.DVE, f2(Gm[:, g4 * 4:(g4 + 1) * 4, :]), ps[:], self.msk[:], ALU.mult,
                                    [psb, self.msk_b], [Gm_b[g4]])
                            pt1, pt1b = ptr.next()
                            for i in range(4):
                                h, hp, par, c = pr(g4 * 4 + i)
                                self.tp(pt1[:, i * 128:(i + 1) * 128], f2(BKz[par][:, hp, c, :, :]), self.ident_bf[:],
                                        [BK_b[hp], self.identbf_b], [pt1b])
                            self.cp(K.ACT, f2(BKt[:, g4 * 4:(g4 + 1) * 4, :]), pt1[:], [pt1b], [BKt_b[g4]])
                        for gv in range(4):
                            pt2, pt2b = ptr.next()
                            for i in range(4):
                                hpl, cl = divmod(i, 2)
                                hp, c = gv * 2 + hpl, 2 * hf + cl
                                self.tp(pt2[:, i * 128:(i + 1) * 128], f2(VV[:, hp, c, :, :]), self.ident_bf[:],
                                        [VV_b[hp], self.identbf_b], [pt2b])
                            pv = pt2[64:128, :].rearrange("p (h c v) -> p h c v", h=2, c=2)
                            self.cp(K.ACT, UV5[64:128, gv * 2:gv * 2 + 2, 0, :, 0:64], pv[:, :, :, 0:64], [pt2b], [UVv_b[gv]])
                            self.cp(K.DVE, UV5[64:128, gv * 2:gv * 2 + 2, 1, :, 64:128], pv[:, :, :, 64:128], [pt2b],
                                    [UVv_b[gv]])
                        c0 = 2 * hf
                        for g in range(4):
                            ps2, ps2b = pa.next()
                            for i in range(4):
                                h = g * 4 + i
                                hp, par = h // 2, h % 2
                                bk0 = BKz[par][:, hp, c0:c0 + 2, 0, :]
                                a0 = AR[:, hp, c0:c0 + 2, 0, :]
                                az0 = Az[par][:, hp, c0:c0 + 2, :]
                                o2 = ps2[:, i * 128:(i + 1) * 128].rearrange("p (a t) -> p a t", a=2)
                                self.mm(o2, az0.rearrange("p a t -> p (a t)"), bk0, True, True, [BK_b[hp], AR_b[hp]], [ps2b])
                            self.tt(K.DVE, f2(Lnb[:, g * 4:(g + 1) * 4, :]), ps2[:], m3[:, 512:1024], ALU.mult,
                                    [ps2b, m3_b], [Lnb_b[g]])
                            ptl, ptlb = ptr.next()
                            for i in range(4):
                                self.tp(ptl[:, i * 128:(i + 1) * 128], Lnb[:, g * 4 + i, :], self.ident_bf[:],
                                        [Lnb_b[g], self.identbf_b], [ptlb])
                            self.cp(K.ACT, f2(Ltb[:, g * 4:(g + 1) * 4, :]), ptl[:], [ptlb], [Ltb_b[g]])
                        if getattr(cfg, 'rstop', 99) < 3:
                            continue
                        ist = {}
                        for g in range(4):
                            G = Gb[:, g * 4:(g + 1) * 4, :]
                            gb = Gb_b[g]
                            self.tt(K.DVE, f2(G), f2(Ltb[:, g * 4:(g + 1) * 4, :]), m3[:, 1024:1536], ALU.add,
                                    [Ltb_b[g], m3_b], [gb])
                            ist[g] = dict(G=G, G2=f2(G), gb=gb, lt_bufs=[Ltb_b[g]], ln_bufs=[Lnb_b[g]],
                                          lt_cur=[Ltb[:, g * 4 + i, :] for i in range(4)],
                                          ln_cur=[Lnb[:, g * 4 + i, :] for i in range(4)])
                        for lvl in range(5):
                            last = (lvl == 4)
                            for g in range(4):
                                z = ist[g]
                                G, G2, gb = z["G"], z["G2"], z["gb"]
                                lt_cur, ln_cur, lt_bufs, ln_bufs = z["lt_cur"], z["ln_cur"], z["lt_bufs"], z["ln_bufs"]
                                if not last:
                                    p_lt, p_ltb = pa.next()
                                    for i in range(4):
                                        self.mm(p_lt[:, i * 128:(i + 1) * 128], ln_cur[i], lt_cur[i], True, True,
                                                lt_bufs + ln_bufs, [p_ltb])
                                p_ln, p_lnb = pa.next()
                                for i in range(4):
                                    self.mm(p_ln[:, i * 128:(i + 1) * 128], lt_cur[i], ln_cur[i], True, True,
                                            lt_bufs + ln_bufs, [p_lnb])
                                ln_n, ln_nb = ln_r.next()
                                self.cp(K.ACT, f2(ln_n[:]), p_ln[:], [p_lnb], [ln_nb])
                                if not last:
                                    lt_n, lt_nb = lt_r.next()
                                    self.cp(K.DVE, f2(lt_n[:]), p_lt[:], [p_ltb], [lt_nb])
                                p_g, p_gb = pa.next()
                                for i in range(4):
                                    self.mm(p_g[:, i * 128:(i + 1) * 128], ln_n[:, i, :], G[:, i, :], True, True,
                                            [ln_nb, gb], [p_gb])
                                self.tt(K.DVE, G2, p_g[:], G2, ALU.add, [p_gb, gb], [gb])
                                z["ln_cur"] = [ln_n[:, i, :] for i in range(4)]
                                z["ln_bufs"] = [ln_nb]
                                if not last:
                                    z["lt_cur"] = [lt_n[:, i, :] for i in range(4)]
                                    z["lt_bufs"] = [lt_nb]
                        TT4 = TTa[:].rearrange("p (h c) t -> p h c t", c=2)
                        for g in range(4):
                            gb = Gb_b[g]
                            self.cp(K.ACT, TT4[:, g * 4:(g + 1) * 4, 0, :], Gb[0:64, g * 4:(g + 1) * 4, 0:C], [gb], [TT_b[g]])
                            ps, psb = pa.next()
                            for i in range(4):
                                self.mm(ps[0:64, i * C:(i + 1) * C], self.ident_bf[:, 64:128], Gb[:, g * 4 + i, C:2 * C],
                                        True, True, [gb, self.identbf_b], [psb])
                            self.cp(K.DVE, TT4[:, g * 4:(g + 1) * 4, 1, :],
                                    ps[0:64, 0:4 * C].rearrange("p (a t) -> p a t", t=C), [psb], [TT_b[g]])
                        if getattr(cfg, 'rstop', 99) < 4:
                            continue
                        for cl in range(2):
                            c = 2 * hf + cl
                            sst = {0: {}, 1: {}}

                            def stage1(hg, cl=cl, c=c):
                                hb_ = Hb_b[hg]
                                uvb = UVu_b[hg][cl]
                                xss = []
                                for bank in range(2):
                                    xp, xpb = pc.next()
                                    for jj in range(4):
                                        h = hg * 8 + bank * 4 + jj
                                        hp, par, q = h // 2, h % 2, h * 2 + cl
                                        self.mm(xp[0:64, jj * 128:(jj + 1) * 128], Az[par][:, hp, c, :], Hb[:, hp, :],
                                                True, False, [AR_b[hp], hb_], [xpb])
                                        self.mm(xp[0:64, jj * 128:(jj + 1) * 128], Gm[:, q, 0:C], UV[:, q, :],
                                                False, True, [Gm_b[q // 4], UVv_b[hp // 2], uvb], [xpb])
                                    xs, xsb = Xs.next()
                                    self.cp(K.ACT if bank == 0 else K.DVE, xs[:], xp[0:64, :], [xpb], [xsb])
                                    xss.append((xs, xsb))
                                sst[hg]["xss"] = xss

                            def stage2(hg, cl=cl, c=c):
                                uvb = UVu_b[hg][cl]
                                for bank in range(2):
                                    xs, xsb = sst[hg]["xss"][bank]
                                    up, upb = pc.next()
                                    for jj in range(4):
                                        h = hg * 8 + bank * 4 + jj
                                        q = h * 2 + cl
                                        self.mm(up[0:64, jj * 128:(jj + 1) * 128], TTa[:, q, :],
                                                xs[:, jj * 128:(jj + 1) * 128], True, True, [TT_b[q // 8], xsb], [upb])
                                    hp0 = hg * 4 + bank * 2
                                    self.cp(K.DVE if bank == 0 else K.ACT, UV5[0:64, hp0:hp0 + 2, :, cl, :],
                                            up[0:64, :].rearrange("p (h r v) -> p h r v", h=2, r=2), [upb], [uvb])

                            def stage3(hg, cl=cl, c=c):
                                hb_, hf_ = Hb_b[hg], Hf_b[hg]
                                uvb = UVu_b[hg][cl]
                                op_, opb = pa.next()
                                for hl in range(4):
                                    hp = hg * 4 + hl
                                    qa, qb_ = (2 * hp) * 2 + cl, (2 * hp + 1) * 2 + cl
                                    o_ap = op_[:, hl * C:(hl + 1) * C]
                                    self.mm(o_ap, Hb[:, hp, :], AR[:, hp, c, 1, :], True, False, [hb_, AR_b[hp]], [opb])
                                    self.mm(o_ap, UV[:, qa, :], Gm[:, qa, C:2 * C], False, False,
                                            [uvb, UVv_b[hp // 2], Gm_b[qa // 4]], [opb])
                                    self.mm(o_ap, UV[:, qb_, :], Gm[:, qb_, C:2 * C], False, True,
                                            [uvb, UVv_b[hp // 2], Gm_b[qb_ // 4]], [opb])
                                self.cp(K.ACT, osb[:, hg * 4:(hg + 1) * 4, c * C:(c + 1) * C],
                                        op_[:, 0:4 * C].rearrange("p (a t) -> p a t", t=C), [opb], [osb_b[hg][c]])
                                hp_, hpb = pa.next()
                                for hl in range(4):
                                    hp = hg * 4 + hl
                                    qa, qb_ = (2 * hp) * 2 + cl, (2 * hp + 1) * 2 + cl
                                    h_ap = hp_[:, hl * 128:(hl + 1) * 128]
                                    self.mm(h_ap, BKt[:, qa, :], UV[:, qa, :], True, False,
                                            [BKt_b[qa // 4], uvb, UVv_b[hp // 2]], [hpb])
                                    self.mm(h_ap, BKt[:, qb_, :], UV[:, qb_, :], False, True,
                                            [BKt_b[qb_ // 4], uvb, UVv_b[hp // 2]], [hpb])
                                hs = Hf[:, hg * 4:(hg + 1) * 4, :]
                                self.tt(K.DVE, hs, hp_[:].rearrange("p (a v) -> p a v", v=128), hs, ALU.add,
                                        [hpb, hf_], [hf_])
                                pcb = PC[:, hg * 4:(hg + 1) * 4, c:c + 1].to_broadcast([P, 4, 128])
                                self.tt(K.DVE, hs, hs, pcb, ALU.mult, [hf_] + PC_b[hg * 4:(hg + 1) * 4], [hf_])
                                self.cp(K.ACT, Hb[:, hg * 4:(hg + 1) * 4, :], hs, [hf_], [hb_])

                            for stage in (stage1, stage2, stage3):
                                for hg in range(2):
                                    stage(hg)
                    if getattr(cfg, 'rstop', 99) < 5:
                        continue
                    for n in range(8):
                        hg = n // 4
                        o_n = osb[:, n, :]
                        o_bufs = osb_b[hg]
                        ob_, ob_b = tbf.next()
                        self.cp(K.ACT, ob_[:], o_n, o_bufs, [ob_b])
                        ps, psb = pa.next()
                        self.mm(ps[:, :RT], self.bones_bf[:], ob_[:], True, True, [self.bones_b, ob_b], [psb])
                        d, db = tmp.next()
                        self.stt(d[:], ps[:, :RT], -1.0 / 64.0, o_n, ALU.mult, ALU.add, [psb] + o_bufs, [db])
                        d2, d2b = tbf.next()
                        self.act(d2[:], d[:], AF.Square, [db], [d2b])
                        ps2, ps2b = pa.next()
                        self.mm(ps2[:, :RT], self.bones_bf[:], d2[:], True, True, [self.bones_b, d2b], [ps2b])
                        sd, sdb = tmp.next()
                        self.act(sd[:], ps2[:, :RT], AF.Sqrt, [ps2b, eps2_b], [sdb], scale=1.0 / 64.0, bias=eps2[:])
                        self.recip(sd[:], sd[:], [sdb], [sdb])
                        self.tt(K.DVE, d[:], d[:], sd[:], ALU.mult, [db, sdb], [db])
                        self.ts(K.DVE, d[:], d[:], V['lnx_w'][:, n:n + 1], V['lnx_b'][:, n:n + 1], ALU.mult, ALU.add,
                                [db, Vb['lnx_w'], Vb['lnx_b']], [db])
                        self.tt(K.POOL, d[:], d[:], bonus[:, n, :], ALU.add, [db, bonus_b[n]], [db])
                        ps3, ps3b = pa.next()
                        self.mm(ps3[:, :RT], wg2[:, n * P:(n + 1) * P], sgt[:], True, True, [wg2_b, sgt_b], [ps3b])
                        self.tt(K.DVE, og[:, n, :], d[:], ps3[:, :RT], ALU.mult, [db, ps3b], [og_b[n]])
                    for m_ in range(16):
                        ps, psb = pa.next()
                        for n in range(8):
                            self.mm(ps[:, :RT], wob[:, n, m_ * P:(m_ + 1) * P], og[:, n, :], n == 0, n == 7,
                                    [wob_b, og_b[n]], [psb])
                        st, stb = yst.next()
                        self.cp(K.ACT if m_ % 2 == 0 else K.DVE, st[:], ps[:, :RT], [psb], [stb])
                        K.dma(K.ACT, self.scr["yb"][m_ * P:(m_ + 1) * P, t0:t0 + RT], st[:], reads=[stb],
                              writes=self.sbufs("yb", t0, RT))
        K.barrier()

    def dump(self, name, srcap, bufs):
        K = self.K
        b = Buf()
        K.dma(K.SP, self.dbg[name][:, :], srcap, reads=bufs, writes=[b])
        self.out_bufs.append(b)

    def build(self):
        K, cfg = self.K, self.cfg
        self.out_bufs = []
        ph = cfg.phases
        with contextlib.ExitStack() as es:
            self.setup_consts(es)
            names = []
            for p_ in ph:
                names += PHASE_W[p_]
            self.cast_weights(names)
            self.phase_in()
            cur = 0
            if "ffn1" in ph:
                self.phase_ffn("f1", cur, 1 - cur, 'n_ffn1_pre', 'n_ffn1_post',
                               'w_ffn1_gate', 'w_ffn1_up', 'w_ffn1_down')
                cur = 1 - cur
            if "proj" in ph:
                self.phase_proj(cur)
            if "mla" in ph:
                self.phase_mla()
            if "rwkv" in ph:
                self.phase_rwkv()
            if "merge" in ph:
                self.phase_merge(cur, 1 - cur, use_b=("rwkv" in ph))
                cur = 1 - cur
            if "xattn" in ph:
                self.phase_xattn(cur, 1 - cur)
                cur = 1 - cur
            if "ffn2" in ph:
                self.phase_ffn("f2", cur, 1 - cur, 'n_ffn2_pre', 'n_ffn2_post',
                               'w_ffn2_gate', 'w_ffn2_up', 'w_ffn2_down')
                cur = 1 - cur
            for nm, rows in cfg.debug_out:
                self.dump(nm, self.scr[nm][:, :], self.scr_b[nm])
            self.phase_out(cur)
            K.barrier()
        return self.nc


def _consts():
    ident = np.eye(P, dtype=np.float32)
    cst = np.zeros((P, 8), np.float32)
    half = 32
    inv = (10000.0 ** (-np.arange(half, dtype=np.float32) / half)).astype(np.float32)
    cst[0:32, 0] = inv / (2 * np.pi)
    cst[32:64, 0] = inv / (2 * np.pi)
    cst[0:32, 1] = -1.0
    cst[32:64, 1] = 1.0
    s_ = np.arange(64)[:, None]
    t_ = np.arange(64)[None, :]
    su = (s_ < t_).astype(np.float32)
    ui = (s_ <= t_).astype(np.float32)
    blk = np.block([[np.zeros_like(su), ui], [su, ui]])
    msk = np.tile(blk, (1, 4)).astype(np.float32)
    msk2 = np.zeros((P, 1536), np.float32)
    sl = (t_ < s_).astype(np.float32)
    msk2[0:64, 0:512] = np.tile(sl, (1, 8))
    msk2[0:64, 512:1024] = np.tile(np.eye(64, dtype=np.float32), (1, 8))
    rs = np.ones(256, np.float32)
    rs[0::64] = 0.0
    msk2[:, 1024:1280] = rs[None, :]
    msk2[0:64, 1280:1536] = np.tile(su, (1, 4))
    z = np.zeros_like(su)
    bdu = np.block([[su, z], [z, su]])
    bdl = np.block([[sl, z], [z, sl]])
    msk3 = np.concatenate([np.tile(bdu, (1, 4)), np.tile(bdl, (1, 4)), np.tile(np.eye(128, dtype=np.float32), (1, 4))],
                          axis=1).astype(np.float32)
    return ident, cst, msk, msk2, msk3


_IDENT, _CST, _MSK, _MSK2, _MSK3 = _consts()


def make_in_map(cfg, inputs, core):
    NB, S = cfg.NB, cfg.S
    b0 = core * NB
    m = {
        "x": np.ascontiguousarray(np.asarray(inputs["x"])[b0:b0 + NB].reshape(NB * S, D)),
        "mem": np.ascontiguousarray(np.asarray(inputs["mem"])[b0:b0 + NB].reshape(NB * N_MEM, D)),
        "pos": np.ascontiguousarray(np.asarray(inputs["positions"])[b0:b0 + NB].reshape(NB * S)).astype(np.int32),
        "ident": _IDENT, "cst": _CST, "msk": _MSK, "msk2": _MSK2, "msk3": _MSK3,
    }
    for nm in W_SHAPES:
        if getattr(cfg, 'lite', False) and not any(nm in PHASE_W[p_] for p_ in cfg.phases):
            continue
        m[nm] = np.ascontiguousarray(np.asarray(inputs[nm])[0])
    for nm in V_SHAPES:
        m[nm] = np.ascontiguousarray(np.asarray(inputs[nm])[0].reshape(-1))
    return m


def kernel(**inputs):
    cfg = Cfg()
    prog = Prog(cfg)
    nc = prog.build()
    n_cores = 8
    in_maps = [make_in_map(cfg, inputs, c) for c in range(n_cores)]
    res = run_bass_kernel_spmd(nc, in_maps, core_ids=list(range(n_cores)))
    outs = [np.asarray(res.results[c]["out"]).reshape(cfg.NB, cfg.S, D) for c in range(n_cores)]
    return np.concatenate(outs, axis=0).astype(np.float32)
```
